# Optimizing a Trainium2 kernel written in Bass

```python
import jax, jax.numpy as jnp
from jax import lax
import numpy as np

D_MODEL = 2048
BATCH = 16
SEQ = 256
DEPTH = 4
DEC_BATCH = 2
DEC_SEQ = 1024
PAST_LEN = 512

GRID_W = 64
D_ATT = D_MODEL // 2
N_HEADS = 8
HEAD_DIM = D_ATT // N_HEADS
D_CONV = D_MODEL // 4
D_LRU = D_MODEL // 4
D_MIX = D_ATT + D_CONV + D_LRU
D_IN = 3 * D_ATT + 2 * D_CONV + 2 * D_LRU
IN_SPLITS = (D_ATT, 2 * D_ATT, 3 * D_ATT, 3 * D_ATT + D_CONV, 3 * D_ATT + 2 * D_CONV, 3 * D_ATT + 2 * D_CONV + D_LRU)
WIN_ROWS = 8
WIN_COLS = 16
CONV_WIDTH = 31
LRU_CONV_WIDTH = 4
LRU_BLOCKS = 8
LRU_BW = D_LRU // LRU_BLOCKS
LRU_C = 8.0
D_FF = 5632
FFN_CONV_WIDTH = 3
Q_BLOCK = 128
EPS = 1e-6
ATT_SCALE = HEAD_DIM ** -0.5
NEG_INF = -1e30

kernel_name = 'hybrid_flow_trunk_step'


def _rmsnorm(x, g):
    xf = x.astype(jnp.float32)
    y = xf * lax.rsqrt(jnp.mean(xf * xf, axis=-1, keepdims=True) + EPS)
    return (y * g.astype(jnp.float32)).astype(x.dtype)


def _layernorm(x, g, b):
    xf = x.astype(jnp.float32)
    mu = jnp.mean(xf, axis=-1, keepdims=True)
    var = jnp.mean(jnp.square(xf - mu), axis=-1, keepdims=True)
    y = (xf - mu) * lax.rsqrt(var + EPS) * g.astype(jnp.float32) + b.astype(jnp.float32)
    return y.astype(x.dtype)


def _dwconv(x, w, b, pad_l, pad_r):
    C = x.shape[-1]
    y = lax.conv_general_dilated(x, w[:, None, :].astype(x.dtype), window_strides=(1,),
                                 padding=[(pad_l, pad_r)], dimension_numbers=('NWC', 'WIO', 'NWC'),
                                 feature_group_count=C)
    return y + b.astype(x.dtype)


def _context_attention(q, k, v):
    B, S, H, Dh = q.shape
    qb = q.reshape(B, S // Q_BLOCK, Q_BLOCK, H, Dh).swapaxes(0, 1)

    def one(qi):
        s = jnp.einsum('bqhd,bkhd->bhqk', qi, k).astype(jnp.float32) * ATT_SCALE
        p = jax.nn.softmax(s, axis=-1).astype(v.dtype)
        return jnp.einsum('bhqk,bkhd->bqhd', p, v)

    o = lax.map(one, qb)
    return o.swapaxes(0, 1).reshape(B, S, H, Dh)


def _latent_attention(q, k, v, k_ctx, v_ctx, rel_bias):
    B, N, H, Dh = q.shape
    R = N // GRID_W
    wr = min(WIN_ROWS, R)
    rows = jnp.arange(R)
    row_idx = jnp.clip(rows - wr // 2, 0, R - wr)[:, None] + jnp.arange(wr)[None, :]
    cols = jnp.arange(GRID_W)
    col_start = jnp.clip(cols - WIN_COLS // 2, 0, GRID_W - WIN_COLS)
    in_win = (cols[None, :] >= col_start[:, None]) & (cols[None, :] < col_start[:, None] + WIN_COLS)
    ro = row_idx - rows[:, None] + WIN_ROWS - 1
    co = jnp.clip(cols[None, :] - cols[:, None], -(WIN_COLS - 1), WIN_COLS - 1) + WIN_COLS - 1
    bias = rel_bias[:, ro[:, :, None, None], co[None, None, :, :]]
    bias = bias.transpose(1, 0, 3, 2, 4).reshape(R, H, GRID_W, wr * GRID_W).astype(jnp.float32)
    mask = jnp.broadcast_to(in_win[:, None, :], (GRID_W, wr, GRID_W)).reshape(GRID_W, wr * GRID_W)
    n_win = wr * GRID_W
    qg = q.reshape(B, R, GRID_W, H, Dh)
    kg = k.reshape(B, R, GRID_W, H, Dh)[:, row_idx].reshape(B, R, n_win, H, Dh)
    vg = v.reshape(B, R, GRID_W, H, Dh)[:, row_idx].reshape(B, R, n_win, H, Dh)
    s_win = jnp.einsum('brqhd,brkhd->brhqk', qg, kg).astype(jnp.float32) * ATT_SCALE + bias[None]
    s_win = jnp.where(mask, s_win, NEG_INF)
    s_ctx = jnp.einsum('brqhd,bphd->brhqp', qg, k_ctx).astype(jnp.float32) * ATT_SCALE
    p = jax.nn.softmax(jnp.concatenate([s_win, s_ctx], axis=-1), axis=-1).astype(v.dtype)
    o = (jnp.einsum('brhqk,brkhd->brqhd', p[..., :n_win], vg)
         + jnp.einsum('brhqp,bphd->brqhd', p[..., n_win:], v_ctx))
    return o.reshape(B, N, H, Dh)


def _conv_module(a, g, p):
    u = a * jax.nn.sigmoid(g)
    u = _dwconv(u, p['cv_w'], p['cv_b'], CONV_WIDTH // 2, CONV_WIDTH // 2)
    u = _layernorm(u, p['cv_ln_g'], p['cv_ln_b'])
    return jax.nn.silu(u)


def _blockdiag(x, w, b):
    B, S, C = x.shape
    y = jnp.einsum('bsnc,ncd->bsnd', x.reshape(B, S, LRU_BLOCKS, LRU_BW), w).reshape(B, S, C)
    return (y + b).astype(jnp.float32)


def _rglru_dir(x, w_a, b_a, w_i, b_i, lam, h0, reverse):
    r = jax.nn.sigmoid(_blockdiag(x, w_a, b_a))
    i = jax.nn.sigmoid(_blockdiag(x, w_i, b_i))
    log_a = -LRU_C * r * jax.nn.softplus(-lam.astype(jnp.float32))
    a = jnp.exp(log_a)
    bx = jnp.sqrt(-jnp.expm1(2.0 * log_a)) * (i * x.astype(jnp.float32))

    def step(h, ab):
        h = ab[0] * h + ab[1]
        return h, h

    h_last, hs = lax.scan(step, h0.astype(jnp.float32), (a.swapaxes(0, 1), bx.swapaxes(0, 1)), reverse=reverse)
    return hs.swapaxes(0, 1), h_last


def _recurrent(xr, gate, p, h0):
    xc = _dwconv(xr, p['lru_conv_w'], p['lru_conv_b'], LRU_CONV_WIDTH // 2, LRU_CONV_WIDTH - 1 - LRU_CONV_WIDTH // 2)
    hf, lf = _rglru_dir(xc, p['lru_wa'][0], p['lru_ba'][0], p['lru_wi'][0], p['lru_bi'][0], p['lru_lam'][0], h0[:, 0], False)
    hb, lb = _rglru_dir(xc, p['lru_wa'][1], p['lru_ba'][1], p['lru_wi'][1], p['lru_bi'][1], p['lru_lam'][1], h0[:, 1], True)
    y = (hf + hb).astype(xr.dtype) * jax.nn.gelu(gate)
    return y, jnp.stack([lf, lb], axis=1)


def _layer(x, cond, p, ctx):
    B, S, _ = x.shape
    mod = (jax.nn.silu(cond) @ p['w_mod'] + p['b_mod'])[:, None, :]
    sh1, sc1, g1, sh2, sc2, g2 = jnp.split(mod, 6, axis=-1)
    h = _rmsnorm(x, p['ln1_g']) * (1 + sc1) + sh1
    z = h @ p['w_in']
    q, k, v, cva, cvg, lx, lg = jnp.split(z, IN_SPLITS, axis=-1)
    q = q.reshape(B, S, N_HEADS, HEAD_DIM)
    k = k.reshape(B, S, N_HEADS, HEAD_DIM)
    v = v.reshape(B, S, N_HEADS, HEAD_DIM)
    if ctx is None:
        att = _context_attention(q, k, v)
        h0 = jnp.zeros((B, 2, D_LRU), jnp.float32)
    else:
        k_ctx, v_ctx, h0 = ctx
        att = _latent_attention(q, k, v, k_ctx, v_ctx, p['na_bias'])
    cv = _conv_module(cva, cvg, p)
    rec, h_last = _recurrent(lx, lg, p, h0)
    mix = jnp.concatenate([att.reshape(B, S, D_ATT), cv, rec], axis=-1) @ p['w_out']
    x = x + g1 * mix
    h = _rmsnorm(x, p['ln2_g']) * (1 + sc2) + sh2
    u = _dwconv(h @ p['ffn_up'], p['ffn_conv_w'], p['ffn_conv_b'], FFN_CONV_WIDTH // 2, FFN_CONV_WIDTH // 2)
    ug, uv = jnp.split(u, 2, axis=-1)
    x = x + g2 * ((jax.nn.silu(ug) * uv) @ p['ffn_down'])
    return x, k, v, h_last


def setup_inputs(seed: int = 0) -> dict:
    key = jax.random.key(seed)
    ks = jax.random.split(key, 48)
    cnt = [0]

    def nxt():
        kk = ks[cnt[0]]
        cnt[0] += 1
        return kk

    def nrm(shape, scale=1.0):
        return jax.random.normal(nxt(), shape, jnp.float32) * scale

    L, D = DEPTH, D_MODEL
    u = jax.random.uniform(nxt(), (L, 2, D_LRU), jnp.float32, 0.9, 0.999)
    a_base = u ** (1.0 / LRU_C)
    lru_lam = jnp.log(a_base) - jnp.log1p(-a_base)
    return {
        'x_prompt': nrm((BATCH, SEQ, D)),
        'x_sample': nrm((DEC_BATCH, DEC_SEQ, D)),
        'cache_k': nrm((DEC_BATCH, DEPTH, PAST_LEN, N_HEADS, HEAD_DIM)),
        'cache_v': nrm((DEC_BATCH, DEPTH, PAST_LEN, N_HEADS, HEAD_DIM)),
        'state_lru': nrm((DEC_BATCH, DEPTH, 2, D_LRU), 0.5),
        'c': nrm((DEC_BATCH, D)),
        'c_ctx': nrm((D,)),
        'ln1_g': 1.0 + nrm((L, D), 0.02),
        'w_mod': nrm((L, D, 6 * D), 0.5 * D ** -0.5),
        'b_mod': nrm((L, 6 * D), 0.02),
        'w_in': nrm((L, D, D_IN), D ** -0.5),
        'na_bias': nrm((L, N_HEADS, 2 * WIN_ROWS - 1, 2 * WIN_COLS - 1), 0.5),
        'cv_w': nrm((L, CONV_WIDTH, D_CONV), CONV_WIDTH ** -0.5),
        'cv_b': nrm((L, D_CONV), 0.02),
        'cv_ln_g': 1.0 + nrm((L, D_CONV), 0.02),
        'cv_ln_b': nrm((L, D_CONV), 0.02),
        'lru_conv_w': nrm((L, LRU_CONV_WIDTH, D_LRU), LRU_CONV_WIDTH ** -0.5),
        'lru_conv_b': nrm((L, D_LRU), 0.02),
        'lru_wa': nrm((L, 2, LRU_BLOCKS, LRU_BW, LRU_BW), LRU_BW ** -0.5),
        'lru_ba': nrm((L, 2, D_LRU), 0.02),
        'lru_wi': nrm((L, 2, LRU_BLOCKS, LRU_BW, LRU_BW), LRU_BW ** -0.5),
        'lru_bi': nrm((L, 2, D_LRU), 0.02),
        'lru_lam': lru_lam,
        'w_out': nrm((L, D_MIX, D), D_MIX ** -0.5),
        'ln2_g': 1.0 + nrm((L, D), 0.02),
        'ffn_up': nrm((L, D, 2 * D_FF), D ** -0.5),
        'ffn_conv_w': nrm((L, FFN_CONV_WIDTH, 2 * D_FF), FFN_CONV_WIDTH ** -0.5),
        'ffn_conv_b': nrm((L, 2 * D_FF), 0.02),
        'ffn_down': nrm((L, D_FF, D), D_FF ** -0.5),
        'final_g': 1.0 + nrm((D,), 0.02),
    }


def reference(x_prompt, x_sample, cache_k, cache_v, state_lru, c, c_ctx, ln1_g, w_mod, b_mod, w_in, na_bias,
              cv_w, cv_b, cv_ln_g, cv_ln_b, lru_conv_w, lru_conv_b, lru_wa, lru_ba, lru_wi, lru_bi, lru_lam,
              w_out, ln2_g, ffn_up, ffn_conv_w, ffn_conv_b, ffn_down, final_g):
    yp = x_prompt
    ys = x_sample
    cond_ctx = c_ctx[None, :]
    ks, vs, hs = [], [], []
    for l in range(DEPTH):
        p = {'ln1_g': ln1_g[l], 'w_mod': w_mod[l], 'b_mod': b_mod[l], 'w_in': w_in[l], 'na_bias': na_bias[l],
             'cv_w': cv_w[l], 'cv_b': cv_b[l], 'cv_ln_g': cv_ln_g[l], 'cv_ln_b': cv_ln_b[l],
             'lru_conv_w': lru_conv_w[l], 'lru_conv_b': lru_conv_b[l], 'lru_wa': lru_wa[l], 'lru_ba': lru_ba[l],
             'lru_wi': lru_wi[l], 'lru_bi': lru_bi[l], 'lru_lam': lru_lam[l], 'w_out': w_out[l],
             'ln2_g': ln2_g[l], 'ffn_up': ffn_up[l], 'ffn_conv_w': ffn_conv_w[l], 'ffn_conv_b': ffn_conv_b[l],
             'ffn_down': ffn_down[l]}
        yp, k_l, v_l, h_l = _layer(yp, cond_ctx, p, None)
        ks.append(k_l)
        vs.append(v_l)
        hs.append(h_l)
        ys, _, _, _ = _layer(ys, c, p, (cache_k[:, l], cache_v[:, l], state_lru[:, l]))
    y_prompt = _rmsnorm(yp, final_g)
    y_sample = _rmsnorm(ys, final_g)
    new_k = jnp.stack(ks, axis=1)
    new_v = jnp.stack(vs, axis=1)
    new_lru = jnp.stack(hs, axis=1)
    return (y_prompt, y_sample, new_k, new_v, new_lru)
```

```python
import numpy as np
from contextlib import ExitStack
import concourse.bass as bass
import concourse.mybir as mybir
from concourse.bass_utils import run_bass_kernel_spmd

F32 = mybir.dt.float32
BF16 = mybir.dt.bfloat16
AF = mybir.ActivationFunctionType
ALU = mybir.AluOpType

D = 2048
L = 4
NH = 8
DFF = 5632
TS = 1024
TP = 512
TT = TS + TP
NEG = -1e30
NLAYERS = 4
STAGE = 99
ATT_STOP = 0
SKIP_KV_OUT = 0
NCORES = 8


class _Stop(Exception):
    pass
EPS = 1e-6
ATT_SCALE = 128 ** -0.5


class _Rec:
    def __init__(self):
        self.call = None

    def __getattr__(self, name):
        def f(*a, **k):
            self.call = (name, a, k)
            return self
        return f


class _Op:
    __slots__ = ("eng", "fn", "deps", "signal", "dma", "dsem", "dval", "cnt")

    def __init__(self, eng, fn, dma):
        self.eng = eng
        rec = _Rec()
        fn(rec)
        self.fn = rec.call
        self.deps = []
        self.signal = False
        self.dma = dma
        self.dsem = None
        self.dval = 0
        self.cnt = 0


class Sched:
    ENGS = ("pe", "act", "dve", "pool", "sp")

    def __init__(self, nc, stack, n_dma_sems=16):
        self.nc = nc
        self.ops = {e: [] for e in self.ENGS}
        self.last_w = {}
        self.readers = {}
        self.sems = {e: stack.enter_context(nc.semaphore("s_" + e)) for e in self.ENGS}
        self.dsems = [stack.enter_context(nc.semaphore("d_%d" % i)) for i in range(n_dma_sems)]
        self.dcount = [0] * n_dma_sems
        self.dlast = [None] * n_dma_sems
        self.dnext = 0
        self.pending_bar = {e: None for e in self.ENGS}

    def barrier(self):
        deps = []
        for e in self.ENGS:
            if self.ops[e]:
                deps.append(self.ops[e][-1])
        for d in self.dlast:
            if d is not None:
                deps.append(d)
        for e in self.ENGS:
            old = self.pending_bar[e]
            self.pending_bar[e] = deps if old is None else deps

    def op(self, eng, fn, reads=(), writes=(), dma=False, nobar=False):
        o = _Op(eng, fn, dma)
        deps = []
        if not nobar and self.pending_bar[eng] is not None:
            deps.extend(self.pending_bar[eng])
            self.pending_bar[eng] = None
        for k in reads:
            w = self.last_w.get(k)
            if w is not None:
                deps.append(w)
        for k in writes:
            w = self.last_w.get(k)
            if w is not None:
                deps.append(w)
            rs = self.readers.get(k)
            if rs:
                deps.extend(rs)
        if dma:
            s = self.dnext
            self.dnext = (self.dnext + 1) % len(self.dsems)
            if self.dlast[s] is not None:
                deps.append(self.dlast[s])
            self.dcount[s] += 16
            o.dsem = s
            o.dval = self.dcount[s]
            self.dlast[s] = o
        seen = set()
        for d in deps:
            if d is o or id(d) in seen:
                continue
            seen.add(id(d))
            if d.eng == "pe" and eng == "pe" and not d.dma and not dma:
                continue
            o.deps.append(d)
            d.signal = True
        for k in reads:
            lst = self.readers.setdefault(k, [])
            if not dma:
                lst[:] = [r for r in lst if r.dma or r.eng != eng]
            lst.append(o)
        for k in writes:
            self.last_w[k] = o
            self.readers[k] = []
        self.ops[eng].append(o)
        return o

    def emit(self, final_waits=()):
        nc = self.nc
        for e in self.ENGS:
            c = 0
            for o in self.ops[e]:
                if o.signal and not o.dma:
                    c += 1
                    o.cnt = c
        final_waits = list(final_waits)

        def run(e, engh):
            waited = {}

            def wait(d):
                if d.dma:
                    key, sem, val = ("d", d.dsem), self.dsems[d.dsem], d.dval
                else:
                    key, sem, val = ("e", d.eng), self.sems[d.eng], d.cnt
                if waited.get(key, 0) >= val:
                    return
                waited[key] = val
                engh.wait_ge(sem, val)

            for o in self.ops[e]:
                for d in o.deps:
                    wait(d)
                name, a, k = o.fn
                ins = getattr(engh, name)(*a, **k)
                if o.dma:
                    ins.then_inc(self.dsems[o.dsem], 16)
                elif o.signal:
                    ins.then_inc(self.sems[e], 1)
            if e == "sp":
                for d in final_waits:
                    wait(d)

        with nc.Block() as block:
            @block.tensor
            def _(t):
                run("pe", t)

            @block.scalar
            def _(t):
                run("act", t)

            @block.vector
            def _(t):
                run("dve", t)

            @block.gpsimd
            def _(t):
                run("pool", t)

            @block.sync
            def _(t):
                run("sp", t)


PV_ITEMS = [
    ("ln1g", L * 16), ("ln2g", L * 16), ("fing", 16), ("bmod", L * 96), ("cond", 32),
    ("cvw", L * 4 * 31), ("cvb", L * 4), ("cvg", L * 4), ("cvlb", L * 4),
    ("lcw", L * 4 * 4), ("lcb", L * 4), ("lba", L * 8), ("lbi", L * 8), ("lam", L * 8), ("h0", L * 8),
    ("fcw", L * 88 * 3), ("fcb", L * 88),
]
PV_OFF = {}
_o = 0
for _n, _c in PV_ITEMS:
    PV_OFF[_n] = _o
    _o += _c
NV = _o


def _fm(v, nch):
    v = np.asarray(v, np.float32)
    lead = v.shape[:-1]
    return np.moveaxis(v.reshape(*lead, nch, 128), -1, 0)


def _pair_win(i):
    r0, r1 = 2 * i, 2 * i + 1
    s0 = min(max(r0 - 4, 0), 8)
    s1 = min(max(r1 - 4, 0), 8)
    a = s0 & ~1
    e = (s1 + 8 + 1) & ~1
    return a, e - a


def _build_bias(na_bias):
    na = np.asarray(na_bias, np.float32)
    cols = np.arange(64)
    cs = np.clip(cols - 8, 0, 48)
    inwin = (cols[None, :] >= cs[:, None]) & (cols[None, :] < cs[:, None] + 16)
    co = np.clip(cols[None, :] - cols[:, None], -15, 15) + 15
    out = np.full((L, NH, 8, 128, 640), NEG, np.float32)
    for i in range(8):
        a, nw = _pair_win(i)
        for half in range(2):
            r = 2 * i + half
            st = min(max(r - 4, 0), 8)
            for j in range(nw):
                kr = a + j
                if not (st <= kr < st + 8):
                    continue
                blk = na[:, :, kr - r + 7, :][:, :, co]
                blk = np.where(inwin[None, None], blk, np.float32(NEG))
                out[:, :, i, half * 64:(half + 1) * 64, j * 64:(j + 1) * 64] = blk
    return out


def build_nc():
    nc = bass.Bass("TRN2", target_bir_lowering=False)

    def din(name, shape):
        return nc.dram_tensor(name, list(shape), F32, kind="ExternalInput").ap()

    def dout(name, shape):
        return nc.dram_tensor(name, list(shape), F32, kind="ExternalOutput").ap()

    xs_d = din("xs", [TS, D])
    xp_d = din("xp", [TP, D])
    ck_d = din("ck", [NLAYERS, 512, 1024])
    cv_d = din("cvv", [NLAYERS, 512, 1024])
    pv_d = din("pv", [128, NV])
    wmod_d = din("wmod", [NLAYERS, D, 6 * D])
    win_d = din("win", [NLAYERS, D, 5120])
    wout_d = din("wout", [NLAYERS, D, D])
    fup_d = din("fup", [NLAYERS, D, 2 * DFF])
    fdn_d = din("fdn", [NLAYERS, DFF, D])
    lw_d = din("lw", [NLAYERS, 128, 16, 128])
    bt_d = din("bt", [NLAYERS, NH, 8, 128, 640])
    id_d = din("ident", [128, 128])
    ys_d = dout("ys", [TS, D])
    yp_d = dout("yp", [TP, D])
    nk_d = dout("nk", [2, L, 256, 1024])
    nv_d = dout("nv", [2, L, 256, 1024])
    nl_d = dout("nl", [64, 128])
    X_d = nc.dram_tensor("Xscr", [128, 16, TT], F32).ap()

    st = ExitStack()
    with st:
        S = Sched(nc, st)

        def sb(name, shape, dt):
            return st.enter_context(nc.sbuf_tensor(name, list(shape), dt))

        pvt = sb("pvt", [128, NV], F32)
        ident_f = sb("ident_f", [128, 128], F32)
        ident_b = sb("ident_b", [128, 128], BF16)
        ones_d = sb("ones_d", [128, 128], BF16)
        ones_c = sb("ones_c", [128, 128], BF16)
        ones_1 = sb("ones_1", [128, 128], BF16)
        scond = sb("scond", [128, 32], BF16)
        modt = sb("modt", [128, L, 96, 2], F32)
        amod = sb("amod", [128, L, 2, 2, 16], F32)
        cneg = sb("cneg", [128, L * 8], F32)
        nl_t = sb("nl_t", [128, 64], F32)
        cst = sb("cst", [128, 2], F32)
        hb = sb("hb", [128, 16, TS], BF16)
        wbuf = [sb("wb%d" % i, [128, 16, 512], BF16) for i in range(3)]
        big = sb("big", [128, 22528], F32)
        lwt = sb("lwt", [128, 16, 128], BF16)
        misc = sb("misc", [128, 4224], F32)
        rstd = [misc[:, i * 512:(i + 1) * 512] for i in range(2)]
        tmpf = [misc[:, 1024 + i * 512:1024 + (i + 1) * 512] for i in range(4)]
        tmpb = [misc[:, 3072 + i * 256:3072 + (i + 1) * 256].bitcast(BF16) for i in range(4)]
        sbx_up = [misc[:, i * 1028:(i + 1) * 1028] for i in range(2)]
        sbx_cc = [misc[:, 2056 + i * 1024:2056 + (i + 1) * 1024] for i in range(2)]
        xio = [misc[:, i * 512:(i + 1) * 512] for i in range(4)]
        ps = [st.enter_context(nc.psum_tensor("ps%d" % i, [128, 512], F32)) for i in range(6)]
        psbT = [st.enter_context(nc.psum_tensor("psbT%d" % i, [128, 1024], BF16)) for i in range(2)]
        psb = [psbT[0][:, 0:512], psbT[1][:, 0:512]]

        def bview(off_f32, shape, dt):
            n = int(np.prod(shape[1:]))
            if dt == BF16:
                v = big[:, off_f32:off_f32 + (n + 1) // 2].bitcast(BF16)
                v = v[:, 0:n]
            else:
                v = big[:, off_f32:off_f32 + n]
            if len(shape) == 3:
                v = v.rearrange("p (a b) -> p a b", a=shape[1])
            elif len(shape) == 4:
                v = v.rearrange("p (a b c) -> p a b c", a=shape[1], b=shape[2])
            return v

        mix = bview(0, [128, 16, TS], BF16)
        act = bview(0, [128, 44, TS], BF16)
        xt16 = [bview(6144, [128, 16, 512], F32), bview(6144 + 8192, [128, 16, 512], F32)]
        SCR = 8192

        cnt = {"w": 0, "bank": 0, "tf": 0, "tb": 0, "xio": 0, "ev": 0}

        def pvc(name, idx):
            o = PV_OFF[name] + idx
            return pvt[:, o:o + 1]

        def load_panel(src_ap, nk, ncol):
            b = cnt["w"] % 3
            cnt["w"] += 1
            S.op("pool", lambda e: e.dma_start(out=wbuf[b][:, 0:nk, 0:ncol], in_=src_ap),
                 writes=[("wb", b)], dma=True, nobar=True)
            return b

        def wview(w2d, c0, ncol, k0=0, nk=16):
            return w2d.rearrange("(kc p) n -> p kc n", p=128)[:, k0:k0 + nk, c0:c0 + ncol]

        def next_bank(lo=0, n=4):
            b = lo + cnt["bank"] % n
            cnt["bank"] += 1
            return b

        def evac_eng():
            cnt["ev"] += 1
            return "act" if cnt["ev"] % 2 else "dve"

        def copy_op(eng, out, in_, reads, writes):
            if eng == "act":
                return S.op("act", lambda e: e.activation(out=out, in_=in_, func=AF.Copy), reads=reads, writes=writes)
            return S.op(eng, lambda e: e.tensor_copy(out=out, in_=in_), reads=reads, writes=writes)

        def mm_fm(b, nmc, src, srckey, Tg, evac, nk=16):
            for mc in range(nmc):
                for tt in range(Tg // 512):
                    bank = next_bank()
                    for kc in range(nk):
                        S.op("pe", lambda e, kc=kc, mc=mc, tt=tt, bank=bank: e.matmul(
                            ps[bank][:], wbuf[b][:, kc, mc * 128:(mc + 1) * 128],
                            src[:, kc, tt * 512:(tt + 1) * 512], start=(kc == 0), stop=(kc == nk - 1)),
                            reads=[("wb", b), (srckey, kc, tt)], writes=[("ps", bank)])
                    evac(mc, tt, bank)

        def mm_tm(b, src, srckey, Tg, evac):
            for ti in range(Tg // 128):
                bank = next_bank()
                for kc in range(16):
                    S.op("pe", lambda e, kc=kc, ti=ti, bank=bank: e.matmul(
                        ps[bank][:], src[:, kc, ti * 128:(ti + 1) * 128], wbuf[b][:, kc, :],
                        start=(kc == 0), stop=(kc == 15)),
                        reads=[("wb", b), (srckey, kc, ti // 4)], writes=[("ps", bank)])
                evac(ti, bank)

        S.op("sp", lambda e: e.dma_start(out=pvt[:], in_=pv_d), writes=["pv"], dma=True)
        S.op("sp", lambda e: e.dma_start(out=ident_f[:], in_=id_d), writes=["idf"], dma=True)
        S.op("pool", lambda e: e.dma_start(out=ident_b[:], in_=id_d), writes=["idb"], dma=True)
        S.op("dve", lambda e: e.memset(ones_d[:], 1.0 / D), writes=["ones_d"])
        S.op("dve", lambda e: e.memset(ones_c[:], 1.0 / 512), writes=["ones_c"])
        S.op("dve", lambda e: e.memset(ones_1[:], 1.0), writes=["ones_1"])
        S.op("dve", lambda e: e.memset(nl_t[:], 0.0), writes=["nl_t"])
        S.op("dve", lambda e: e.memset(cst[:, 0:1], EPS), writes=["cst"])
        S.op("dve", lambda e: e.memset(cst[:, 1:2], 1.0), writes=["cst"])
        oc = PV_OFF["cond"]
        S.op("act", lambda e: e.activation(out=scond[:], in_=pvt[:, oc:oc + 32], func=AF.Silu),
             reads=["pv"], writes=["scond"])
        ol = PV_OFF["lam"]
        S.op("act", lambda e: e.activation(out=cneg[:], in_=pvt[:, ol:ol + 32], func=AF.Exp, scale=-1.0),
             reads=["pv"], writes=["cneg"])
        S.op("act", lambda e: e.activation(out=cneg[:], in_=cneg[:], func=AF.Ln, bias=cst[:, 1:2]),
             reads=["cneg", "cst"], writes=["cneg"])
        S.op("dve", lambda e: e.tensor_scalar(cneg[:], cneg[:], -8.0, None, ALU.mult),
             reads=["cneg"], writes=["cneg"])

        def mod_finish(l):
            for ci in range(2):
                for wn, (sc_o, g_name) in enumerate(((16, "ln1g"), (64, "ln2g"))):
                    og = PV_OFF[g_name] + l * 16
                    S.op("dve", lambda e: e.scalar_tensor_tensor(
                        out=amod[:, l, ci, wn, :], in0=modt[:, l, sc_o:sc_o + 16, ci], scalar=1.0,
                        in1=pvt[:, og:og + 16], op0=ALU.add, op1=ALU.mult),
                        reads=[("modt", l), "pv"], writes=[("amod", l)])

        mod_state = {"l": 1, "p": 0}

        def mod_step():
            l2, p = mod_state["l"], mod_state["p"]
            if l2 >= NLAYERS:
                return
            b = load_panel(wview(wmod_d[l2], p * 512, 512), 16, 512)
            for m in range(4):
                for kc in range(16):
                    S.op("pe", lambda e: e.matmul(
                        ps[5][:, 384 + 2 * m:386 + 2 * m], wbuf[b][:, kc, m * 128:(m + 1) * 128],
                        scond[:, 2 * kc:2 * kc + 2], start=(kc == 0), stop=(kc == 15)),
                        reads=[("wb", b), "scond"], writes=[("ps", 5)])
            ob = PV_OFF["bmod"] + l2 * 96 + p * 4
            for ci in range(2):
                S.op("dve", lambda e: e.tensor_tensor(
                    out=modt[:, l2, p * 4:(p + 1) * 4, ci], in0=ps[5][:, 384 + ci:392:2], in1=pvt[:, ob:ob + 4],
                    op=ALU.add),
                    reads=[("ps", 5), "pv"], writes=[("modt", l2)])
            mod_state["p"] += 1
            if mod_state["p"] == 24:
                mod_finish(l2)
                mod_state["l"], mod_state["p"] = l2 + 1, 0

        for l in range(1):
            for p in range(24):
                b = load_panel(wview(wmod_d[l], p * 512, 512), 16, 512)
                for m in range(4):
                    c0 = (p * 4 + m) * 2
                    for kc in range(16):
                        S.op("pe", lambda e, b=b, m=m, kc=kc, c0=c0: e.matmul(
                            ps[5][:, c0:c0 + 2], wbuf[b][:, kc, m * 128:(m + 1) * 128],
                            scond[:, 2 * kc:2 * kc + 2], start=(kc == 0), stop=(kc == 15)),
                            reads=[("wb", b), "scond"], writes=[("ps", 5)])
            ob = PV_OFF["bmod"] + l * 96
            for ci in range(2):
                S.op("dve", lambda e, l=l, ci=ci, ob=ob: e.tensor_tensor(
                    out=modt[:, l, :, ci], in0=ps[5][:, ci:192:2], in1=pvt[:, ob:ob + 96], op=ALU.add),
                    reads=[("ps", 5), "pv"], writes=[("modt", l)])
            for ci in range(2):
                for wn, (sc_o, g_name) in enumerate(((16, "ln1g"), (64, "ln2g"))):
                    og = PV_OFF[g_name] + l * 16
                    S.op("dve", lambda e, l=l, ci=ci, wn=wn, sc_o=sc_o, og=og: e.scalar_tensor_tensor(
                        out=amod[:, l, ci, wn, :], in0=modt[:, l, sc_o:sc_o + 16, ci], scalar=1.0,
                        in1=pvt[:, og:og + 16], op0=ALU.add, op1=ALU.mult),
                        reads=[("modt", l), "pv"], writes=[("amod", l)])

        S.barrier()
        tin = [bview(0, [128, D], F32), bview(2048, [128, D], F32)]
        xst = [bview(4096, [128, 16, 128], F32), bview(6144, [128, 16, 128], F32)]
        for ti in range(TT // 128):
            src = xs_d[ti * 128:(ti + 1) * 128, :] if ti < 8 else xp_d[(ti - 8) * 128:(ti - 7) * 128, :]
            q = ti % 2
            S.op("sp", lambda e, q=q, src=src: e.dma_start(out=tin[q], in_=src), writes=[("tin", q)], dma=True)
            for cg in range(4):
                bank = next_bank()
                for j in range(4):
                    kc = cg * 4 + j
                    S.op("pe", lambda e, q=q, kc=kc, j=j, bank=bank: e.transpose(
                        ps[bank][:, j * 128:(j + 1) * 128], tin[q][:, kc * 128:(kc + 1) * 128], ident_f[:]),
                        reads=[("tin", q), "idf"], writes=[("ps", bank)])
                copy_op(evac_eng(), xst[q][:, cg * 4:cg * 4 + 4, :],
                        ps[bank][:].rearrange("p (a b) -> p a b", a=4), [("ps", bank)], [("xst", q, cg)])
            S.op("sp", lambda e, q=q, ti=ti: e.dma_start(out=X_d[:, :, ti * 128:(ti + 1) * 128], in_=xst[q]),
                 reads=[("xst", q, cg) for cg in range(4)], writes=[("X", ti // 4, c) for c in range(16)], dma=True)

        def norm(G, l, wn):
            Tg, xoff, ci = G["T"], G["xoff"], G["ci"]
            for tt in range(Tg // 512):
                q = tt % 2
                gt = (xoff + tt * 512) // 512
                S.op("sp", lambda e: e.dma_start(
                    out=xt16[q], in_=X_d[:, :, xoff + tt * 512: xoff + (tt + 1) * 512]),
                    reads=[("X", gt, c) for c in range(16)], writes=[("xt16", q)], dma=True)
                for kc in range(16):
                    tb = cnt["tb"] % 4
                    cnt["tb"] += 1
                    S.op("act", lambda e: e.activation(out=tmpb[tb], in_=xt16[q][:, kc, :], func=AF.Square),
                         reads=[("xt16", q)], writes=[("tmpb", tb)])
                    S.op("pe", lambda e: e.matmul(ps[4][:], ones_d[:], tmpb[tb], start=(kc == 0), stop=(kc == 15)),
                         reads=[("tmpb", tb), "ones_d"], writes=[("ps", 4)])
                S.op("act", lambda e: e.activation(out=rstd[q], in_=ps[4][:], func=AF.Sqrt, bias=cst[:, 0:1]),
                     reads=[("ps", 4), "cst"], writes=[("rstd", q)])
                S.op("dve", lambda e: e.reciprocal(rstd[q], rstd[q]), reads=[("rstd", q)], writes=[("rstd", q)])
                sh_o = 0 if wn == 0 else 48
                for kc in range(16):
                    tf = cnt["tf"] % 4
                    cnt["tf"] += 1
                    S.op("dve", lambda e: e.scalar_tensor_tensor(
                        out=tmpf[tf], in0=xt16[q][:, kc, :], scalar=amod[:, l, ci, wn, kc:kc + 1],
                        in1=rstd[q], op0=ALU.mult, op1=ALU.mult),
                        reads=[("xt16", q), ("rstd", q), ("amod", l)], writes=[("tmpf", tf)])
                    S.op("act", lambda e: e.activation(
                        out=hb[:, kc, tt * 512:(tt + 1) * 512], in_=tmpf[tf], func=AF.Identity,
                        bias=modt[:, l, sh_o + kc, ci:ci + 1]),
                        reads=[("tmpf", tf), ("modt", l)], writes=[("h", kc, tt)])

        finals = []

        def resid(G, l, chunk, tt, bank, g_o):
            Tg, xoff, ci = G["T"], G["xoff"], G["ci"]
            xi = cnt["xio"] % 4
            cnt["xio"] += 1
            gt = (xoff + tt * 512) // 512
            c0 = xoff + tt * 512
            S.op("sp", lambda e: e.dma_start(out=xio[xi], in_=X_d[:, chunk, c0:c0 + 512]),
                 reads=[("X", gt, chunk)], writes=[("xio", xi)], dma=True)
            S.op("dve", lambda e: e.scalar_tensor_tensor(
                out=xio[xi], in0=ps[bank][:], scalar=modt[:, l, g_o + chunk, ci:ci + 1], in1=xio[xi],
                op0=ALU.mult, op1=ALU.add),
                reads=[("ps", bank), ("xio", xi), ("modt", l)], writes=[("xio", xi)])
            S.op("sp", lambda e: e.dma_start(out=X_d[:, chunk, c0:c0 + 512], in_=xio[xi]),
                 reads=[("xio", xi)], writes=[("X", gt, chunk)], dma=True)

        def conv_module(G, l):
            Tg, seqs = G["T"], G["seqs"]
            ns, Ls = len(seqs), seqs[0][1]
            Lp = Ls + 30
            cva = bview(SCR, [128, 4, Tg], F32)
            sg = bview(SCR + 4 * Tg, [128, 4, Tg], F32)
            upad = bview(SCR + 8 * Tg, [128, 4, ns * Lp], BF16)
            dg = bview(SCR + 8 * Tg + 2 * ns * Lp + 8, [128, 31, 128], BF16)
            mus = bview(SCR + 8 * Tg + 2 * ns * Lp + 8 + 1984, [128, 512], F32)
            S.op("dve", lambda e: e.memset(upad, 0.0), writes=["upad"])
            b = load_panel(wview(win_d[l], 3072, 512), 16, 512)

            def ev_a(mc, tt, bank):
                copy_op("dve", cva[:, mc, tt * 512:(tt + 1) * 512], ps[bank][:], [("ps", bank)], [("cva", mc, tt)])
            mm_fm(b, 4, hb, "h", Tg, ev_a)
            b = load_panel(wview(win_d[l], 3584, 512), 16, 512)

            def ev_g(mc, tt, bank):
                S.op("act", lambda e: e.activation(out=sg[:, mc, tt * 512:(tt + 1) * 512], in_=ps[bank][:],
                                                   func=AF.Sigmoid),
                     reads=[("ps", bank)], writes=[("sg", mc, tt)])
            mm_fm(b, 4, hb, "h", Tg, ev_g)
            for c in range(4):
                for si, (so, sl) in enumerate(seqs):
                    S.op("dve", lambda e, c=c, si=si, so=so, sl=sl: e.tensor_tensor(
                        out=upad[:, c, si * Lp + 15: si * Lp + 15 + sl], in0=cva[:, c, so:so + sl],
                        in1=sg[:, c, so:so + sl], op=ALU.mult),
                        reads=[("cva", c, t) for t in range(Tg // 512)] + [("sg", c, t) for t in range(Tg // 512)]
                        + ["upad"], writes=[("upc", c)])
            for c in range(4):
                for k in range(31):
                    ow = PV_OFF["cvw"] + (l * 4 + c) * 31 + k
                    S.op("dve", lambda e, k=k, ow=ow: e.tensor_scalar(
                        dg[:, k, :], ident_b[:], pvt[:, ow:ow + 1], None, ALU.mult),
                        reads=["idb", "pv"], writes=[("dg", k)])
                obb = PV_OFF["cvb"] + l * 4 + c
                for si, (so, sl) in enumerate(seqs):
                    for t0 in range(0, sl, 512):
                        n = min(512, sl - t0)
                        bank = next_bank()
                        for ki, k in enumerate([15] + [kk for kk in range(31) if kk != 15]):
                            S.op("pe", lambda e, c=c, k=k, ki=ki, si=si, t0=t0, n=n, bank=bank: e.matmul(
                                ps[bank][:, 0:n], dg[:, k, :], upad[:, c, si * Lp + t0 + k: si * Lp + t0 + k + n],
                                start=(ki == 0), stop=(ki == 30)),
                                reads=[("dg", k), ("upc", c)], writes=[("ps", bank)])
                        S.op("act", lambda e, c=c, so=so, t0=t0, n=n, bank=bank, obb=obb: e.activation(
                            out=cva[:, c, so + t0: so + t0 + n], in_=ps[bank][:, 0:n], func=AF.Identity,
                            bias=pvt[:, obb:obb + 1]),
                            reads=[("ps", bank), "pv"], writes=[("cc", c, (so + t0) // 512)])
            for tt in range(Tg // 512):
                sl_ = slice(tt * 512, (tt + 1) * 512)
                for c in range(4):
                    tb = (cnt["tb"] // 2 * 2) % 4
                    cnt["tb"] = cnt["tb"] // 2 * 2 + 2
                    S.op("act", lambda e, c=c, tb=tb: e.activation(out=tmpb[tb], in_=cva[:, c, sl_], func=AF.Square),
                         reads=[("cc", c, tt)], writes=[("tmpb", tb)])
                    S.op("dve", lambda e, c=c, tb=tb: e.tensor_copy(out=tmpb[tb + 1], in_=cva[:, c, sl_]),
                         reads=[("cc", c, tt)], writes=[("tmpb", tb + 1)])
                    S.op("pe", lambda e, c=c, tb=tb: e.matmul(ps[4][:], ones_c[:], tmpb[tb + 1],
                                                              start=(c == 0), stop=(c == 3)),
                         reads=[("tmpb", tb + 1), "ones_c"], writes=[("ps", 4)])
                    S.op("pe", lambda e, c=c, tb=tb: e.matmul(ps[5][:], ones_c[:], tmpb[tb],
                                                              start=(c == 0), stop=(c == 3)),
                         reads=[("tmpb", tb), "ones_c"], writes=[("ps", 5)])
                q = tt % 2
                S.op("act", lambda e: e.activation(out=mus, in_=ps[4][:], func=AF.Copy),
                     reads=[("ps", 4)], writes=["mus"])
                S.op("dve", lambda e, q=q: e.tensor_tensor(out=rstd[q], in0=mus, in1=mus, op=ALU.mult),
                     reads=["mus"], writes=[("rstd", q)])
                S.op("dve", lambda e, q=q: e.tensor_tensor(out=rstd[q], in0=ps[5][:], in1=rstd[q],
                                                           op=ALU.subtract),
                     reads=[("ps", 5), ("rstd", q)], writes=[("rstd", q)])
                S.op("act", lambda e: e.activation(out=rstd[q], in_=rstd[q], func=AF.Sqrt, bias=cst[:, 0:1]),
                     reads=[("rstd", q), "cst"], writes=[("rstd", q)])
                S.op("dve", lambda e: e.reciprocal(rstd[q], rstd[q]), reads=[("rstd", q)], writes=[("rstd", q)])
                for c in range(4):
                    tf = cnt["tf"] % 4
                    cnt["tf"] += 1
                    S.op("dve", lambda e, c=c, tf=tf: e.tensor_tensor(out=tmpf[tf], in0=cva[:, c, sl_], in1=mus,
                                                                    op=ALU.subtract),
                         reads=[("cc", c, tt), "mus"], writes=[("tmpf", tf)])
                    S.op("dve", lambda e, q=q, tf=tf: e.tensor_tensor(out=tmpf[tf], in0=tmpf[tf], in1=rstd[q],
                                                                    op=ALU.mult),
                         reads=[("tmpf", tf), ("rstd", q)], writes=[("tmpf", tf)])
                    og_ = PV_OFF["cvg"] + l * 4 + c
                    ob_ = PV_OFF["cvlb"] + l * 4 + c
                    S.op("act", lambda e, c=c, tf=tf, og_=og_, ob_=ob_: e.activation(
                        out=mix[:, 8 + c, sl_], in_=tmpf[tf], func=AF.Silu,
                        scale=pvt[:, og_:og_ + 1], bias=pvt[:, ob_:ob_ + 1]),
                        reads=[("tmpf", tf), "pv"], writes=[("mix", 8 + c, tt)])

        def lru(G, l):
            Tg, seqs = G["T"], G["seqs"]
            ns, Ls = len(seqs), seqs[0][1]
            Lp = Ls + 4
            lxp = bview(SCR, [128, 4, ns * Lp], F32)
            lgf = bview(SCR + 4 * ns * Lp, [128, 4, Tg], F32)
            o = SCR + 4 * ns * Lp + 4 * Tg
            xc = bview(o, [128, Tg], F32)
            xcb = bview(o + Tg, [128, Tg], BF16)
            o2 = o + Tg + Tg // 2
            r_ = bview(o2, [128, Tg], F32)
            i_ = bview(o2 + Tg, [128, Tg], F32)
            hd = [bview(o2 + 2 * Tg, [128, Tg], F32), bview(o2 + 3 * Tg, [128, Tg], F32)]
            S.op("dve", lambda e: e.memset(lxp, 0.0), writes=["lxp"])
            S.op("pool", lambda e: e.dma_start(out=lwt[:], in_=lw_d[l]), writes=["lwt"], dma=True)
            b = load_panel(wview(win_d[l], 4096, 512), 16, 512)

            def ev_x(mc, tt, bank):
                for si, (so, sl) in enumerate(seqs):
                    lo, hi = max(so, tt * 512), min(so + sl, (tt + 1) * 512)
                    if lo >= hi:
                        continue
                    copy_op("dve", lxp[:, mc, si * Lp + 2 + lo - so: si * Lp + 2 + hi - so],
                            ps[bank][:, lo - tt * 512: hi - tt * 512], [("ps", bank), "lxp"], [("lx", mc, tt)])
            mm_fm(b, 4, hb, "h", Tg, ev_x)
            b = load_panel(wview(win_d[l], 4608, 512), 16, 512)

            def ev_g(mc, tt, bank):
                copy_op("act", lgf[:, mc, tt * 512:(tt + 1) * 512], ps[bank][:], [("ps", bank)], [("lg", mc, tt)])
            mm_fm(b, 4, hb, "h", Tg, ev_g)
            ntt = Tg // 512
            for c in range(4):
                lxk = [("lx", c, t) for t in range(ntt)]
                for si, (so, sl) in enumerate(seqs):
                    for k in range(4):
                        ow = PV_OFF["lcw"] + (l * 4 + c) * 4 + k
                        src = lxp[:, c, si * Lp + k: si * Lp + k + sl]
                        if k == 0:
                            obb = PV_OFF["lcb"] + l * 4 + c
                            S.op("dve", lambda e, src=src, so=so, sl=sl, ow=ow, obb=obb: e.tensor_scalar(
                                xc[:, so:so + sl], src, pvt[:, ow:ow + 1], pvt[:, obb:obb + 1], ALU.mult, ALU.add),
                                reads=lxk + ["pv"], writes=[("xc", si)])
                        else:
                            S.op("dve", lambda e, src=src, so=so, sl=sl, ow=ow: e.scalar_tensor_tensor(
                                out=xc[:, so:so + sl], in0=src, scalar=pvt[:, ow:ow + 1], in1=xc[:, so:so + sl],
                                op0=ALU.mult, op1=ALU.add),
                                reads=lxk + ["pv", ("xc", si)], writes=[("xc", si)])
                xck = [("xc", si) for si in range(ns)]
                S.op("act", lambda e: e.activation(out=xcb, in_=xc, func=AF.Copy), reads=xck, writes=["xcb"])
                for d in range(2):
                    for gi, (dst, bname) in enumerate(((r_, "lba"), (i_, "lbi"))):
                        obb = PV_OFF[bname] + (l * 2 + d) * 4 + c
                        for tt in range(ntt):
                            bank = next_bank()
                            wi = (d * 2 + gi) * 4 + c
                            S.op("pe", lambda e, wi=wi, tt=tt, bank=bank: e.matmul(
                                ps[bank][:], lwt[:, wi, :], xcb[:, tt * 512:(tt + 1) * 512], start=True, stop=True),
                                reads=["lwt", "xcb"], writes=[("ps", bank)])
                            S.op("act", lambda e, dst=dst, tt=tt, bank=bank, obb=obb: e.activation(
                                out=dst[:, tt * 512:(tt + 1) * 512], in_=ps[bank][:], func=AF.Sigmoid,
                                bias=pvt[:, obb:obb + 1]),
                                reads=[("ps", bank), "pv"], writes=[("gate", gi)])
                    oc_ = (l * 2 + d) * 4 + c
                    S.op("act", lambda e, oc_=oc_: e.activation(out=r_, in_=r_, func=AF.Exp, scale=cneg[:, oc_:oc_ + 1]),
                         reads=[("gate", 0), "cneg"], writes=[("gate", 0)])
                    tfs = misc[:, 0:Tg]
                    S.op("dve", lambda e: e.tensor_tensor(out=tfs, in0=r_, in1=r_, op=ALU.mult),
                         reads=[("gate", 0)], writes=["tfs"])
                    S.op("act", lambda e: e.activation(out=tfs, in_=tfs, func=AF.Sqrt, scale=-1.0, bias=cst[:, 1:2]),
                         reads=["tfs", "cst"], writes=["tfs"])
                    S.op("dve", lambda e: e.tensor_tensor(out=i_, in0=i_, in1=tfs, op=ALU.mult),
                         reads=[("gate", 1), "tfs"], writes=[("gate", 1)])
                    S.op("dve", lambda e: e.tensor_tensor(out=i_, in0=i_, in1=xc, op=ALU.mult),
                         reads=[("gate", 1)] + xck, writes=[("gate", 1)])
                    for si, (so, sl) in enumerate(seqs):
                        if G["h0"]:
                            oh = PV_OFF["h0"] + (l * 2 + d) * 4 + c
                            init = pvt[:, oh:oh + 1]
                        else:
                            init = 0.0
                        if d == 0:
                            oa, a0, a1 = hd[0][:, so:so + sl], r_[:, so:so + sl], i_[:, so:so + sl]
                        else:
                            oa = hd[1][:, so + sl - 1: so - 1 if so > 0 else None: -1]
                            a0 = r_[:, so + sl - 1: so - 1 if so > 0 else None: -1]
                            a1 = i_[:, so + sl - 1: so - 1 if so > 0 else None: -1]
                        S.op("dve", lambda e, oa=oa, a0=a0, a1=a1, init=init: e.tensor_tensor_scan(
                            out=oa, data0=a0, data1=a1, initial=init, op0=ALU.mult, op1=ALU.add),
                            reads=[("gate", 0), ("gate", 1), "pv"], writes=[("hd", d)])
                        if G["nl"]:
                            col = ((G["nl_b"] + si) * L + l) * 2 * 4 + d * 4 + c
                            tcol = so + sl - 1 if d == 0 else so
                            S.op("act", lambda e, d=d, col=col, tcol=tcol: e.activation(
                                out=nl_t[:, col:col + 1], in_=hd[d][:, tcol:tcol + 1], func=AF.Copy),
                                reads=[("hd", d)], writes=["nl_t"])
                lgk = [("lg", c, t) for t in range(ntt)]
                lgc = lgf[:, c, :]
                S.op("dve", lambda e: e.tensor_tensor(out=hd[0], in0=hd[0], in1=hd[1], op=ALU.add),
                     reads=[("hd", 0), ("hd", 1)], writes=[("hd", 0)])
                S.op("dve", lambda e, lgc=lgc: e.tensor_tensor(out=r_, in0=lgc, in1=lgc, op=ALU.mult),
                     reads=lgk + [("gate", 0)], writes=[("gate", 0)])
                S.op("dve", lambda e: e.tensor_scalar(r_, r_, 0.044715, 1.0, ALU.mult, ALU.add),
                     reads=[("gate", 0)], writes=[("gate", 0)])
                S.op("dve", lambda e, lgc=lgc: e.tensor_tensor(out=r_, in0=r_, in1=lgc, op=ALU.mult),
                     reads=lgk + [("gate", 0)], writes=[("gate", 0)])
                S.op("act", lambda e: e.activation(out=r_, in_=r_, func=AF.Sigmoid, scale=1.5957691216057308),
                     reads=[("gate", 0)], writes=[("gate", 0)])
                S.op("dve", lambda e, lgc=lgc: e.tensor_tensor(out=r_, in0=r_, in1=lgc, op=ALU.mult),
                     reads=lgk + [("gate", 0)], writes=[("gate", 0)])
                S.op("dve", lambda e, c=c: e.tensor_tensor(out=mix[:, 12 + c, 0:Tg], in0=r_, in1=hd[0], op=ALU.mult),
                     reads=[("gate", 0), ("hd", 0)], writes=[("mix", 12 + c, t) for t in range(ntt)])

        def attention(G, l):
            Tg, seqs, sample = G["T"], G["seqs"], G["sample"]
            astop = (ATT_STOP if ATT_STOP < 10 else 0) if sample else (ATT_STOP - 10 if ATT_STOP >= 10 else 0)
            q4 = bview(SCR, [128, 4, Tg], BF16)
            k4 = bview(SCR + 2 * Tg, [128, 4, Tg], BF16)
            v4 = bview(SCR + 4 * Tg, [128, Tg // 128, 512], BF16)
            o = SCR + 6 * Tg
            kctx = bview(o, [128, 4, 512], BF16)
            vctx = bview(o + 1024, [128, 4, 512], BF16)
            cin = misc[:, 0:1024].bitcast(BF16).rearrange("p (a b) -> p a b", a=4)
            o += 2048
            sc = [bview(o, [128, 1152], F32), bview(o + 1152, [128, 1152], F32)]
            o += 2304
            pb = [bview(o, [128, 1152], BF16), bview(o + 576, [128, 1152], BF16)]
            o += 1152
            pt = [bview(o, [128, 9, 128], BF16), bview(o + 576, [128, 9, 128], BF16)]
            o += 1152
            btl = [misc[:, 1024:1664], misc[:, 1664:2304]]
            mx = bview(o, [128, 8], F32)
            rb = [bview(o + 8, [128, 128], F32), bview(o + 136, [128, 128], F32)]
            kvo = [misc[:, 2304:2816], misc[:, 2816:3328]]
            assert o + 264 <= 22528, o
            it = 0
            for hg in range(2):
                S.barrier()
                b = load_panel(wview(win_d[l], hg * 512, 512), 16, 512)

                def ev_q(mc, tt, bank):
                    copy_op(evac_eng(), q4[:, mc, tt * 512:(tt + 1) * 512], ps[bank][:], [("ps", bank)], [("q4", mc, tt)])
                mm_fm(b, 4, hb, "h", Tg, ev_q)
                if astop == 5:
                    raise _Stop()
                b = load_panel(wview(win_d[l], 1024 + hg * 512, 512), 16, 512)

                def ev_k(mc, tt, bank):
                    copy_op(evac_eng(), k4[:, mc, tt * 512:(tt + 1) * 512], ps[bank][:], [("ps", bank)], [("k4", mc, tt)])
                mm_fm(b, 4, hb, "h", Tg, ev_k)
                if astop == 6:
                    raise _Stop()
                if not sample:
                    def ev_ktm(ti, bank):
                        kq = ti % 2
                        copy_op(evac_eng(), kvo[kq], ps[bank][:], [("ps", bank)], [("kvo", kq)])
                        bi, t0 = ti // 2, (ti % 2) * 128
                        if SKIP_KV_OUT:
                            return
                        finals.append(S.op("sp", lambda e: e.dma_start(
                            out=nk_d[bi, l, t0:t0 + 128, hg * 512:(hg + 1) * 512], in_=kvo[kq]),
                            reads=[("kvo", kq)], dma=True))
                    mm_tm(b, hb, "h", Tg, ev_ktm)
                    if astop == 7:
                        raise _Stop()
                b = load_panel(wview(win_d[l], 2048 + hg * 512, 512), 16, 512)

                def ev_v(ti, bank):
                    if sample:
                        copy_op("act", v4[:, ti, :], ps[bank][:], [("ps", bank)], [("v4", ti)])
                        return
                    kq = ti % 2
                    copy_op("act", kvo[kq], ps[bank][:], [("ps", bank)], [("kvo", kq)])
                    copy_op("dve", v4[:, ti, :], kvo[kq], [("kvo", kq)], [("v4", ti)])
                    bi, t0 = ti // 2, (ti % 2) * 128
                    if SKIP_KV_OUT:
                        return
                    finals.append(S.op("sp", lambda e: e.dma_start(
                        out=nv_d[bi, l, t0:t0 + 128, hg * 512:(hg + 1) * 512], in_=kvo[kq]),
                        reads=[("kvo", kq)], dma=True))
                mm_tm(b, hb, "h", Tg, ev_v)
                if astop == 1:
                    raise _Stop()
                if sample:
                    ckv = ck_d[l].rearrange("(a p) f -> p a f", p=128)[:, :, hg * 512:(hg + 1) * 512]
                    cvw_ = cv_d[l].rearrange("(a p) f -> p a f", p=128)[:, :, hg * 512:(hg + 1) * 512]
                    S.op("pool", lambda e: e.dma_start(out=cin, in_=ckv), writes=["cin"], dma=True)
                    S.op("pool", lambda e: e.dma_start(out=vctx, in_=cvw_), writes=["vctx"], dma=True)
                    for hh in range(4):
                        for a in range(4):
                            S.op("pe", lambda e, hh=hh, a=a: e.transpose(
                                psb[hh % 2][:, a * 128:(a + 1) * 128], cin[:, a, hh * 128:(hh + 1) * 128], ident_b[:]),
                                reads=["cin", "idb"], writes=[("psb", hh % 2)])
                        copy_op(evac_eng(), kctx[:, hh, :], psb[hh % 2][:, 0:512], [("psb", hh % 2)], [("kctx", hh)])
                if astop == 2:
                    raise _Stop()
                def stage_a(hh, so, q0, pi, u, itn):
                    h = hg * 4 + hh
                    if sample:
                        a_row, nw = _pair_win(pi)
                        k0, nwk = a_row * 64, nw * 64
                        nk_all = nwk + 512
                        S.op("sp", lambda e: e.dma_start(
                            out=btl[u][:, 0:nwk], in_=bt_d[l, h, pi, :, 0:nwk]), writes=[("btl", u)], dma=True)
                    else:
                        k0, nwk = so, 256
                        nk_all = 256
                    bW, bC, bM = (0, 1, 2) if u == 0 else (3, 4, 5)
                    segs = [(k0, min(nwk, 512), bW, 0, 0)]
                    if nwk > 512:
                        segs.append((k0 + 512, nwk - 512, bM, 512, 0))
                    kkeys = [("k4", hh, t) for t in range(Tg // 512)]
                    for (ks, kn, bank, off, pc) in segs:
                        S.op("pe", lambda e: e.matmul(
                            ps[bank][:, pc:pc + kn], q4[:, hh, q0:q0 + 128], k4[:, hh, ks:ks + kn],
                            start=True, stop=True),
                            reads=[("q4", hh, q0 // 512)] + kkeys, writes=[("ps", bank)])
                        if sample:
                            S.op("dve", lambda e: e.scalar_tensor_tensor(
                                out=sc[u][:, off:off + kn], in0=ps[bank][:, pc:pc + kn], scalar=ATT_SCALE,
                                in1=btl[u][:, off:off + kn], op0=ALU.mult, op1=ALU.add),
                                reads=[("ps", bank), ("btl", u)], writes=[("sc", u)])
                        else:
                            S.op("dve", lambda e: e.tensor_scalar(
                                sc[u][:, off:off + kn], ps[bank][:, pc:pc + kn], ATT_SCALE, None, ALU.mult),
                                reads=[("ps", bank)], writes=[("sc", u)])
                    if sample:
                        S.op("pe", lambda e: e.matmul(
                            ps[bC][:], q4[:, hh, q0:q0 + 128], kctx[:, hh, :], start=True, stop=True),
                            reads=[("q4", hh, q0 // 512), ("kctx", hh)], writes=[("ps", bC)])
                        S.op("act", lambda e: e.activation(
                            out=sc[u][:, nwk:nwk + 512], in_=ps[bC][:], func=AF.Copy, scale=ATT_SCALE),
                            reads=[("ps", bC)], writes=[("sc", u)])
                    mcol = itn % 8
                    S.op("dve", lambda e: e.tensor_reduce(
                        out=mx[:, mcol:mcol + 1], in_=sc[u][:, 0:nk_all], axis=mybir.AxisListType.X, op=ALU.max,
                        negate=True),
                        reads=[("sc", u)], writes=[("mx", mcol)])
                    S.op("act", lambda e: e.activation(
                        out=pb[u][:, 0:nk_all], in_=sc[u][:, 0:nk_all], func=AF.Exp, bias=mx[:, mcol:mcol + 1]),
                        reads=[("sc", u), ("mx", mcol)], writes=[("pb", u)])
                    return (hh, h, so, q0, u, k0, nwk, nk_all, bM)

                def stage_b(ctx):
                    hh, h, so, q0, u, k0, nwk, nk_all, bM = ctx
                    nkt = nk_all // 128
                    for g0 in range(0, nkt, 4):
                        gn = min(4, nkt - g0)
                        pbk = (g0 // 4) % 2
                        for j in range(gn):
                            S.op("pe", lambda e: e.transpose(
                                psb[pbk][:, j * 128:(j + 1) * 128], pb[u][:, (g0 + j) * 128:(g0 + j + 1) * 128],
                                ident_b[:]),
                                reads=[("pb", u), "idb"], writes=[("psb", pbk)])
                        copy_op(evac_eng(), pt[u][:, g0:g0 + gn, :],
                                psb[pbk][:, 0:gn * 128].rearrange("p (a b) -> p a b", a=gn),
                                [("psb", pbk)], [("pt", u, g0 // 4)])
                    ptk = [("pt", u, g) for g in range((nkt + 3) // 4)]
                    for j in range(nkt):
                        S.op("pe", lambda e: e.matmul(
                            ps[bM][:, 128:256], ones_1[:], pt[u][:, j, :], start=(j == 0), stop=(j == nkt - 1)),
                            reads=ptk + ["ones_1"], writes=[("ps", bM)])
                    for j in range(nkt):
                        if sample:
                            if j < nwk // 128:
                                vsrc = v4[:, k0 // 128 + j, hh * 128:(hh + 1) * 128]
                                vk = ("v4", k0 // 128 + j)
                            else:
                                vsrc = vctx[:, j - nwk // 128, hh * 128:(hh + 1) * 128]
                                vk = "vctx"
                        else:
                            vsrc = v4[:, so // 128 + j, hh * 128:(hh + 1) * 128]
                            vk = ("v4", so // 128 + j)
                        S.op("pe", lambda e: e.matmul(
                            ps[bM][:, 256:384], vsrc, pt[u][:, j, :], start=(j == 0), stop=(j == nkt - 1)),
                            reads=ptk + [vk], writes=[("ps", bM)])
                    S.op("dve", lambda e: e.reciprocal(rb[u], ps[bM][:, 128:256]),
                         reads=[("ps", bM)], writes=[("rb", u)])
                    S.op("dve", lambda e: e.tensor_tensor(
                        out=mix[:, h, q0:q0 + 128], in0=ps[bM][:, 256:384], in1=rb[u], op=ALU.mult),
                        reads=[("ps", bM), ("rb", u)], writes=[("mixq", h, q0)])

                tl = []
                for hh in range(4):
                    if sample:
                        tl += [(hh, 0, i * 128, i) for i in range(8)]
                    else:
                        tl += [(hh, so, so + t0, None) for (so, sl) in seqs for t0 in range(0, sl, 128)]
                prev = None
                for (hh, so, q0, pi) in tl:
                    ctx = stage_a(hh, so, q0, pi, it % 2, it)
                    it += 1
                    if prev is not None:
                        stage_b(prev)
                    prev = ctx
                stage_b(prev)
            for h in range(8):
                for tt in range(Tg // 512):
                    S.op("dve", lambda e, h=h, tt=tt: e.tensor_copy(out=mx[:, 0:1], in_=mx[:, 0:1]),
                         reads=[("mixq", h, q0) for q0 in range(tt * 512, (tt + 1) * 512, 128)] + [("mx", 0)],
                         writes=[("mix", h, tt), ("mx", 0)])

        def w_out(G, l):
            Tg = G["T"]
            for p in range(4):
                b = load_panel(wview(wout_d[l], p * 512, 512), 16, 512)

                def ev(mc, tt, bank, p=p):
                    resid(G, l, p * 4 + mc, tt, bank, 32)
                mm_fm(b, 4, mix, "mix", Tg, ev)

        def ffn(G, l):
            Tg, seqs = G["T"], G["seqs"]
            ns, Ls = len(seqs), seqs[0][1]
            Lp = Ls + 2
            upd = [sbx_up[0], sbx_up[1]]
            cc = [sbx_cc[0], sbx_cc[1]]
            for uq in range(2):
                S.op("dve", lambda e, uq=uq: e.memset(upd[uq], 0.0), writes=[("upd", uq)])
            jn = 0
            for p in range(22):
                if G["sample"]:
                    mod_step()
                b = load_panel(wview(fup_d[l], p * 512, 512), 16, 512)
                for mc in range(4):
                    j = p * 4 + mc
                    ja = j % 44
                    uq = jn % 2
                    jn += 1
                    owc = PV_OFF["fcw"] + (l * 88 + j) * 3
                    obc = PV_OFF["fcb"] + l * 88 + j
                    banks = []
                    for tt in range(Tg // 512):
                        bank = next_bank()
                        banks.append(bank)
                        for kc in range(16):
                            S.op("pe", lambda e, kc=kc, mc=mc, tt=tt, bank=bank, b=b: e.matmul(
                                ps[bank][:], wbuf[b][:, kc, mc * 128:(mc + 1) * 128],
                                hb[:, kc, tt * 512:(tt + 1) * 512], start=(kc == 0), stop=(kc == 15)),
                                reads=[("wb", b), ("h", kc, tt)], writes=[("ps", bank)])
                        for si, (so, sl) in enumerate(seqs):
                            lo, hi = max(so, tt * 512), min(so + sl, (tt + 1) * 512)
                            if lo >= hi:
                                continue
                            S.op("act", lambda e, uq=uq, si=si, so=so, lo=lo, hi=hi, tt=tt, bank=bank: e.activation(
                                out=upd[uq][:, si * Lp + 1 + lo - so: si * Lp + 1 + hi - so],
                                in_=ps[bank][:, lo - tt * 512: hi - tt * 512], func=AF.Copy),
                                reads=[("ps", bank), ("upd", uq)], writes=[("updw", uq, tt)])
                        S.op("act", lambda e, uq=uq, tt=tt, bank=bank, owc=owc, obc=obc: e.activation(
                            out=cc[uq][:, tt * 512:(tt + 1) * 512], in_=ps[bank][:], func=AF.Identity,
                            scale=pvt[:, owc + 1:owc + 2], bias=pvt[:, obc:obc + 1]),
                            reads=[("ps", bank), "pv"], writes=[("cc", uq, tt)])
                    ntt = Tg // 512
                    updk = [("updw", uq, t) for t in range(ntt)]
                    cck = [("cc", uq, t) for t in range(ntt)]
                    for si, (so, sl) in enumerate(seqs):
                        for k in (0, 2):
                            S.op("dve", lambda e, uq=uq, si=si, so=so, sl=sl, k=k, owc=owc: e.scalar_tensor_tensor(
                                out=cc[uq][:, so:so + sl], in0=upd[uq][:, si * Lp + k: si * Lp + k + sl],
                                scalar=pvt[:, owc + k:owc + k + 1], in1=cc[uq][:, so:so + sl],
                                op0=ALU.mult, op1=ALU.add),
                                reads=updk + cck + ["pv"], writes=[("ccf", uq)])
                    if j < 44:
                        S.op("act", lambda e, uq=uq, ja=ja: e.activation(out=act[:, ja, 0:Tg], in_=cc[uq][:, 0:Tg],
                                                                        func=AF.Silu),
                             reads=[("ccf", uq)] + cck, writes=[("act", ja)])
                    else:
                        S.op("dve", lambda e, uq=uq, ja=ja: e.tensor_tensor(
                            out=act[:, ja, 0:Tg], in0=act[:, ja, 0:Tg], in1=cc[uq][:, 0:Tg], op=ALU.mult),
                            reads=[("ccf", uq), ("act", ja)] + cck, writes=[("act", ja)])
            S.barrier()
            ntt = Tg // 512
            for mp in range(8):
                base = 0
                for ks, (k0, nk) in enumerate(((0, 16), (16, 16), (32, 12))):
                    if G["sample"] and ks == 0 and mp < 2:
                        mod_step()
                    b = load_panel(wview(fdn_d[l], mp * 256, 256, k0, nk), nk, 256)
                    for oc_ in range(2):
                        for tt in range(ntt):
                            bank = base + oc_ * 2 + tt
                            for kc in range(nk):
                                S.op("pe", lambda e, kc=kc, k0=k0, oc_=oc_, tt=tt, bank=bank, b=b, ks=ks, nk=nk: e.matmul(
                                    ps[bank][:], wbuf[b][:, kc, oc_ * 128:(oc_ + 1) * 128],
                                    act[:, k0 + kc, tt * 512:(tt + 1) * 512],
                                    start=(ks == 0 and kc == 0), stop=(ks == 2 and kc == nk - 1)),
                                    reads=[("wb", b), ("act", k0 + kc)], writes=[("ps", bank)])
                for oc_ in range(2):
                    for tt in range(ntt):
                        resid(G, l, mp * 2 + oc_, tt, base + oc_ * 2 + tt, 80)

        def final_norm(G):
            ystage = [wbuf[i][:].rearrange("p a b -> p (a b)").bitcast(F32).rearrange("p (a f) -> p a f", a=4)
                      for i in range(2)]
            Tg, xoff, ydst = G["T"], G["xoff"], G["ydst"]
            for tt in range(Tg // 512):
                q = tt % 2
                gt = (xoff + tt * 512) // 512
                S.op("sp", lambda e: e.dma_start(
                    out=xt16[q], in_=X_d[:, :, xoff + tt * 512: xoff + (tt + 1) * 512]),
                    reads=[("X", gt, c) for c in range(16)], writes=[("xt16", q)], dma=True)
                for kc in range(16):
                    tb = cnt["tb"] % 4
                    cnt["tb"] += 1
                    S.op("act", lambda e: e.activation(out=tmpb[tb], in_=xt16[q][:, kc, :], func=AF.Square),
                         reads=[("xt16", q)], writes=[("tmpb", tb)])
                    S.op("pe", lambda e: e.matmul(ps[4][:], ones_d[:], tmpb[tb], start=(kc == 0), stop=(kc == 15)),
                         reads=[("tmpb", tb), "ones_d"], writes=[("ps", 4)])
                S.op("act", lambda e: e.activation(out=rstd[q], in_=ps[4][:], func=AF.Sqrt, bias=cst[:, 0:1]),
                     reads=[("ps", 4), "cst"], writes=[("rstd", q)])
                S.op("dve", lambda e: e.reciprocal(rstd[q], rstd[q]), reads=[("rstd", q)], writes=[("rstd", q)])
                for half in range(2):
                    for kk in range(8):
                        kc = half * 8 + kk
                        tf = cnt["tf"] % 4
                        cnt["tf"] += 1
                        og = PV_OFF["fing"] + kc
                        S.op("dve", lambda e: e.scalar_tensor_tensor(
                            out=tmpf[tf], in0=xt16[q][:, kc, :], scalar=pvt[:, og:og + 1], in1=rstd[q],
                            op0=ALU.mult, op1=ALU.mult),
                            reads=[("xt16", q), ("rstd", q), "pv"], writes=[("tmpf", tf)])
                        bank = kc % 4
                        for j in range(4):
                            S.op("pe", lambda e: e.transpose(
                                ps[bank][:, j * 128:(j + 1) * 128], tmpf[tf][:, j * 128:(j + 1) * 128], ident_f[:]),
                                reads=[("tmpf", tf), "idf"], writes=[("ps", bank)])
                        copy_op("act" if kc % 2 else "dve", ystage[half][:, :, kk * 128:(kk + 1) * 128],
                                ps[bank][:].rearrange("p (a b) -> p a b", a=4), [("ps", bank)], [("wb", half)])
                    finals.append(S.op("sp", lambda e: e.dma_start(
                        out=ydst[tt * 512:(tt + 1) * 512, half * 1024:(half + 1) * 1024].rearrange(
                            "(a p) f -> p a f", p=128), in_=ystage[half]),
                        reads=[("wb", half)], dma=True))

        GS = dict(T=TS, xoff=0, ci=1, seqs=[(0, TS)], sample=True, h0=True, nl=False, ydst=ys_d)
        GP = dict(T=TP, xoff=TS, ci=0, seqs=[(0, 256), (256, 256)], sample=False, h0=False, nl=True, nl_b=0,
                  ydst=yp_d)
        def chk(n):
            if STAGE == n:
                raise _Stop()

        _chk0 = chk
        try:
            chk(1)
            for l in range(NLAYERS):
                for gi_, G in enumerate((GS, GP)):
                    _chk = chk
                    chk = (lambda n, gi_=gi_, _c=_chk0: _c(n + 10 * gi_))
                    S.barrier()
                    norm(G, l, 0)
                    chk(2)
                    S.barrier()
                    conv_module(G, l)
                    chk(3)
                    S.barrier()
                    lru(G, l)
                    chk(4)
                    attention(G, l)
                    chk(5)
                    S.barrier()
                    w_out(G, l)
                    chk(6)
                    S.barrier()
                    norm(G, l, 1)
                    S.barrier()
                    ffn(G, l)
                    chk(7)
                while mod_state["l"] == l + 1 and mod_state["l"] < NLAYERS:
                    mod_step()
            _chk0(20)
            S.barrier()
            final_norm(GS)
            final_norm(GP)
            _chk0(21)
            S.barrier()
            S.op("pe", lambda e: e.transpose(ps[0][0:64, 0:128], nl_t[:, 0:64], ident_f[:]),
                 reads=["nl_t", "idf"], writes=[("ps", 0)])
            S.op("dve", lambda e: e.tensor_copy(out=tmpf[0][0:64, 0:128], in_=ps[0][0:64, 0:128]),
                 reads=[("ps", 0)], writes=[("tmpf", 0)])
            finals.append(S.op("sp", lambda e: e.dma_start(out=nl_d, in_=tmpf[0][0:64, 0:128]),
                               reads=[("tmpf", 0)], dma=True))
        except _Stop:
            pass
        for _e in S.ENGS:
            if S.ops[_e]:
                S.ops[_e][-1].signal = True
                finals.append(S.ops[_e][-1])
        finals.extend(d for d in S.dlast if d is not None)
        S.emit(final_waits=finals)
    return nc


_NC = None


def kernel(x_prompt, x_sample, cache_k, cache_v, state_lru, c, c_ctx, ln1_g, w_mod, b_mod, w_in, na_bias,
           cv_w, cv_b, cv_ln_g, cv_ln_b, lru_conv_w, lru_conv_b, lru_wa, lru_ba, lru_wi, lru_bi, lru_lam,
           w_out, ln2_g, ffn_up, ffn_conv_w, ffn_conv_b, ffn_down, final_g):
    global _NC
    f = lambda a: np.ascontiguousarray(np.asarray(a, np.float32))
    x_prompt, x_sample, cache_k, cache_v = f(x_prompt), f(x_sample), f(cache_k), f(cache_v)
    if _NC is None:
        _NC = build_nc()
    nc = _NC
    bt = _build_bias(na_bias)
    lwa, lwi = np.asarray(lru_wa, np.float32), np.asarray(lru_wi, np.float32)
    lw = np.zeros((L, 128, 16, 128), np.float32)
    for d in range(2):
        for gi, w in enumerate((lwa, lwi)):
            for cch in range(4):
                for hb_ in range(2):
                    blk = w[:, d, cch * 2 + hb_]
                    lw[:, hb_ * 64:(hb_ + 1) * 64, (d * 2 + gi) * 4 + cch, hb_ * 64:(hb_ + 1) * 64] = blk
    ident = np.eye(128, dtype=np.float32)
    NLc = NLAYERS
    shared = dict(wmod=f(w_mod)[:NLc], win=f(w_in)[:NLc], wout=f(w_out)[:NLc], fup=f(ffn_up)[:NLc],
                  fdn=f(ffn_down)[:NLc], lw=lw[:NLc], bt=bt[:NLc], ident=ident)

    def pv_for(s):
        parts = {
            "ln1g": _fm(ln1_g, 16), "ln2g": _fm(ln2_g, 16), "fing": _fm(final_g, 16), "bmod": _fm(b_mod, 96),
            "cond": np.moveaxis(_fm(np.stack([np.asarray(c_ctx), np.asarray(c)[s]]), 16), 1, 2),
            "cvw": np.moveaxis(_fm(cv_w, 4), 2, 3),
            "cvb": _fm(cv_b, 4), "cvg": _fm(cv_ln_g, 4), "cvlb": _fm(cv_ln_b, 4),
            "lcw": np.moveaxis(_fm(lru_conv_w, 4), 2, 3), "lcb": _fm(lru_conv_b, 4),
            "lba": _fm(lru_ba, 4), "lbi": _fm(lru_bi, 4), "lam": _fm(lru_lam, 4),
            "h0": _fm(np.asarray(state_lru)[s], 4),
            "fcw": np.moveaxis(_fm(ffn_conv_w, 88), 2, 3), "fcb": _fm(ffn_conv_b, 88),
        }
        cols = []
        for n, cnt_ in PV_ITEMS:
            a = np.ascontiguousarray(parts[n], dtype=np.float32).reshape(128, -1)
            assert a.shape[1] == cnt_, (n, a.shape, cnt_)
            cols.append(a)
        return np.ascontiguousarray(np.concatenate(cols, axis=1))

    pvs = [pv_for(0), pv_for(1)]
    in_maps = []
    for core in range(8):
        s = core % 2
        m = dict(shared)
        m["xs"] = x_sample[s]
        m["xp"] = x_prompt[2 * core:2 * core + 2].reshape(TP, D)
        m["ck"] = cache_k[s].reshape(L, 512, 1024)[:NLc]
        m["cvv"] = cache_v[s].reshape(L, 512, 1024)[:NLc]
        m["pv"] = pvs[s]
        in_maps.append(m)
    res = run_bass_kernel_spmd(nc, in_maps[:NCORES], core_ids=list(range(NCORES)))
    R = list(res.results)
    while len(R) < 8:
        R.append(R[len(R) % NCORES])
    y_prompt = np.concatenate([R[i]["yp"].reshape(2, 256, D) for i in range(8)], axis=0)
    y_sample = np.stack([R[0]["ys"], R[1]["ys"]], axis=0)
    new_k = np.concatenate([R[i]["nk"].reshape(2, L, 256, NH, 128) for i in range(8)], axis=0)
    new_v = np.concatenate([R[i]["nv"].reshape(2, L, 256, NH, 128) for i in range(8)], axis=0)
    new_lru = np.concatenate([R[i]["nl"].reshape(2, L, 2, 512) for i in range(8)], axis=0)
    return (y_prompt.astype(np.float32), y_sample.astype(np.float32), new_k.astype(np.float32),
            new_v.astype(np.float32), new_lru.astype(np.float32))
```

```python
import numpy as np
from contextlib import ExitStack
import concourse.bass as bass
import concourse.mybir as mybir
from concourse.bass_utils import run_bass_kernel_spmd

F32 = mybir.dt.float32
BF16 = mybir.dt.bfloat16
AF = mybir.ActivationFunctionType
ALU = mybir.AluOpType

D = 2048
L = 4
NH = 8
DFF = 5632
TS = 1024
TP = 512
TT = TS + TP
NEG = -1e30
NLAYERS = 4
STAGE = 99
ATT_STOP = 0
SKIP_KV_OUT = 0
NCORES = 8


class _Stop(Exception):
    pass
EPS = 1e-6
ATT_SCALE = 128 ** -0.5


class _Rec:
    def __init__(self):
        self.call = None

    def __getattr__(self, name):
        def f(*a, **k):
            self.call = (name, a, k)
            return self
        return f


class _Op:
    __slots__ = ("eng", "fn", "deps", "signal", "dma", "dsem", "dval", "cnt")

    def __init__(self, eng, fn, dma):
        self.eng = eng
        rec = _Rec()
        fn(rec)
        self.fn = rec.call
        self.deps = []
        self.signal = False
        self.dma = dma
        self.dsem = None
        self.dval = 0
        self.cnt = 0


class Sched:
    ENGS = ("pe", "act", "dve", "pool", "sp")

    def __init__(self, nc, stack, n_dma_sems=16):
        self.nc = nc
        self.ops = {e: [] for e in self.ENGS}
        self.last_w = {}
        self.readers = {}
        self.sems = {e: stack.enter_context(nc.semaphore("s_" + e)) for e in self.ENGS}
        self.dsems = [stack.enter_context(nc.semaphore("d_%d" % i)) for i in range(n_dma_sems)]
        self.dcount = [0] * n_dma_sems
        self.dlast = [None] * n_dma_sems
        self.dnext = 0
        self.pending_bar = {e: None for e in self.ENGS}

    def barrier(self):
        deps = []
        for e in self.ENGS:
            if self.ops[e]:
                deps.append(self.ops[e][-1])
        for d in self.dlast:
            if d is not None:
                deps.append(d)
        for e in self.ENGS:
            old = self.pending_bar[e]
            self.pending_bar[e] = deps if old is None else deps

    def op(self, eng, fn, reads=(), writes=(), dma=False, nobar=False):
        o = _Op(eng, fn, dma)
        deps = []
        if not nobar and self.pending_bar[eng] is not None:
            deps.extend(self.pending_bar[eng])
            self.pending_bar[eng] = None
        for k in reads:
            w = self.last_w.get(k)
            if w is not None:
                deps.append(w)
        for k in writes:
            w = self.last_w.get(k)
            if w is not None:
                deps.append(w)
            rs = self.readers.get(k)
            if rs:
                deps.extend(rs)
        if dma:
            s = self.dnext
            self.dnext = (self.dnext + 1) % len(self.dsems)
            if self.dlast[s] is not None:
                deps.append(self.dlast[s])
            self.dcount[s] += 16
            o.dsem = s
            o.dval = self.dcount[s]
            self.dlast[s] = o
        seen = set()
        for d in deps:
            if d is o or id(d) in seen:
                continue
            seen.add(id(d))
            if d.eng == "pe" and eng == "pe" and not d.dma and not dma:
                continue
            o.deps.append(d)
            d.signal = True
        for k in reads:
            lst = self.readers.setdefault(k, [])
            if not dma:
                lst[:] = [r for r in lst if r.dma or r.eng != eng]
            lst.append(o)
        for k in writes:
            self.last_w[k] = o
            self.readers[k] = []
        self.ops[eng].append(o)
        return o

    def emit(self, final_waits=()):
        nc = self.nc
        for e in self.ENGS:
            c = 0
            for o in self.ops[e]:
                if o.signal and not o.dma:
                    c += 1
                    o.cnt = c
        final_waits = list(final_waits)

        def run(e, engh):
            waited = {}

            def wait(d):
                if d.dma:
                    key, sem, val = ("d", d.dsem), self.dsems[d.dsem], d.dval
                else:
                    key, sem, val = ("e", d.eng), self.sems[d.eng], d.cnt
                if waited.get(key, 0) >= val:
                    return
                waited[key] = val
                engh.wait_ge(sem, val)

            for o in self.ops[e]:
                for d in o.deps:
                    wait(d)
                name, a, k = o.fn
                ins = getattr(engh, name)(*a, **k)
                if o.dma:
                    ins.then_inc(self.dsems[o.dsem], 16)
                elif o.signal:
                    ins.then_inc(self.sems[e], 1)
            if e == "sp":
                for d in final_waits:
                    wait(d)

        with nc.Block() as block:
            @block.tensor
            def _(t):
                run("pe", t)

            @block.scalar
            def _(t):
                run("act", t)

            @block.vector
            def _(t):
                run("dve", t)

            @block.gpsimd
            def _(t):
                run("pool", t)

            @block.sync
            def _(t):
                run("sp", t)


PV_ITEMS = [
    ("ln1g", L * 16), ("ln2g", L * 16), ("fing", 16), ("bmod", L * 96), ("cond", 32),
    ("cvw", L * 4 * 31), ("cvb", L * 4), ("cvg", L * 4), ("cvlb", L * 4),
    ("lcw", L * 4 * 4), ("lcb", L * 4), ("lba", L * 8), ("lbi", L * 8), ("lam", L * 8), ("h0", L * 8),
    ("fcw", L * 88 * 3), ("fcb", L * 88),
]
PV_OFF = {}
_o = 0
for _n, _c in PV_ITEMS:
    PV_OFF[_n] = _o
    _o += _c
NV = _o


def _fm(v, nch):
    v = np.asarray(v, np.float32)
    lead = v.shape[:-1]
    return np.moveaxis(v.reshape(*lead, nch, 128), -1, 0)


def _pair_win(i):
    r0, r1 = 2 * i, 2 * i + 1
    s0 = min(max(r0 - 4, 0), 8)
    s1 = min(max(r1 - 4, 0), 8)
    a = s0 & ~1
    e = (s1 + 8 + 1) & ~1
    return a, e - a


def _build_bias(na_bias):
    na = np.asarray(na_bias, np.float32)
    cols = np.arange(64)
    cs = np.clip(cols - 8, 0, 48)
    inwin = (cols[None, :] >= cs[:, None]) & (cols[None, :] < cs[:, None] + 16)
    co = np.clip(cols[None, :] - cols[:, None], -15, 15) + 15
    out = np.full((L, NH, 8, 128, 640), NEG, np.float32)
    for i in range(8):
        a, nw = _pair_win(i)
        for half in range(2):
            r = 2 * i + half
            st = min(max(r - 4, 0), 8)
            for j in range(nw):
                kr = a + j
                if not (st <= kr < st + 8):
                    continue
                blk = na[:, :, kr - r + 7, :][:, :, co]
                blk = np.where(inwin[None, None], blk, np.float32(NEG))
                out[:, :, i, half * 64:(half + 1) * 64, j * 64:(j + 1) * 64] = blk
    return out


def build_nc():
    nc = bass.Bass("TRN2", target_bir_lowering=False)

    def din(name, shape):
        return nc.dram_tensor(name, list(shape), F32, kind="ExternalInput").ap()

    def dout(name, shape):
        return nc.dram_tensor(name, list(shape), F32, kind="ExternalOutput").ap()

    xs_d = din("xs", [TS, D])
    xp_d = din("xp", [TP, D])
    ck_d = din("ck", [NLAYERS, 512, 1024])
    cv_d = din("cvv", [NLAYERS, 512, 1024])
    pv_d = din("pv", [128, NV])
    wmod_d = din("wmod", [NLAYERS, D, 6 * D])
    win_d = din("win", [NLAYERS, D, 5120])
    wout_d = din("wout", [NLAYERS, D, D])
    fup_d = din("fup", [NLAYERS, D, 2 * DFF])
    fdn_d = din("fdn", [NLAYERS, DFF, D])
    lw_d = din("lw", [NLAYERS, 128, 16, 128])
    bt_d = din("bt", [NLAYERS, NH, 8, 128, 640])
    id_d = din("ident", [128, 128])
    ys_d = dout("ys", [TS, D])
    yp_d = dout("yp", [TP, D])
    nk_d = dout("nk", [2, L, 256, 1024])
    nv_d = dout("nv", [2, L, 256, 1024])
    nl_d = dout("nl", [64, 128])
    X_d = nc.dram_tensor("Xscr", [128, 16, TT], F32).ap()

    st = ExitStack()
    with st:
        S = Sched(nc, st)

        def sb(name, shape, dt):
            return st.enter_context(nc.sbuf_tensor(name, list(shape), dt))

        pvt = sb("pvt", [128, NV], F32)
        ident_f = sb("ident_f", [128, 128], F32)
        ident_b = sb("ident_b", [128, 128], BF16)
        ones_d = sb("ones_d", [128, 128], BF16)
        ones_c = sb("ones_c", [128, 128], BF16)
        ones_1 = sb("ones_1", [128, 128], BF16)
        scond = sb("scond", [128, 32], BF16)
        modt = sb("modt", [128, L, 96, 2], F32)
        amod = sb("amod", [128, L, 2, 2, 16], F32)
        cneg = sb("cneg", [128, L * 8], F32)
        nl_t = sb("nl_t", [128, 64], F32)
        cst = sb("cst", [128, 2], F32)
        hb = sb("hb", [128, 16, TS], BF16)
        wbuf = [sb("wb%d" % i, [128, 16, 512], BF16) for i in range(3)]
        big = sb("big", [128, 22528], F32)
        lwt = sb("lwt", [128, 16, 128], BF16)
        misc = sb("misc", [128, 4224], F32)
        rstd = [misc[:, i * 512:(i + 1) * 512] for i in range(2)]
        tmpf = [misc[:, 1024 + i * 512:1024 + (i + 1) * 512] for i in range(4)]
        tmpb = [misc[:, 3072 + i * 256:3072 + (i + 1) * 256].bitcast(BF16) for i in range(4)]
        sbx_up = [misc[:, i * 1028:(i + 1) * 1028] for i in range(2)]
        sbx_cc = [misc[:, 2056 + i * 1024:2056 + (i + 1) * 1024] for i in range(2)]
        xio = [misc[:, i * 512:(i + 1) * 512] for i in range(4)]
        ps = [st.enter_context(nc.psum_tensor("ps%d" % i, [128, 512], F32)) for i in range(6)]
        psbT = [st.enter_context(nc.psum_tensor("psbT%d" % i, [128, 1024], BF16)) for i in range(2)]
        psb = [psbT[0][:, 0:512], psbT[1][:, 0:512]]

        def bview(off_f32, shape, dt):
            n = int(np.prod(shape[1:]))
            if dt == BF16:
                v = big[:, off_f32:off_f32 + (n + 1) // 2].bitcast(BF16)
                v = v[:, 0:n]
            else:
                v = big[:, off_f32:off_f32 + n]
            if len(shape) == 3:
                v = v.rearrange("p (a b) -> p a b", a=shape[1])
            elif len(shape) == 4:
                v = v.rearrange("p (a b c) -> p a b c", a=shape[1], b=shape[2])
            return v

        mix = bview(0, [128, 16, TS], BF16)
        act = bview(0, [128, 44, TS], BF16)
        xt16 = [bview(6144, [128, 16, 512], F32), bview(6144 + 8192, [128, 16, 512], F32)]
        SCR = 8192

        cnt = {"w": 0, "bank": 0, "tf": 0, "tb": 0, "xio": 0, "ev": 0}

        def pvc(name, idx):
            o = PV_OFF[name] + idx
            return pvt[:, o:o + 1]

        def load_panel(src_ap, nk, ncol):
            b = cnt["w"] % 3
            cnt["w"] += 1
            S.op("pool", lambda e: e.dma_start(out=wbuf[b][:, 0:nk, 0:ncol], in_=src_ap),
                 writes=[("wb", b)], dma=True, nobar=True)
            return b

        def wview(w2d, c0, ncol, k0=0, nk=16):
            return w2d.rearrange("(kc p) n -> p kc n", p=128)[:, k0:k0 + nk, c0:c0 + ncol]

        def next_bank(lo=0, n=4):
            b = lo + cnt["bank"] % n
            cnt["bank"] += 1
            return b

        def evac_eng():
            cnt["ev"] += 1
            return "act" if cnt["ev"] % 2 else "dve"

        def copy_op(eng, out, in_, reads, writes):
            if eng == "act":
                return S.op("act", lambda e: e.activation(out=out, in_=in_, func=AF.Copy), reads=reads, writes=writes)
            return S.op(eng, lambda e: e.tensor_copy(out=out, in_=in_), reads=reads, writes=writes)

        def mm_fm(b, nmc, src, srckey, Tg, evac, nk=16):
            for mc in range(nmc):
                for tt in range(Tg // 512):
                    bank = next_bank()
                    for kc in range(nk):
                        S.op("pe", lambda e, kc=kc, mc=mc, tt=tt, bank=bank: e.matmul(
                            ps[bank][:], wbuf[b][:, kc, mc * 128:(mc + 1) * 128],
                            src[:, kc, tt * 512:(tt + 1) * 512], start=(kc == 0), stop=(kc == nk - 1)),
                            reads=[("wb", b), (srckey, kc, tt)], writes=[("ps", bank)])
                    evac(mc, tt, bank)

        def mm_tm(b, src, srckey, Tg, evac):
            for ti in range(Tg // 128):
                bank = next_bank()
                for kc in range(16):
                    S.op("pe", lambda e, kc=kc, ti=ti, bank=bank: e.matmul(
                        ps[bank][:], src[:, kc, ti * 128:(ti + 1) * 128], wbuf[b][:, kc, :],
                        start=(kc == 0), stop=(kc == 15)),
                        reads=[("wb", b), (srckey, kc, ti // 4)], writes=[("ps", bank)])
                evac(ti, bank)

        S.op("sp", lambda e: e.dma_start(out=pvt[:], in_=pv_d), writes=["pv"], dma=True)
        S.op("sp", lambda e: e.dma_start(out=ident_f[:], in_=id_d), writes=["idf"], dma=True)
        S.op("pool", lambda e: e.dma_start(out=ident_b[:], in_=id_d), writes=["idb"], dma=True)
        S.op("dve", lambda e: e.memset(ones_d[:], 1.0 / D), writes=["ones_d"])
        S.op("dve", lambda e: e.memset(ones_c[:], 1.0 / 512), writes=["ones_c"])
        S.op("dve", lambda e: e.memset(ones_1[:], 1.0), writes=["ones_1"])
        S.op("dve", lambda e: e.memset(nl_t[:], 0.0), writes=["nl_t"])
        S.op("dve", lambda e: e.memset(cst[:, 0:1], EPS), writes=["cst"])
        S.op("dve", lambda e: e.memset(cst[:, 1:2], 1.0), writes=["cst"])
        oc = PV_OFF["cond"]
        S.op("act", lambda e: e.activation(out=scond[:], in_=pvt[:, oc:oc + 32], func=AF.Silu),
             reads=["pv"], writes=["scond"])
        ol = PV_OFF["lam"]
        S.op("act", lambda e: e.activation(out=cneg[:], in_=pvt[:, ol:ol + 32], func=AF.Exp, scale=-1.0),
             reads=["pv"], writes=["cneg"])
        S.op("act", lambda e: e.activation(out=cneg[:], in_=cneg[:], func=AF.Ln, bias=cst[:, 1:2]),
             reads=["cneg", "cst"], writes=["cneg"])
        S.op("dve", lambda e: e.tensor_scalar(cneg[:], cneg[:], -8.0, None, ALU.mult),
             reads=["cneg"], writes=["cneg"])

        def mod_finish(l):
            for ci in range(2):
                for wn, (sc_o, g_name) in enumerate(((16, "ln1g"), (64, "ln2g"))):
                    og = PV_OFF[g_name] + l * 16
                    S.op("dve", lambda e: e.scalar_tensor_tensor(
                        out=amod[:, l, ci, wn, :], in0=modt[:, l, sc_o:sc_o + 16, ci], scalar=1.0,
                        in1=pvt[:, og:og + 16], op0=ALU.add, op1=ALU.mult),
                        reads=[("modt", l), "pv"], writes=[("amod", l)])

        mod_state = {"l": 1, "p": 0}

        def mod_step(maxl=99):
            l2, p = mod_state["l"], mod_state["p"]
            if l2 >= NLAYERS or l2 > maxl:
                return
            b = load_panel(wview(wmod_d[l2], p * 512, 512), 16, 512)
            for m in range(4):
                for kc in range(16):
                    S.op("pe", lambda e: e.matmul(
                        ps[5][:, 384 + 2 * m:386 + 2 * m], wbuf[b][:, kc, m * 128:(m + 1) * 128],
                        scond[:, 2 * kc:2 * kc + 2], start=(kc == 0), stop=(kc == 15)),
                        reads=[("wb", b), "scond"], writes=[("ps", 5)])
            ob = PV_OFF["bmod"] + l2 * 96 + p * 4
            for ci in range(2):
                S.op("dve", lambda e: e.tensor_tensor(
                    out=modt[:, l2, p * 4:(p + 1) * 4, ci], in0=ps[5][:, 384 + ci:392:2], in1=pvt[:, ob:ob + 4],
                    op=ALU.add),
                    reads=[("ps", 5), "pv"], writes=[("modt", l2)])
            mod_state["p"] += 1
            if mod_state["p"] == 24:
                mod_finish(l2)
                mod_state["l"], mod_state["p"] = l2 + 1, 0

        for l in range(1):
            for p in range(24):
                b = load_panel(wview(wmod_d[l], p * 512, 512), 16, 512)
                for m in range(4):
                    c0 = (p * 4 + m) * 2
                    for kc in range(16):
                        S.op("pe", lambda e, b=b, m=m, kc=kc, c0=c0: e.matmul(
                            ps[5][:, c0:c0 + 2], wbuf[b][:, kc, m * 128:(m + 1) * 128],
                            scond[:, 2 * kc:2 * kc + 2], start=(kc == 0), stop=(kc == 15)),
                            reads=[("wb", b), "scond"], writes=[("ps", 5)])
            ob = PV_OFF["bmod"] + l * 96
            for ci in range(2):
                S.op("dve", lambda e, l=l, ci=ci, ob=ob: e.tensor_tensor(
                    out=modt[:, l, :, ci], in0=ps[5][:, ci:192:2], in1=pvt[:, ob:ob + 96], op=ALU.add),
                    reads=[("ps", 5), "pv"], writes=[("modt", l)])
            for ci in range(2):
                for wn, (sc_o, g_name) in enumerate(((16, "ln1g"), (64, "ln2g"))):
                    og = PV_OFF[g_name] + l * 16
                    S.op("dve", lambda e, l=l, ci=ci, wn=wn, sc_o=sc_o, og=og: e.scalar_tensor_tensor(
                        out=amod[:, l, ci, wn, :], in0=modt[:, l, sc_o:sc_o + 16, ci], scalar=1.0,
                        in1=pvt[:, og:og + 16], op0=ALU.add, op1=ALU.mult),
                        reads=[("modt", l), "pv"], writes=[("amod", l)])

        S.barrier()
        tin = [bview(0, [128, D], F32), bview(2048, [128, D], F32)]
        xst = [bview(4096, [128, 16, 128], F32), bview(6144, [128, 16, 128], F32)]
        for ti in range(TT // 128):
            src = xs_d[ti * 128:(ti + 1) * 128, :] if ti < 8 else xp_d[(ti - 8) * 128:(ti - 7) * 128, :]
            q = ti % 2
            S.op("sp", lambda e, q=q, src=src: e.dma_start(out=tin[q], in_=src), writes=[("tin", q)], dma=True)
            for cg in range(4):
                bank = next_bank()
                for j in range(4):
                    kc = cg * 4 + j
                    S.op("pe", lambda e, q=q, kc=kc, j=j, bank=bank: e.transpose(
                        ps[bank][:, j * 128:(j + 1) * 128], tin[q][:, kc * 128:(kc + 1) * 128], ident_f[:]),
                        reads=[("tin", q), "idf"], writes=[("ps", bank)])
                copy_op(evac_eng(), xst[q][:, cg * 4:cg * 4 + 4, :],
                        ps[bank][:].rearrange("p (a b) -> p a b", a=4), [("ps", bank)], [("xst", q, cg)])
            S.op("sp", lambda e, q=q, ti=ti: e.dma_start(out=X_d[:, :, ti * 128:(ti + 1) * 128], in_=xst[q]),
                 reads=[("xst", q, cg) for cg in range(4)], writes=[("X", ti // 4, c) for c in range(16)], dma=True)

        def norm(G, l, wn):
            Tg, xoff, ci = G["T"], G["xoff"], G["ci"]
            for tt in range(Tg // 512):
                q = tt % 2
                gt = (xoff + tt * 512) // 512
                S.op("sp", lambda e: e.dma_start(
                    out=xt16[q], in_=X_d[:, :, xoff + tt * 512: xoff + (tt + 1) * 512]),
                    reads=[("X", gt, c) for c in range(16)], writes=[("xt16", q)], dma=True)
                for kc in range(16):
                    tb = cnt["tb"] % 4
                    cnt["tb"] += 1
                    S.op("act", lambda e: e.activation(out=tmpb[tb], in_=xt16[q][:, kc, :], func=AF.Square),
                         reads=[("xt16", q)], writes=[("tmpb", tb)])
                    S.op("pe", lambda e: e.matmul(ps[4][:], ones_d[:], tmpb[tb], start=(kc == 0), stop=(kc == 15)),
                         reads=[("tmpb", tb), "ones_d"], writes=[("ps", 4)])
                S.op("act", lambda e: e.activation(out=rstd[q], in_=ps[4][:], func=AF.Sqrt, bias=cst[:, 0:1]),
                     reads=[("ps", 4), "cst"], writes=[("rstd", q)])
                S.op("dve", lambda e: e.reciprocal(rstd[q], rstd[q]), reads=[("rstd", q)], writes=[("rstd", q)])
                sh_o = 0 if wn == 0 else 48
                for kc in range(16):
                    tf = cnt["tf"] % 4
                    cnt["tf"] += 1
                    S.op("dve", lambda e: e.scalar_tensor_tensor(
                        out=tmpf[tf], in0=xt16[q][:, kc, :], scalar=amod[:, l, ci, wn, kc:kc + 1],
                        in1=rstd[q], op0=ALU.mult, op1=ALU.mult),
                        reads=[("xt16", q), ("rstd", q), ("amod", l)], writes=[("tmpf", tf)])
                    S.op("act", lambda e: e.activation(
                        out=hb[:, kc, tt * 512:(tt + 1) * 512], in_=tmpf[tf], func=AF.Identity,
                        bias=modt[:, l, sh_o + kc, ci:ci + 1]),
                        reads=[("tmpf", tf), ("modt", l)], writes=[("h", kc, tt)])

        finals = []

        def resid(G, l, chunk, tt, bank, g_o):
            Tg, xoff, ci = G["T"], G["xoff"], G["ci"]
            xi = cnt["xio"] % 4
            cnt["xio"] += 1
            gt = (xoff + tt * 512) // 512
            c0 = xoff + tt * 512
            S.op("sp", lambda e: e.dma_start(out=xio[xi], in_=X_d[:, chunk, c0:c0 + 512]),
                 reads=[("X", gt, chunk)], writes=[("xio", xi)], dma=True)
            S.op("dve", lambda e: e.scalar_tensor_tensor(
                out=xio[xi], in0=ps[bank][:], scalar=modt[:, l, g_o + chunk, ci:ci + 1], in1=xio[xi],
                op0=ALU.mult, op1=ALU.add),
                reads=[("ps", bank), ("xio", xi), ("modt", l)], writes=[("xio", xi)])
            S.op("sp", lambda e: e.dma_start(out=X_d[:, chunk, c0:c0 + 512], in_=xio[xi]),
                 reads=[("xio", xi)], writes=[("X", gt, chunk)], dma=True)

        def conv_module(G, l):
            Tg, seqs = G["T"], G["seqs"]
            ns, Ls = len(seqs), seqs[0][1]
            Lp = Ls + 30
            cva = bview(SCR, [128, 4, Tg], F32)
            sg = bview(SCR + 4 * Tg, [128, 4, Tg], F32)
            upad = bview(SCR + 8 * Tg, [128, 4, ns * Lp], BF16)
            dg = bview(SCR + 8 * Tg + 2 * ns * Lp + 8, [128, 31, 128], BF16)
            mus = bview(SCR + 8 * Tg + 2 * ns * Lp + 8 + 1984, [128, 512], F32)
            S.op("dve", lambda e: e.memset(upad, 0.0), writes=["upad"])
            b = load_panel(wview(win_d[l], 3072, 512), 16, 512)

            def ev_a(mc, tt, bank):
                copy_op("dve", cva[:, mc, tt * 512:(tt + 1) * 512], ps[bank][:], [("ps", bank)], [("cva", mc, tt)])
            mm_fm(b, 4, hb, "h", Tg, ev_a)
            b = load_panel(wview(win_d[l], 3584, 512), 16, 512)

            def ev_g(mc, tt, bank):
                S.op("act", lambda e: e.activation(out=sg[:, mc, tt * 512:(tt + 1) * 512], in_=ps[bank][:],
                                                   func=AF.Sigmoid),
                     reads=[("ps", bank)], writes=[("sg", mc, tt)])
            mm_fm(b, 4, hb, "h", Tg, ev_g)
            for c in range(4):
                for si, (so, sl) in enumerate(seqs):
                    S.op("dve", lambda e, c=c, si=si, so=so, sl=sl: e.tensor_tensor(
                        out=upad[:, c, si * Lp + 15: si * Lp + 15 + sl], in0=cva[:, c, so:so + sl],
                        in1=sg[:, c, so:so + sl], op=ALU.mult),
                        reads=[("cva", c, t) for t in range(Tg // 512)] + [("sg", c, t) for t in range(Tg // 512)]
                        + ["upad"], writes=[("upc", c)])
            for c in range(4):
                for k in range(31):
                    ow = PV_OFF["cvw"] + (l * 4 + c) * 31 + k
                    S.op("dve", lambda e, k=k, ow=ow: e.tensor_scalar(
                        dg[:, k, :], ident_b[:], pvt[:, ow:ow + 1], None, ALU.mult),
                        reads=["idb", "pv"], writes=[("dg", k)])
                obb = PV_OFF["cvb"] + l * 4 + c
                for si, (so, sl) in enumerate(seqs):
                    for t0 in range(0, sl, 512):
                        n = min(512, sl - t0)
                        bank = next_bank()
                        for ki, k in enumerate([15] + [kk for kk in range(31) if kk != 15]):
                            S.op("pe", lambda e, c=c, k=k, ki=ki, si=si, t0=t0, n=n, bank=bank: e.matmul(
                                ps[bank][:, 0:n], dg[:, k, :], upad[:, c, si * Lp + t0 + k: si * Lp + t0 + k + n],
                                start=(ki == 0), stop=(ki == 30)),
                                reads=[("dg", k), ("upc", c)], writes=[("ps", bank)])
                        S.op("act", lambda e, c=c, so=so, t0=t0, n=n, bank=bank, obb=obb: e.activation(
                            out=cva[:, c, so + t0: so + t0 + n], in_=ps[bank][:, 0:n], func=AF.Identity,
                            bias=pvt[:, obb:obb + 1]),
                            reads=[("ps", bank), "pv"], writes=[("cc", c, (so + t0) // 512)])
            for tt in range(Tg // 512):
                sl_ = slice(tt * 512, (tt + 1) * 512)
                for c in range(4):
                    tb = (cnt["tb"] // 2 * 2) % 4
                    cnt["tb"] = cnt["tb"] // 2 * 2 + 2
                    S.op("act", lambda e, c=c, tb=tb: e.activation(out=tmpb[tb], in_=cva[:, c, sl_], func=AF.Square),
                         reads=[("cc", c, tt)], writes=[("tmpb", tb)])
                    S.op("dve", lambda e, c=c, tb=tb: e.tensor_copy(out=tmpb[tb + 1], in_=cva[:, c, sl_]),
                         reads=[("cc", c, tt)], writes=[("tmpb", tb + 1)])
                    S.op("pe", lambda e, c=c, tb=tb: e.matmul(ps[4][:], ones_c[:], tmpb[tb + 1],
                                                              start=(c == 0), stop=(c == 3)),
                         reads=[("tmpb", tb + 1), "ones_c"], writes=[("ps", 4)])
                    S.op("pe", lambda e, c=c, tb=tb: e.matmul(ps[5][:], ones_c[:], tmpb[tb],
                                                              start=(c == 0), stop=(c == 3)),
                         reads=[("tmpb", tb), "ones_c"], writes=[("ps", 5)])
                q = tt % 2
                S.op("act", lambda e: e.activation(out=mus, in_=ps[4][:], func=AF.Copy),
                     reads=[("ps", 4)], writes=["mus"])
                S.op("dve", lambda e, q=q: e.tensor_tensor(out=rstd[q], in0=mus, in1=mus, op=ALU.mult),
                     reads=["mus"], writes=[("rstd", q)])
                S.op("dve", lambda e, q=q: e.tensor_tensor(out=rstd[q], in0=ps[5][:], in1=rstd[q],
                                                           op=ALU.subtract),
                     reads=[("ps", 5), ("rstd", q)], writes=[("rstd", q)])
                S.op("act", lambda e: e.activation(out=rstd[q], in_=rstd[q], func=AF.Sqrt, bias=cst[:, 0:1]),
                     reads=[("rstd", q), "cst"], writes=[("rstd", q)])
                S.op("dve", lambda e: e.reciprocal(rstd[q], rstd[q]), reads=[("rstd", q)], writes=[("rstd", q)])
                for c in range(4):
                    tf = cnt["tf"] % 4
                    cnt["tf"] += 1
                    S.op("dve", lambda e, c=c, tf=tf: e.tensor_tensor(out=tmpf[tf], in0=cva[:, c, sl_], in1=mus,
                                                                    op=ALU.subtract),
                         reads=[("cc", c, tt), "mus"], writes=[("tmpf", tf)])
                    S.op("dve", lambda e, q=q, tf=tf: e.tensor_tensor(out=tmpf[tf], in0=tmpf[tf], in1=rstd[q],
                                                                    op=ALU.mult),
                         reads=[("tmpf", tf), ("rstd", q)], writes=[("tmpf", tf)])
                    og_ = PV_OFF["cvg"] + l * 4 + c
                    ob_ = PV_OFF["cvlb"] + l * 4 + c
                    S.op("act", lambda e, c=c, tf=tf, og_=og_, ob_=ob_: e.activation(
                        out=mix[:, 8 + c, sl_], in_=tmpf[tf], func=AF.Silu,
                        scale=pvt[:, og_:og_ + 1], bias=pvt[:, ob_:ob_ + 1]),
                        reads=[("tmpf", tf), "pv"], writes=[("mix", 8 + c, tt)])

        def lru(G, l):
            Tg, seqs = G["T"], G["seqs"]
            ns, Ls = len(seqs), seqs[0][1]
            Lp = Ls + 4
            lxp = bview(SCR, [128, 4, ns * Lp], F32)
            lgf = bview(SCR + 4 * ns * Lp, [128, 4, Tg], F32)
            o = SCR + 4 * ns * Lp + 4 * Tg
            xc = bview(o, [128, Tg], F32)
            xcb = bview(o + Tg, [128, Tg], BF16)
            o2 = o + Tg + Tg // 2
            r_ = bview(o2, [128, Tg], F32)
            i_ = bview(o2 + Tg, [128, Tg], F32)
            hd = [bview(o2 + 2 * Tg, [128, Tg], F32), bview(o2 + 3 * Tg, [128, Tg], F32)]
            S.op("dve", lambda e: e.memset(lxp, 0.0), writes=["lxp"])
            S.op("pool", lambda e: e.dma_start(out=lwt[:], in_=lw_d[l]), writes=["lwt"], dma=True)
            b = load_panel(wview(win_d[l], 4096, 512), 16, 512)

            def ev_x(mc, tt, bank):
                for si, (so, sl) in enumerate(seqs):
                    lo, hi = max(so, tt * 512), min(so + sl, (tt + 1) * 512)
                    if lo >= hi:
                        continue
                    copy_op("dve", lxp[:, mc, si * Lp + 2 + lo - so: si * Lp + 2 + hi - so],
                            ps[bank][:, lo - tt * 512: hi - tt * 512], [("ps", bank), "lxp"], [("lx", mc, tt)])
            mm_fm(b, 4, hb, "h", Tg, ev_x)
            b = load_panel(wview(win_d[l], 4608, 512), 16, 512)

            def ev_g(mc, tt, bank):
                copy_op("act", lgf[:, mc, tt * 512:(tt + 1) * 512], ps[bank][:], [("ps", bank)], [("lg", mc, tt)])
            mm_fm(b, 4, hb, "h", Tg, ev_g)
            ntt = Tg // 512
            for c in range(4):
                for _ in range(3):
                    mod_step(l + 1)
                lxk = [("lx", c, t) for t in range(ntt)]
                for si, (so, sl) in enumerate(seqs):
                    for k in range(4):
                        ow = PV_OFF["lcw"] + (l * 4 + c) * 4 + k
                        src = lxp[:, c, si * Lp + k: si * Lp + k + sl]
                        if k == 0:
                            obb = PV_OFF["lcb"] + l * 4 + c
                            S.op("dve", lambda e, src=src, so=so, sl=sl, ow=ow, obb=obb: e.tensor_scalar(
                                xc[:, so:so + sl], src, pvt[:, ow:ow + 1], pvt[:, obb:obb + 1], ALU.mult, ALU.add),
                                reads=lxk + ["pv"], writes=[("xc", si)])
                        else:
                            S.op("dve", lambda e, src=src, so=so, sl=sl, ow=ow: e.scalar_tensor_tensor(
                                out=xc[:, so:so + sl], in0=src, scalar=pvt[:, ow:ow + 1], in1=xc[:, so:so + sl],
                                op0=ALU.mult, op1=ALU.add),
                                reads=lxk + ["pv", ("xc", si)], writes=[("xc", si)])
                xck = [("xc", si) for si in range(ns)]
                S.op("act", lambda e: e.activation(out=xcb, in_=xc, func=AF.Copy), reads=xck, writes=["xcb"])
                for d in range(2):
                    for gi, (dst, bname) in enumerate(((r_, "lba"), (i_, "lbi"))):
                        obb = PV_OFF[bname] + (l * 2 + d) * 4 + c
                        for tt in range(ntt):
                            bank = next_bank()
                            wi = (d * 2 + gi) * 4 + c
                            S.op("pe", lambda e, wi=wi, tt=tt, bank=bank: e.matmul(
                                ps[bank][:], lwt[:, wi, :], xcb[:, tt * 512:(tt + 1) * 512], start=True, stop=True),
                                reads=["lwt", "xcb"], writes=[("ps", bank)])
                            S.op("act", lambda e, dst=dst, tt=tt, bank=bank, obb=obb: e.activation(
                                out=dst[:, tt * 512:(tt + 1) * 512], in_=ps[bank][:], func=AF.Sigmoid,
                                bias=pvt[:, obb:obb + 1]),
                                reads=[("ps", bank), "pv"], writes=[("gate", gi)])
                    oc_ = (l * 2 + d) * 4 + c
                    S.op("act", lambda e, oc_=oc_: e.activation(out=r_, in_=r_, func=AF.Exp, scale=cneg[:, oc_:oc_ + 1]),
                         reads=[("gate", 0), "cneg"], writes=[("gate", 0)])
                    tfs = misc[:, 0:Tg]
                    S.op("dve", lambda e: e.tensor_tensor(out=tfs, in0=r_, in1=r_, op=ALU.mult),
                         reads=[("gate", 0)], writes=["tfs"])
                    S.op("act", lambda e: e.activation(out=tfs, in_=tfs, func=AF.Sqrt, scale=-1.0, bias=cst[:, 1:2]),
                         reads=["tfs", "cst"], writes=["tfs"])
                    S.op("dve", lambda e: e.tensor_tensor(out=i_, in0=i_, in1=tfs, op=ALU.mult),
                         reads=[("gate", 1), "tfs"], writes=[("gate", 1)])
                    S.op("dve", lambda e: e.tensor_tensor(out=i_, in0=i_, in1=xc, op=ALU.mult),
                         reads=[("gate", 1)] + xck, writes=[("gate", 1)])
                    for si, (so, sl) in enumerate(seqs):
                        if G["h0"]:
                            oh = PV_OFF["h0"] + (l * 2 + d) * 4 + c
                            init = pvt[:, oh:oh + 1]
                        else:
                            init = 0.0
                        if d == 0:
                            oa, a0, a1 = hd[0][:, so:so + sl], r_[:, so:so + sl], i_[:, so:so + sl]
                        else:
                            oa = hd[1][:, so + sl - 1: so - 1 if so > 0 else None: -1]
                            a0 = r_[:, so + sl - 1: so - 1 if so > 0 else None: -1]
                            a1 = i_[:, so + sl - 1: so - 1 if so > 0 else None: -1]
                        S.op("dve", lambda e, oa=oa, a0=a0, a1=a1, init=init: e.tensor_tensor_scan(
                            out=oa, data0=a0, data1=a1, initial=init, op0=ALU.mult, op1=ALU.add),
                            reads=[("gate", 0), ("gate", 1), "pv"], writes=[("hd", d)])
                        if G["nl"]:
                            col = ((G["nl_b"] + si) * L + l) * 2 * 4 + d * 4 + c
                            tcol = so + sl - 1 if d == 0 else so
                            S.op("act", lambda e, d=d, col=col, tcol=tcol: e.activation(
                                out=nl_t[:, col:col + 1], in_=hd[d][:, tcol:tcol + 1], func=AF.Copy),
                                reads=[("hd", d)], writes=["nl_t"])
                lgk = [("lg", c, t) for t in range(ntt)]
                lgc = lgf[:, c, :]
                S.op("dve", lambda e: e.tensor_tensor(out=hd[0], in0=hd[0], in1=hd[1], op=ALU.add),
                     reads=[("hd", 0), ("hd", 1)], writes=[("hd", 0)])
                S.op("dve", lambda e, lgc=lgc: e.tensor_tensor(out=r_, in0=lgc, in1=lgc, op=ALU.mult),
                     reads=lgk + [("gate", 0)], writes=[("gate", 0)])
                S.op("dve", lambda e: e.tensor_scalar(r_, r_, 0.044715, 1.0, ALU.mult, ALU.add),
                     reads=[("gate", 0)], writes=[("gate", 0)])
                S.op("dve", lambda e, lgc=lgc: e.tensor_tensor(out=r_, in0=r_, in1=lgc, op=ALU.mult),
                     reads=lgk + [("gate", 0)], writes=[("gate", 0)])
                S.op("act", lambda e: e.activation(out=r_, in_=r_, func=AF.Sigmoid, scale=1.5957691216057308),
                     reads=[("gate", 0)], writes=[("gate", 0)])
                S.op("dve", lambda e, lgc=lgc: e.tensor_tensor(out=r_, in0=r_, in1=lgc, op=ALU.mult),
                     reads=lgk + [("gate", 0)], writes=[("gate", 0)])
                S.op("dve", lambda e, c=c: e.tensor_tensor(out=mix[:, 12 + c, 0:Tg], in0=r_, in1=hd[0], op=ALU.mult),
                     reads=[("gate", 0), ("hd", 0)], writes=[("mix", 12 + c, t) for t in range(ntt)])

        def attention(G, l):
            Tg, seqs, sample = G["T"], G["seqs"], G["sample"]
            astop = (ATT_STOP if ATT_STOP < 10 else 0) if sample else (ATT_STOP - 10 if ATT_STOP >= 10 else 0)
            q4 = bview(SCR, [128, 4, Tg], BF16)
            k4 = bview(SCR + 2 * Tg, [128, 4, Tg], BF16)
            v4 = bview(SCR + 4 * Tg, [128, Tg // 128, 512], BF16)
            o = SCR + 6 * Tg
            kctx = bview(o, [128, 4, 512], BF16)
            vctx = bview(o + 1024, [128, 4, 512], BF16)
            cin = misc[:, 0:1024].bitcast(BF16).rearrange("p (a b) -> p a b", a=4)
            o += 2048
            sc = [bview(o, [128, 1152], F32), bview(o + 1152, [128, 1152], F32)]
            o += 2304
            pb = [bview(o, [128, 1152], BF16), bview(o + 576, [128, 1152], BF16)]
            o += 1152
            pt = [bview(o, [128, 9, 128], BF16), bview(o + 576, [128, 9, 128], BF16)]
            o += 1152
            btl = [misc[:, 1024:1664], misc[:, 1664:2304]]
            mx = bview(o, [128, 8], F32)
            rb = [bview(o + 8, [128, 128], F32), bview(o + 136, [128, 128], F32)]
            kvo = [misc[:, 2304:2816], misc[:, 2816:3328]]
            assert o + 264 <= 22528, o
            it = 0
            for hg in range(2):
                S.barrier()
                b = load_panel(wview(win_d[l], hg * 512, 512), 16, 512)

                def ev_q(mc, tt, bank):
                    copy_op(evac_eng(), q4[:, mc, tt * 512:(tt + 1) * 512], ps[bank][:], [("ps", bank)], [("q4", mc, tt)])
                mm_fm(b, 4, hb, "h", Tg, ev_q)
                if astop == 5:
                    raise _Stop()
                b = load_panel(wview(win_d[l], 1024 + hg * 512, 512), 16, 512)

                def ev_k(mc, tt, bank):
                    copy_op(evac_eng(), k4[:, mc, tt * 512:(tt + 1) * 512], ps[bank][:], [("ps", bank)], [("k4", mc, tt)])
                mm_fm(b, 4, hb, "h", Tg, ev_k)
                if astop == 6:
                    raise _Stop()
                if not sample:
                    def ev_ktm(ti, bank):
                        kq = ti % 2
                        copy_op(evac_eng(), kvo[kq], ps[bank][:], [("ps", bank)], [("kvo", kq)])
                        bi, t0 = ti // 2, (ti % 2) * 128
                        if SKIP_KV_OUT:
                            return
                        finals.append(S.op("sp", lambda e: e.dma_start(
                            out=nk_d[bi, l, t0:t0 + 128, hg * 512:(hg + 1) * 512], in_=kvo[kq]),
                            reads=[("kvo", kq)], dma=True))
                    mm_tm(b, hb, "h", Tg, ev_ktm)
                    if astop == 7:
                        raise _Stop()
                b = load_panel(wview(win_d[l], 2048 + hg * 512, 512), 16, 512)

                def ev_v(ti, bank):
                    if sample:
                        copy_op("act", v4[:, ti, :], ps[bank][:], [("ps", bank)], [("v4", ti)])
                        return
                    kq = ti % 2
                    copy_op("act", kvo[kq], ps[bank][:], [("ps", bank)], [("kvo", kq)])
                    copy_op("dve", v4[:, ti, :], kvo[kq], [("kvo", kq)], [("v4", ti)])
                    bi, t0 = ti // 2, (ti % 2) * 128
                    if SKIP_KV_OUT:
                        return
                    finals.append(S.op("sp", lambda e: e.dma_start(
                        out=nv_d[bi, l, t0:t0 + 128, hg * 512:(hg + 1) * 512], in_=kvo[kq]),
                        reads=[("kvo", kq)], dma=True))
                mm_tm(b, hb, "h", Tg, ev_v)
                if astop == 1:
                    raise _Stop()
                if sample:
                    ckv = ck_d[l].rearrange("(a p) f -> p a f", p=128)[:, :, hg * 512:(hg + 1) * 512]
                    cvw_ = cv_d[l].rearrange("(a p) f -> p a f", p=128)[:, :, hg * 512:(hg + 1) * 512]
                    S.op("pool", lambda e: e.dma_start(out=cin, in_=ckv), writes=["cin"], dma=True)
                    S.op("pool", lambda e: e.dma_start(out=vctx, in_=cvw_), writes=["vctx"], dma=True)
                    for hh in range(4):
                        for a in range(4):
                            S.op("pe", lambda e, hh=hh, a=a: e.transpose(
                                psb[hh % 2][:, a * 128:(a + 1) * 128], cin[:, a, hh * 128:(hh + 1) * 128], ident_b[:]),
                                reads=["cin", "idb"], writes=[("psb", hh % 2)])
                        copy_op(evac_eng(), kctx[:, hh, :], psb[hh % 2][:, 0:512], [("psb", hh % 2)], [("kctx", hh)])
                if astop == 2:
                    raise _Stop()
                def stage_a(hh, so, q0, pi, u, itn):
                    h = hg * 4 + hh
                    if sample:
                        a_row, nw = _pair_win(pi)
                        k0, nwk = a_row * 64, nw * 64
                        nk_all = nwk + 512
                        S.op("sp", lambda e: e.dma_start(
                            out=btl[u][:, 0:nwk], in_=bt_d[l, h, pi, :, 0:nwk]), writes=[("btl", u)], dma=True)
                    else:
                        k0, nwk = so, 256
                        nk_all = 256
                    bW, bC, bM = (0, 1, 2) if u == 0 else (3, 4, 5)
                    segs = [(k0, min(nwk, 512), bW, 0, 0)]
                    if nwk > 512:
                        segs.append((k0 + 512, nwk - 512, bM, 512, 0))
                    kkeys = [("k4", hh, t) for t in range(Tg // 512)]
                    for (ks, kn, bank, off, pc) in segs:
                        S.op("pe", lambda e: e.matmul(
                            ps[bank][:, pc:pc + kn], q4[:, hh, q0:q0 + 128], k4[:, hh, ks:ks + kn],
                            start=True, stop=True),
                            reads=[("q4", hh, q0 // 512)] + kkeys, writes=[("ps", bank)])
                        if sample:
                            S.op("dve", lambda e: e.scalar_tensor_tensor(
                                out=sc[u][:, off:off + kn], in0=ps[bank][:, pc:pc + kn], scalar=ATT_SCALE,
                                in1=btl[u][:, off:off + kn], op0=ALU.mult, op1=ALU.add),
                                reads=[("ps", bank), ("btl", u)], writes=[("sc", u)])
                        else:
                            S.op("dve", lambda e: e.tensor_scalar(
                                sc[u][:, off:off + kn], ps[bank][:, pc:pc + kn], ATT_SCALE, None, ALU.mult),
                                reads=[("ps", bank)], writes=[("sc", u)])
                    if sample:
                        S.op("pe", lambda e: e.matmul(
                            ps[bC][:], q4[:, hh, q0:q0 + 128], kctx[:, hh, :], start=True, stop=True),
                            reads=[("q4", hh, q0 // 512), ("kctx", hh)], writes=[("ps", bC)])
                        S.op("act", lambda e: e.activation(
                            out=sc[u][:, nwk:nwk + 512], in_=ps[bC][:], func=AF.Copy, scale=ATT_SCALE),
                            reads=[("ps", bC)], writes=[("sc", u)])
                    mcol = itn % 8
                    S.op("dve", lambda e: e.tensor_reduce(
                        out=mx[:, mcol:mcol + 1], in_=sc[u][:, 0:nk_all], axis=mybir.AxisListType.X, op=ALU.max,
                        negate=True),
                        reads=[("sc", u)], writes=[("mx", mcol)])
                    S.op("act", lambda e: e.activation(
                        out=pb[u][:, 0:nk_all], in_=sc[u][:, 0:nk_all], func=AF.Exp, bias=mx[:, mcol:mcol + 1]),
                        reads=[("sc", u), ("mx", mcol)], writes=[("pb", u)])
                    return (hh, h, so, q0, u, k0, nwk, nk_all, bM)

                def stage_b(ctx):
                    hh, h, so, q0, u, k0, nwk, nk_all, bM = ctx
                    nkt = nk_all // 128
                    for g0 in range(0, nkt, 4):
                        gn = min(4, nkt - g0)
                        pbk = (g0 // 4) % 2
                        for j in range(gn):
                            S.op("pe", lambda e: e.transpose(
                                psb[pbk][:, j * 128:(j + 1) * 128], pb[u][:, (g0 + j) * 128:(g0 + j + 1) * 128],
                                ident_b[:]),
                                reads=[("pb", u), "idb"], writes=[("psb", pbk)])
                        copy_op(evac_eng(), pt[u][:, g0:g0 + gn, :],
                                psb[pbk][:, 0:gn * 128].rearrange("p (a b) -> p a b", a=gn),
                                [("psb", pbk)], [("pt", u, g0 // 4)])
                    ptk = [("pt", u, g) for g in range((nkt + 3) // 4)]
                    for j in range(nkt):
                        S.op("pe", lambda e: e.matmul(
                            ps[bM][:, 128:256], ones_1[:], pt[u][:, j, :], start=(j == 0), stop=(j == nkt - 1)),
                            reads=ptk + ["ones_1"], writes=[("ps", bM)])
                    for j in range(nkt):
                        if sample:
                            if j < nwk // 128:
                                vsrc = v4[:, k0 // 128 + j, hh * 128:(hh + 1) * 128]
                                vk = ("v4", k0 // 128 + j)
                            else:
                                vsrc = vctx[:, j - nwk // 128, hh * 128:(hh + 1) * 128]
                                vk = "vctx"
                        else:
                            vsrc = v4[:, so // 128 + j, hh * 128:(hh + 1) * 128]
                            vk = ("v4", so // 128 + j)
                        S.op("pe", lambda e: e.matmul(
                            ps[bM][:, 256:384], vsrc, pt[u][:, j, :], start=(j == 0), stop=(j == nkt - 1)),
                            reads=ptk + [vk], writes=[("ps", bM)])
                    S.op("dve", lambda e: e.reciprocal(rb[u], ps[bM][:, 128:256]),
                         reads=[("ps", bM)], writes=[("rb", u)])
                    S.op("dve", lambda e: e.tensor_tensor(
                        out=mix[:, h, q0:q0 + 128], in0=ps[bM][:, 256:384], in1=rb[u], op=ALU.mult),
                        reads=[("ps", bM), ("rb", u)], writes=[("mixq", h, q0)])

                tl = []
                for hh in range(4):
                    if sample:
                        tl += [(hh, 0, i * 128, i) for i in range(8)]
                    else:
                        tl += [(hh, so, so + t0, None) for (so, sl) in seqs for t0 in range(0, sl, 128)]
                prev = None
                for (hh, so, q0, pi) in tl:
                    ctx = stage_a(hh, so, q0, pi, it % 2, it)
                    it += 1
                    if prev is not None:
                        stage_b(prev)
                    prev = ctx
                stage_b(prev)
            for h in range(8):
                for tt in range(Tg // 512):
                    S.op("dve", lambda e, h=h, tt=tt: e.tensor_copy(out=mx[:, 0:1], in_=mx[:, 0:1]),
                         reads=[("mixq", h, q0) for q0 in range(tt * 512, (tt + 1) * 512, 128)] + [("mx", 0)],
                         writes=[("mix", h, tt), ("mx", 0)])

        def w_out(G, l):
            Tg = G["T"]
            for p in range(4):
                b = load_panel(wview(wout_d[l], p * 512, 512), 16, 512)

                def ev(mc, tt, bank, p=p):
                    resid(G, l, p * 4 + mc, tt, bank, 32)
                mm_fm(b, 4, mix, "mix", Tg, ev)

        def ffn(G, l):
            Tg, seqs = G["T"], G["seqs"]
            ns, Ls = len(seqs), seqs[0][1]
            Lp = Ls + 2
            upd = [sbx_up[0], sbx_up[1]]
            cc = [sbx_cc[0], sbx_cc[1]]
            for uq in range(2):
                S.op("dve", lambda e, uq=uq: e.memset(upd[uq], 0.0), writes=[("upd", uq)])
            jn = 0
            for p in range(22):
                if G["sample"]:
                    mod_step(l + 1)
                b = load_panel(wview(fup_d[l], p * 512, 512), 16, 512)
                for mc in range(4):
                    j = p * 4 + mc
                    ja = j % 44
                    uq = jn % 2
                    jn += 1
                    owc = PV_OFF["fcw"] + (l * 88 + j) * 3
                    obc = PV_OFF["fcb"] + l * 88 + j
                    banks = []
                    for tt in range(Tg // 512):
                        bank = next_bank()
                        banks.append(bank)
                        for kc in range(16):
                            S.op("pe", lambda e, kc=kc, mc=mc, tt=tt, bank=bank, b=b: e.matmul(
                                ps[bank][:], wbuf[b][:, kc, mc * 128:(mc + 1) * 128],
                                hb[:, kc, tt * 512:(tt + 1) * 512], start=(kc == 0), stop=(kc == 15)),
                                reads=[("wb", b), ("h", kc, tt)], writes=[("ps", bank)])
                        for si, (so, sl) in enumerate(seqs):
                            lo, hi = max(so, tt * 512), min(so + sl, (tt + 1) * 512)
                            if lo >= hi:
                                continue
                            S.op("act", lambda e, uq=uq, si=si, so=so, lo=lo, hi=hi, tt=tt, bank=bank: e.activation(
                                out=upd[uq][:, si * Lp + 1 + lo - so: si * Lp + 1 + hi - so],
                                in_=ps[bank][:, lo - tt * 512: hi - tt * 512], func=AF.Copy),
                                reads=[("ps", bank), ("upd", uq)], writes=[("updw", uq, tt)])
                        S.op("act", lambda e, uq=uq, tt=tt, bank=bank, owc=owc, obc=obc: e.activation(
                            out=cc[uq][:, tt * 512:(tt + 1) * 512], in_=ps[bank][:], func=AF.Identity,
                            scale=pvt[:, owc + 1:owc + 2], bias=pvt[:, obc:obc + 1]),
                            reads=[("ps", bank), "pv"], writes=[("cc", uq, tt)])
                    ntt = Tg // 512
                    updk = [("updw", uq, t) for t in range(ntt)]
                    cck = [("cc", uq, t) for t in range(ntt)]
                    for si, (so, sl) in enumerate(seqs):
                        for k in (0, 2):
                            S.op("dve", lambda e, uq=uq, si=si, so=so, sl=sl, k=k, owc=owc: e.scalar_tensor_tensor(
                                out=cc[uq][:, so:so + sl], in0=upd[uq][:, si * Lp + k: si * Lp + k + sl],
                                scalar=pvt[:, owc + k:owc + k + 1], in1=cc[uq][:, so:so + sl],
                                op0=ALU.mult, op1=ALU.add),
                                reads=updk + cck + ["pv"], writes=[("ccf", uq)])
                    if j < 44:
                        S.op("act", lambda e, uq=uq, ja=ja: e.activation(out=act[:, ja, 0:Tg], in_=cc[uq][:, 0:Tg],
                                                                        func=AF.Silu),
                             reads=[("ccf", uq)] + cck, writes=[("act", ja)])
                    else:
                        S.op("dve", lambda e, uq=uq, ja=ja: e.tensor_tensor(
                            out=act[:, ja, 0:Tg], in0=act[:, ja, 0:Tg], in1=cc[uq][:, 0:Tg], op=ALU.mult),
                            reads=[("ccf", uq), ("act", ja)] + cck, writes=[("act", ja)])
            S.barrier()
            ntt = Tg // 512
            for mp in range(8):
                base = 0
                for ks, (k0, nk) in enumerate(((0, 16), (16, 16), (32, 12))):
                    if G["sample"] and ks == 0 and mp < 2:
                        mod_step(l + 1)
                    b = load_panel(wview(fdn_d[l], mp * 256, 256, k0, nk), nk, 256)
                    for oc_ in range(2):
                        for tt in range(ntt):
                            bank = base + oc_ * 2 + tt
                            for kc in range(nk):
                                S.op("pe", lambda e, kc=kc, k0=k0, oc_=oc_, tt=tt, bank=bank, b=b, ks=ks, nk=nk: e.matmul(
                                    ps[bank][:], wbuf[b][:, kc, oc_ * 128:(oc_ + 1) * 128],
                                    act[:, k0 + kc, tt * 512:(tt + 1) * 512],
                                    start=(ks == 0 and kc == 0), stop=(ks == 2 and kc == nk - 1)),
                                    reads=[("wb", b), ("act", k0 + kc)], writes=[("ps", bank)])
                for oc_ in range(2):
                    for tt in range(ntt):
                        resid(G, l, mp * 2 + oc_, tt, base + oc_ * 2 + tt, 80)

        def final_norm(G):
            ystage = [wbuf[i][:].rearrange("p a b -> p (a b)").bitcast(F32).rearrange("p (a f) -> p a f", a=4)
                      for i in range(2)]
            Tg, xoff, ydst = G["T"], G["xoff"], G["ydst"]
            for tt in range(Tg // 512):
                q = tt % 2
                gt = (xoff + tt * 512) // 512
                S.op("sp", lambda e: e.dma_start(
                    out=xt16[q], in_=X_d[:, :, xoff + tt * 512: xoff + (tt + 1) * 512]),
                    reads=[("X", gt, c) for c in range(16)], writes=[("xt16", q)], dma=True)
                for kc in range(16):
                    tb = cnt["tb"] % 4
                    cnt["tb"] += 1
                    S.op("act", lambda e: e.activation(out=tmpb[tb], in_=xt16[q][:, kc, :], func=AF.Square),
                         reads=[("xt16", q)], writes=[("tmpb", tb)])
                    S.op("pe", lambda e: e.matmul(ps[4][:], ones_d[:], tmpb[tb], start=(kc == 0), stop=(kc == 15)),
                         reads=[("tmpb", tb), "ones_d"], writes=[("ps", 4)])
                S.op("act", lambda e: e.activation(out=rstd[q], in_=ps[4][:], func=AF.Sqrt, bias=cst[:, 0:1]),
                     reads=[("ps", 4), "cst"], writes=[("rstd", q)])
                S.op("dve", lambda e: e.reciprocal(rstd[q], rstd[q]), reads=[("rstd", q)], writes=[("rstd", q)])
                for half in range(2):
                    for kk in range(8):
                        kc = half * 8 + kk
                        tf = cnt["tf"] % 4
                        cnt["tf"] += 1
                        og = PV_OFF["fing"] + kc
                        S.op("dve", lambda e: e.scalar_tensor_tensor(
                            out=tmpf[tf], in0=xt16[q][:, kc, :], scalar=pvt[:, og:og + 1], in1=rstd[q],
                            op0=ALU.mult, op1=ALU.mult),
                            reads=[("xt16", q), ("rstd", q), "pv"], writes=[("tmpf", tf)])
                        bank = kc % 4
                        for j in range(4):
                            S.op("pe", lambda e: e.transpose(
                                ps[bank][:, j * 128:(j + 1) * 128], tmpf[tf][:, j * 128:(j + 1) * 128], ident_f[:]),
                                reads=[("tmpf", tf), "idf"], writes=[("ps", bank)])
                        copy_op("act" if kc % 2 else "dve", ystage[half][:, :, kk * 128:(kk + 1) * 128],
                                ps[bank][:].rearrange("p (a b) -> p a b", a=4), [("ps", bank)], [("wb", half)])
                    finals.append(S.op("sp", lambda e: e.dma_start(
                        out=ydst[tt * 512:(tt + 1) * 512, half * 1024:(half + 1) * 1024].rearrange(
                            "(a p) f -> p a f", p=128), in_=ystage[half]),
                        reads=[("wb", half)], dma=True))

        GS = dict(T=TS, xoff=0, ci=1, seqs=[(0, TS)], sample=True, h0=True, nl=False, ydst=ys_d)
        GP = dict(T=TP, xoff=TS, ci=0, seqs=[(0, 256), (256, 256)], sample=False, h0=False, nl=True, nl_b=0,
                  ydst=yp_d)
        def chk(n):
            if STAGE == n:
                raise _Stop()

        _chk0 = chk
        try:
            chk(1)
            for l in range(NLAYERS):
                for gi_, G in enumerate((GS, GP)):
                    _chk = chk
                    chk = (lambda n, gi_=gi_, _c=_chk0: _c(n + 10 * gi_))
                    S.barrier()
                    norm(G, l, 0)
                    chk(2)
                    S.barrier()
                    conv_module(G, l)
                    chk(3)
                    S.barrier()
                    lru(G, l)
                    chk(4)
                    attention(G, l)
                    chk(5)
                    S.barrier()
                    w_out(G, l)
                    chk(6)
                    S.barrier()
                    norm(G, l, 1)
                    S.barrier()
                    ffn(G, l)
                    chk(7)
                while mod_state["l"] == l + 1 and mod_state["l"] < NLAYERS:
                    mod_step()
            _chk0(20)
            S.barrier()
            final_norm(GS)
            final_norm(GP)
            _chk0(21)
            S.barrier()
            S.op("pe", lambda e: e.transpose(ps[0][0:64, 0:128], nl_t[:, 0:64], ident_f[:]),
                 reads=["nl_t", "idf"], writes=[("ps", 0)])
            S.op("dve", lambda e: e.tensor_copy(out=tmpf[0][0:64, 0:128], in_=ps[0][0:64, 0:128]),
                 reads=[("ps", 0)], writes=[("tmpf", 0)])
            finals.append(S.op("sp", lambda e: e.dma_start(out=nl_d, in_=tmpf[0][0:64, 0:128]),
                               reads=[("tmpf", 0)], dma=True))
        except _Stop:
            pass
        for _e in S.ENGS:
            if S.ops[_e]:
                S.ops[_e][-1].signal = True
                finals.append(S.ops[_e][-1])
        finals.extend(d for d in S.dlast if d is not None)
        S.emit(final_waits=finals)
    return nc


_NC = None


def kernel(x_prompt, x_sample, cache_k, cache_v, state_lru, c, c_ctx, ln1_g, w_mod, b_mod, w_in, na_bias,
           cv_w, cv_b, cv_ln_g, cv_ln_b, lru_conv_w, lru_conv_b, lru_wa, lru_ba, lru_wi, lru_bi, lru_lam,
           w_out, ln2_g, ffn_up, ffn_conv_w, ffn_conv_b, ffn_down, final_g):
    global _NC
    f = lambda a: np.ascontiguousarray(np.asarray(a, np.float32))
    x_prompt, x_sample, cache_k, cache_v = f(x_prompt), f(x_sample), f(cache_k), f(cache_v)
    if _NC is None:
        _NC = build_nc()
    nc = _NC
    bt = _build_bias(na_bias)
    lwa, lwi = np.asarray(lru_wa, np.float32), np.asarray(lru_wi, np.float32)
    lw = np.zeros((L, 128, 16, 128), np.float32)
    for d in range(2):
        for gi, w in enumerate((lwa, lwi)):
            for cch in range(4):
                for hb_ in range(2):
                    blk = w[:, d, cch * 2 + hb_]
                    lw[:, hb_ * 64:(hb_ + 1) * 64, (d * 2 + gi) * 4 + cch, hb_ * 64:(hb_ + 1) * 64] = blk
    ident = np.eye(128, dtype=np.float32)
    NLc = NLAYERS
    shared = dict(wmod=f(w_mod)[:NLc], win=f(w_in)[:NLc], wout=f(w_out)[:NLc], fup=f(ffn_up)[:NLc],
                  fdn=f(ffn_down)[:NLc], lw=lw[:NLc], bt=bt[:NLc], ident=ident)

    def pv_for(s):
        parts = {
            "ln1g": _fm(ln1_g, 16), "ln2g": _fm(ln2_g, 16), "fing": _fm(final_g, 16), "bmod": _fm(b_mod, 96),
            "cond": np.moveaxis(_fm(np.stack([np.asarray(c_ctx), np.asarray(c)[s]]), 16), 1, 2),
            "cvw": np.moveaxis(_fm(cv_w, 4), 2, 3),
            "cvb": _fm(cv_b, 4), "cvg": _fm(cv_ln_g, 4), "cvlb": _fm(cv_ln_b, 4),
            "lcw": np.moveaxis(_fm(lru_conv_w, 4), 2, 3), "lcb": _fm(lru_conv_b, 4),
            "lba": _fm(lru_ba, 4), "lbi": _fm(lru_bi, 4), "lam": _fm(lru_lam, 4),
            "h0": _fm(np.asarray(state_lru)[s], 4),
            "fcw": np.moveaxis(_fm(ffn_conv_w, 88), 2, 3), "fcb": _fm(ffn_conv_b, 88),
        }
        cols = []
        for n, cnt_ in PV_ITEMS:
            a = np.ascontiguousarray(parts[n], dtype=np.float32).reshape(128, -1)
            assert a.shape[1] == cnt_, (n, a.shape, cnt_)
            cols.append(a)
        return np.ascontiguousarray(np.concatenate(cols, axis=1))

    pvs = [pv_for(0), pv_for(1)]
    in_maps = []
    for core in range(8):
        s = core % 2
        m = dict(shared)
        m["xs"] = x_sample[s]
        m["xp"] = x_prompt[2 * core:2 * core + 2].reshape(TP, D)
        m["ck"] = cache_k[s].reshape(L, 512, 1024)[:NLc]
        m["cvv"] = cache_v[s].reshape(L, 512, 1024)[:NLc]
        m["pv"] = pvs[s]
        in_maps.append(m)
    res = run_bass_kernel_spmd(nc, in_maps[:NCORES], core_ids=list(range(NCORES)))
    R = list(res.results)
    while len(R) < 8:
        R.append(R[len(R) % NCORES])
    y_prompt = np.concatenate([R[i]["yp"].reshape(2, 256, D) for i in range(8)], axis=0)
    y_sample = np.stack([R[0]["ys"], R[1]["ys"]], axis=0)
    new_k = np.concatenate([R[i]["nk"].reshape(2, L, 256, NH, 128) for i in range(8)], axis=0)
    new_v = np.concatenate([R[i]["nv"].reshape(2, L, 256, NH, 128) for i in range(8)], axis=0)
    new_lru = np.concatenate([R[i]["nl"].reshape(2, L, 2, 512) for i in range(8)], axis=0)
    return (y_prompt.astype(np.float32), y_sample.astype(np.float32), new_k.astype(np.float32),
            new_v.astype(np.float32), new_lru.astype(np.float32))
```

```python
import numpy as np
from contextlib import ExitStack
import concourse.bass as bass
import concourse.mybir as mybir
from concourse.bass_utils import run_bass_kernel_spmd

F32 = mybir.dt.float32
BF16 = mybir.dt.bfloat16
AF = mybir.ActivationFunctionType
ALU = mybir.AluOpType

D = 2048
L = 4
NH = 8
DFF = 5632
TS = 1024
TP = 512
TT = TS + TP
NEG = -1e30
NLAYERS = 4
STAGE = 99
ATT_STOP = 0
SKIP_KV_OUT = 0
NCORES = 8


class _Stop(Exception):
    pass
EPS = 1e-6
ATT_SCALE = 128 ** -0.5


class _Rec:
    def __init__(self):
        self.call = None

    def __getattr__(self, name):
        def f(*a, **k):
            self.call = (name, a, k)
            return self
        return f


class _Op:
    __slots__ = ("eng", "fn", "deps", "signal", "dma", "dsem", "dval", "cnt")

    def __init__(self, eng, fn, dma):
        self.eng = eng
        rec = _Rec()
        fn(rec)
        self.fn = rec.call
        self.deps = []
        self.signal = False
        self.dma = dma
        self.dsem = None
        self.dval = 0
        self.cnt = 0


class Sched:
    ENGS = ("pe", "act", "dve", "pool", "sp")

    def __init__(self, nc, stack, n_dma_sems=16):
        self.nc = nc
        self.ops = {e: [] for e in self.ENGS}
        self.last_w = {}
        self.readers = {}
        self.sems = {e: stack.enter_context(nc.semaphore("s_" + e)) for e in self.ENGS}
        self.dsems = [stack.enter_context(nc.semaphore("d_%d" % i)) for i in range(n_dma_sems)]
        self.dcount = [0] * n_dma_sems
        self.dlast = [None] * n_dma_sems
        self.dnext = 0
        self.pending_bar = {e: None for e in self.ENGS}

    def barrier(self):
        deps = []
        for e in self.ENGS:
            if self.ops[e]:
                deps.append(self.ops[e][-1])
        for d in self.dlast:
            if d is not None:
                deps.append(d)
        for e in self.ENGS:
            old = self.pending_bar[e]
            self.pending_bar[e] = deps if old is None else deps

    def op(self, eng, fn, reads=(), writes=(), dma=False, nobar=False):
        o = _Op(eng, fn, dma)
        deps = []
        if not nobar and self.pending_bar[eng] is not None:
            deps.extend(self.pending_bar[eng])
            self.pending_bar[eng] = None
        for k in reads:
            w = self.last_w.get(k)
            if w is not None:
                deps.append(w)
        for k in writes:
            w = self.last_w.get(k)
            if w is not None:
                deps.append(w)
            rs = self.readers.get(k)
            if rs:
                deps.extend(rs)
        if dma:
            s = self.dnext
            self.dnext = (self.dnext + 1) % len(self.dsems)
            if self.dlast[s] is not None:
                deps.append(self.dlast[s])
            self.dcount[s] += 16
            o.dsem = s
            o.dval = self.dcount[s]
            self.dlast[s] = o
        seen = set()
        for d in deps:
            if d is o or id(d) in seen:
                continue
            seen.add(id(d))
            if d.eng == "pe" and eng == "pe" and not d.dma and not dma:
                continue
            o.deps.append(d)
            d.signal = True
        for k in reads:
            lst = self.readers.setdefault(k, [])
            if not dma:
                lst[:] = [r for r in lst if r.dma or r.eng != eng]
            lst.append(o)
        for k in writes:
            self.last_w[k] = o
            self.readers[k] = []
        self.ops[eng].append(o)
        return o

    def emit(self, final_waits=()):
        nc = self.nc
        for e in self.ENGS:
            c = 0
            for o in self.ops[e]:
                if o.signal and not o.dma:
                    c += 1
                    o.cnt = c
        final_waits = list(final_waits)

        def run(e, engh):
            waited = {}

            def wait(d):
                if d.dma:
                    key, sem, val = ("d", d.dsem), self.dsems[d.dsem], d.dval
                else:
                    key, sem, val = ("e", d.eng), self.sems[d.eng], d.cnt
                if waited.get(key, 0) >= val:
                    return
                waited[key] = val
                engh.wait_ge(sem, val)

            for o in self.ops[e]:
                for d in o.deps:
                    wait(d)
                name, a, k = o.fn
                ins = getattr(engh, name)(*a, **k)
                if o.dma:
                    ins.then_inc(self.dsems[o.dsem], 16)
                elif o.signal:
                    ins.then_inc(self.sems[e], 1)
            if e == "sp":
                for d in final_waits:
                    wait(d)

        with nc.Block() as block:
            @block.tensor
            def _(t):
                run("pe", t)

            @block.scalar
            def _(t):
                run("act", t)

            @block.vector
            def _(t):
                run("dve", t)

            @block.gpsimd
            def _(t):
                run("pool", t)

            @block.sync
            def _(t):
                run("sp", t)


PV_ITEMS = [
    ("ln1g", L * 16), ("ln2g", L * 16), ("fing", 16), ("bmod", L * 96), ("cond", 32),
    ("cvw", L * 4 * 31), ("cvb", L * 4), ("cvg", L * 4), ("cvlb", L * 4),
    ("lcw", L * 4 * 4), ("lcb", L * 4), ("lba", L * 8), ("lbi", L * 8), ("lam", L * 8), ("h0", L * 8),
    ("fcw", L * 88 * 3), ("fcb", L * 88),
]
PV_OFF = {}
_o = 0
for _n, _c in PV_ITEMS:
    PV_OFF[_n] = _o
    _o += _c
NV = _o


def _fm(v, nch):
    v = np.asarray(v, np.float32)
    lead = v.shape[:-1]
    return np.moveaxis(v.reshape(*lead, nch, 128), -1, 0)


def _pair_win(i):
    r0, r1 = 2 * i, 2 * i + 1
    s0 = min(max(r0 - 4, 0), 8)
    s1 = min(max(r1 - 4, 0), 8)
    a = s0 & ~1
    e = (s1 + 8 + 1) & ~1
    return a, e - a


def _build_bias(na_bias):
    na = np.asarray(na_bias, np.float32)
    cols = np.arange(64)
    cs = np.clip(cols - 8, 0, 48)
    inwin = (cols[None, :] >= cs[:, None]) & (cols[None, :] < cs[:, None] + 16)
    co = np.clip(cols[None, :] - cols[:, None], -15, 15) + 15
    out = np.full((L, NH, 8, 128, 640), NEG, np.float32)
    for i in range(8):
        a, nw = _pair_win(i)
        for half in range(2):
            r = 2 * i + half
            st = min(max(r - 4, 0), 8)
            for j in range(nw):
                kr = a + j
                if not (st <= kr < st + 8):
                    continue
                blk = na[:, :, kr - r + 7, :][:, :, co]
                blk = np.where(inwin[None, None], blk, np.float32(NEG))
                out[:, :, i, half * 64:(half + 1) * 64, j * 64:(j + 1) * 64] = blk
    return out


def build_nc():
    nc = bass.Bass("TRN2", target_bir_lowering=False)

    def din(name, shape):
        return nc.dram_tensor(name, list(shape), F32, kind="ExternalInput").ap()

    def dout(name, shape):
        return nc.dram_tensor(name, list(shape), F32, kind="ExternalOutput").ap()

    xs_d = din("xs", [TS, D])
    xp_d = din("xp", [TP, D])
    ck_d = din("ck", [NLAYERS, 512, 1024])
    cv_d = din("cvv", [NLAYERS, 512, 1024])
    pv_d = din("pv", [128, NV])
    wmod_d = din("wmod", [NLAYERS, D, 6 * D])
    win_d = din("win", [NLAYERS, D, 5120])
    wout_d = din("wout", [NLAYERS, D, D])
    fup_d = din("fup", [NLAYERS, D, 2 * DFF])
    fdn_d = din("fdn", [NLAYERS, DFF, D])
    lw_d = din("lw", [NLAYERS, 128, 16, 128])
    bt_d = din("bt", [NLAYERS, NH, 8, 128, 640])
    id_d = din("ident", [128, 128])
    ys_d = dout("ys", [TS, D])
    yp_d = dout("yp", [TP, D])
    nk_d = dout("nk", [2, L, 256, 1024])
    nv_d = dout("nv", [2, L, 256, 1024])
    nl_d = dout("nl", [64, 128])
    X_d = nc.dram_tensor("Xscr", [128, 16, TT], F32).ap()

    st = ExitStack()
    with st:
        S = Sched(nc, st)

        def sb(name, shape, dt):
            return st.enter_context(nc.sbuf_tensor(name, list(shape), dt))

        pvt = sb("pvt", [128, NV], F32)
        ident_f = sb("ident_f", [128, 128], F32)
        ident_b = sb("ident_b", [128, 128], BF16)
        ones_d = sb("ones_d", [128, 128], BF16)
        ones_c = sb("ones_c", [128, 128], BF16)
        ones_1 = sb("ones_1", [128, 128], BF16)
        scond = sb("scond", [128, 32], BF16)
        modt = sb("modt", [128, L, 96, 2], F32)
        amod = sb("amod", [128, L, 2, 2, 16], F32)
        cneg = sb("cneg", [128, L * 8], F32)
        nl_t = sb("nl_t", [128, 64], F32)
        cst = sb("cst", [128, 2], F32)
        hb = sb("hb", [128, 16, TS], BF16)
        wbuf = [sb("wb%d" % i, [128, 16, 512], BF16) for i in range(3)]
        big = sb("big", [128, 22528], F32)
        lwt = sb("lwt", [128, 16, 128], BF16)
        misc = sb("misc", [128, 4224], F32)
        rstd = [misc[:, i * 512:(i + 1) * 512] for i in range(2)]
        tmpf = [misc[:, 1024 + i * 512:1024 + (i + 1) * 512] for i in range(4)]
        tmpb = [misc[:, 3072 + i * 256:3072 + (i + 1) * 256].bitcast(BF16) for i in range(4)]
        sbx_up = [misc[:, i * 1028:(i + 1) * 1028] for i in range(2)]
        sbx_cc = [misc[:, 2056 + i * 1024:2056 + (i + 1) * 1024] for i in range(2)]
        xio = [misc[:, i * 512:(i + 1) * 512] for i in range(4)]
        ps = [st.enter_context(nc.psum_tensor("ps%d" % i, [128, 512], F32)) for i in range(6)]
        psbT = [st.enter_context(nc.psum_tensor("psbT%d" % i, [128, 1024], BF16)) for i in range(2)]
        psb = [psbT[0][:, 0:512], psbT[1][:, 0:512]]

        def bview(off_f32, shape, dt):
            n = int(np.prod(shape[1:]))
            if dt == BF16:
                v = big[:, off_f32:off_f32 + (n + 1) // 2].bitcast(BF16)
                v = v[:, 0:n]
            else:
                v = big[:, off_f32:off_f32 + n]
            if len(shape) == 3:
                v = v.rearrange("p (a b) -> p a b", a=shape[1])
            elif len(shape) == 4:
                v = v.rearrange("p (a b c) -> p a b c", a=shape[1], b=shape[2])
            return v

        mix = bview(0, [128, 16, TS], BF16)
        act = bview(0, [128, 44, TS], BF16)
        xt16 = [bview(6144, [128, 16, 512], F32), bview(6144 + 8192, [128, 16, 512], F32)]
        SCR = 8192

        cnt = {"w": 0, "bank": 0, "tf": 0, "tb": 0, "xio": 0, "ev": 0}

        def pvc(name, idx):
            o = PV_OFF[name] + idx
            return pvt[:, o:o + 1]

        def load_panel(src_ap, nk, ncol):
            b = cnt["w"] % 3
            cnt["w"] += 1
            S.op("pool", lambda e: e.dma_start(out=wbuf[b][:, 0:nk, 0:ncol], in_=src_ap),
                 writes=[("wb", b)], dma=True, nobar=True)
            return b

        def wview(w2d, c0, ncol, k0=0, nk=16):
            return w2d.rearrange("(kc p) n -> p kc n", p=128)[:, k0:k0 + nk, c0:c0 + ncol]

        def next_bank(lo=0, n=4):
            b = lo + cnt["bank"] % n
            cnt["bank"] += 1
            return b

        def evac_eng():
            cnt["ev"] += 1
            return "act" if cnt["ev"] % 2 else "dve"

        def copy_op(eng, out, in_, reads, writes):
            if eng == "act":
                return S.op("act", lambda e: e.activation(out=out, in_=in_, func=AF.Copy), reads=reads, writes=writes)
            return S.op(eng, lambda e: e.tensor_copy(out=out, in_=in_), reads=reads, writes=writes)

        def mm_fm(b, nmc, src, srckey, Tg, evac, nk=16):
            for mc in range(nmc):
                for tt in range(Tg // 512):
                    bank = next_bank()
                    for kc in range(nk):
                        S.op("pe", lambda e, kc=kc, mc=mc, tt=tt, bank=bank: e.matmul(
                            ps[bank][:], wbuf[b][:, kc, mc * 128:(mc + 1) * 128],
                            src[:, kc, tt * 512:(tt + 1) * 512], start=(kc == 0), stop=(kc == nk - 1)),
                            reads=[("wb", b), (srckey, kc, tt)], writes=[("ps", bank)])
                    evac(mc, tt, bank)

        def mm_tm(b, src, srckey, Tg, evac):
            for ti in range(Tg // 128):
                bank = next_bank()
                for kc in range(16):
                    S.op("pe", lambda e, kc=kc, ti=ti, bank=bank: e.matmul(
                        ps[bank][:], src[:, kc, ti * 128:(ti + 1) * 128], wbuf[b][:, kc, :],
                        start=(kc == 0), stop=(kc == 15)),
                        reads=[("wb", b), (srckey, kc, ti // 4)], writes=[("ps", bank)])
                evac(ti, bank)

        S.op("sp", lambda e: e.dma_start(out=pvt[:], in_=pv_d), writes=["pv"], dma=True)
        S.op("sp", lambda e: e.dma_start(out=ident_f[:], in_=id_d), writes=["idf"], dma=True)
        S.op("pool", lambda e: e.dma_start(out=ident_b[:], in_=id_d), writes=["idb"], dma=True)
        S.op("dve", lambda e: e.memset(ones_d[:], 1.0 / D), writes=["ones_d"])
        S.op("dve", lambda e: e.memset(ones_c[:], 1.0 / 512), writes=["ones_c"])
        S.op("dve", lambda e: e.memset(ones_1[:], 1.0), writes=["ones_1"])
        S.op("dve", lambda e: e.memset(nl_t[:], 0.0), writes=["nl_t"])
        S.op("dve", lambda e: e.memset(cst[:, 0:1], EPS), writes=["cst"])
        S.op("dve", lambda e: e.memset(cst[:, 1:2], 1.0), writes=["cst"])
        oc = PV_OFF["cond"]
        S.op("act", lambda e: e.activation(out=scond[:], in_=pvt[:, oc:oc + 32], func=AF.Silu),
             reads=["pv"], writes=["scond"])
        ol = PV_OFF["lam"]
        S.op("act", lambda e: e.activation(out=cneg[:], in_=pvt[:, ol:ol + 32], func=AF.Exp, scale=-1.0),
             reads=["pv"], writes=["cneg"])
        S.op("act", lambda e: e.activation(out=cneg[:], in_=cneg[:], func=AF.Ln, bias=cst[:, 1:2]),
             reads=["cneg", "cst"], writes=["cneg"])
        S.op("dve", lambda e: e.tensor_scalar(cneg[:], cneg[:], -8.0, None, ALU.mult),
             reads=["cneg"], writes=["cneg"])

        def mod_finish(l):
            for ci in range(2):
                for wn, (sc_o, g_name) in enumerate(((16, "ln1g"), (64, "ln2g"))):
                    og = PV_OFF[g_name] + l * 16
                    S.op("dve", lambda e: e.scalar_tensor_tensor(
                        out=amod[:, l, ci, wn, :], in0=modt[:, l, sc_o:sc_o + 16, ci], scalar=1.0,
                        in1=pvt[:, og:og + 16], op0=ALU.add, op1=ALU.mult),
                        reads=[("modt", l), "pv"], writes=[("amod", l)])

        mod_state = {"l": 1, "p": 0}

        def mod_step(maxl=99):
            l2, p = mod_state["l"], mod_state["p"]
            if l2 >= NLAYERS or l2 > maxl:
                return
            b = load_panel(wview(wmod_d[l2], p * 512, 512), 16, 512)
            for m in range(4):
                for kc in range(16):
                    S.op("pe", lambda e: e.matmul(
                        ps[5][:, 384 + 2 * m:386 + 2 * m], wbuf[b][:, kc, m * 128:(m + 1) * 128],
                        scond[:, 2 * kc:2 * kc + 2], start=(kc == 0), stop=(kc == 15)),
                        reads=[("wb", b), "scond"], writes=[("ps", 5)])
            ob = PV_OFF["bmod"] + l2 * 96 + p * 4
            for ci in range(2):
                S.op("dve", lambda e: e.tensor_tensor(
                    out=modt[:, l2, p * 4:(p + 1) * 4, ci], in0=ps[5][:, 384 + ci:392:2], in1=pvt[:, ob:ob + 4],
                    op=ALU.add),
                    reads=[("ps", 5), "pv"], writes=[("modt", l2)])
            mod_state["p"] += 1
            if mod_state["p"] == 24:
                mod_finish(l2)
                mod_state["l"], mod_state["p"] = l2 + 1, 0

        for l in range(1):
            for p in range(24):
                b = load_panel(wview(wmod_d[l], p * 512, 512), 16, 512)
                for m in range(4):
                    c0 = (p * 4 + m) * 2
                    for kc in range(16):
                        S.op("pe", lambda e, b=b, m=m, kc=kc, c0=c0: e.matmul(
                            ps[5][:, c0:c0 + 2], wbuf[b][:, kc, m * 128:(m + 1) * 128],
                            scond[:, 2 * kc:2 * kc + 2], start=(kc == 0), stop=(kc == 15)),
                            reads=[("wb", b), "scond"], writes=[("ps", 5)])
            ob = PV_OFF["bmod"] + l * 96
            for ci in range(2):
                S.op("dve", lambda e, l=l, ci=ci, ob=ob: e.tensor_tensor(
                    out=modt[:, l, :, ci], in0=ps[5][:, ci:192:2], in1=pvt[:, ob:ob + 96], op=ALU.add),
                    reads=[("ps", 5), "pv"], writes=[("modt", l)])
            for ci in range(2):
                for wn, (sc_o, g_name) in enumerate(((16, "ln1g"), (64, "ln2g"))):
                    og = PV_OFF[g_name] + l * 16
                    S.op("dve", lambda e, l=l, ci=ci, wn=wn, sc_o=sc_o, og=og: e.scalar_tensor_tensor(
                        out=amod[:, l, ci, wn, :], in0=modt[:, l, sc_o:sc_o + 16, ci], scalar=1.0,
                        in1=pvt[:, og:og + 16], op0=ALU.add, op1=ALU.mult),
                        reads=[("modt", l), "pv"], writes=[("amod", l)])

        S.barrier()
        tin = [bview(0, [128, D], F32), bview(2048, [128, D], F32)]
        xst = [bview(4096, [128, 16, 128], F32), bview(6144, [128, 16, 128], F32)]
        for ti in range(TT // 128):
            src = xs_d[ti * 128:(ti + 1) * 128, :] if ti < 8 else xp_d[(ti - 8) * 128:(ti - 7) * 128, :]
            q = ti % 2
            S.op("sp", lambda e, q=q, src=src: e.dma_start(out=tin[q], in_=src), writes=[("tin", q)], dma=True)
            for cg in range(4):
                bank = next_bank()
                for j in range(4):
                    kc = cg * 4 + j
                    S.op("pe", lambda e, q=q, kc=kc, j=j, bank=bank: e.transpose(
                        ps[bank][:, j * 128:(j + 1) * 128], tin[q][:, kc * 128:(kc + 1) * 128], ident_f[:]),
                        reads=[("tin", q), "idf"], writes=[("ps", bank)])
                copy_op(evac_eng(), xst[q][:, cg * 4:cg * 4 + 4, :],
                        ps[bank][:].rearrange("p (a b) -> p a b", a=4), [("ps", bank)], [("xst", q, cg)])
            S.op("sp", lambda e, q=q, ti=ti: e.dma_start(out=X_d[:, :, ti * 128:(ti + 1) * 128], in_=xst[q]),
                 reads=[("xst", q, cg) for cg in range(4)], writes=[("X", ti // 4, c) for c in range(16)], dma=True)

        def norm(G, l, wn):
            Tg, xoff, ci = G["T"], G["xoff"], G["ci"]
            for tt in range(Tg // 512):
                q = tt % 2
                gt = (xoff + tt * 512) // 512
                S.op("sp", lambda e: e.dma_start(
                    out=xt16[q], in_=X_d[:, :, xoff + tt * 512: xoff + (tt + 1) * 512]),
                    reads=[("X", gt, c) for c in range(16)], writes=[("xt16", q)], dma=True)
                for kc in range(16):
                    tb = cnt["tb"] % 4
                    cnt["tb"] += 1
                    S.op("act", lambda e: e.activation(out=tmpb[tb], in_=xt16[q][:, kc, :], func=AF.Square),
                         reads=[("xt16", q)], writes=[("tmpb", tb)])
                    S.op("pe", lambda e: e.matmul(ps[4][:], ones_d[:], tmpb[tb], start=(kc == 0), stop=(kc == 15)),
                         reads=[("tmpb", tb), "ones_d"], writes=[("ps", 4)])
                S.op("act", lambda e: e.activation(out=rstd[q], in_=ps[4][:], func=AF.Sqrt, bias=cst[:, 0:1]),
                     reads=[("ps", 4), "cst"], writes=[("rstd", q)])
                S.op("dve", lambda e: e.reciprocal(rstd[q], rstd[q]), reads=[("rstd", q)], writes=[("rstd", q)])
                sh_o = 0 if wn == 0 else 48
                for kc in range(16):
                    tf = cnt["tf"] % 4
                    cnt["tf"] += 1
                    S.op("dve", lambda e: e.scalar_tensor_tensor(
                        out=tmpf[tf], in0=xt16[q][:, kc, :], scalar=amod[:, l, ci, wn, kc:kc + 1],
                        in1=rstd[q], op0=ALU.mult, op1=ALU.mult),
                        reads=[("xt16", q), ("rstd", q), ("amod", l)], writes=[("tmpf", tf)])
                    S.op("act", lambda e: e.activation(
                        out=hb[:, kc, tt * 512:(tt + 1) * 512], in_=tmpf[tf], func=AF.Identity,
                        bias=modt[:, l, sh_o + kc, ci:ci + 1]),
                        reads=[("tmpf", tf), ("modt", l)], writes=[("h", kc, tt)])

        finals = []

        def resid(G, l, chunk, tt, bank, g_o):
            Tg, xoff, ci = G["T"], G["xoff"], G["ci"]
            xi = cnt["xio"] % 4
            cnt["xio"] += 1
            gt = (xoff + tt * 512) // 512
            c0 = xoff + tt * 512
            S.op("sp", lambda e: e.dma_start(out=xio[xi], in_=X_d[:, chunk, c0:c0 + 512]),
                 reads=[("X", gt, chunk)], writes=[("xio", xi)], dma=True)
            S.op("dve", lambda e: e.scalar_tensor_tensor(
                out=xio[xi], in0=ps[bank][:], scalar=modt[:, l, g_o + chunk, ci:ci + 1], in1=xio[xi],
                op0=ALU.mult, op1=ALU.add),
                reads=[("ps", bank), ("xio", xi), ("modt", l)], writes=[("xio", xi)])
            S.op("sp", lambda e: e.dma_start(out=X_d[:, chunk, c0:c0 + 512], in_=xio[xi]),
                 reads=[("xio", xi)], writes=[("X", gt, chunk)], dma=True)

        def conv_module(G, l):
            Tg, seqs = G["T"], G["seqs"]
            ns, Ls = len(seqs), seqs[0][1]
            Lp = Ls + 30
            cva = bview(SCR, [128, 4, Tg], F32)
            sg = bview(SCR + 4 * Tg, [128, 4, Tg], F32)
            upad = bview(SCR + 8 * Tg, [128, 4, ns * Lp], BF16)
            dg = bview(SCR + 8 * Tg + 2 * ns * Lp + 8, [128, 31, 128], BF16)
            mus = bview(SCR + 8 * Tg + 2 * ns * Lp + 8 + 1984, [128, 512], F32)
            S.op("dve", lambda e: e.memset(upad, 0.0), writes=["upad"])
            b = load_panel(wview(win_d[l], 3072, 512), 16, 512)

            def ev_a(mc, tt, bank):
                copy_op("dve", cva[:, mc, tt * 512:(tt + 1) * 512], ps[bank][:], [("ps", bank)], [("cva", mc, tt)])
            mm_fm(b, 4, hb, "h", Tg, ev_a)
            b = load_panel(wview(win_d[l], 3584, 512), 16, 512)

            def ev_g(mc, tt, bank):
                S.op("act", lambda e: e.activation(out=sg[:, mc, tt * 512:(tt + 1) * 512], in_=ps[bank][:],
                                                   func=AF.Sigmoid),
                     reads=[("ps", bank)], writes=[("sg", mc, tt)])
            mm_fm(b, 4, hb, "h", Tg, ev_g)
            for c in range(4):
                for si, (so, sl) in enumerate(seqs):
                    S.op("dve", lambda e, c=c, si=si, so=so, sl=sl: e.tensor_tensor(
                        out=upad[:, c, si * Lp + 15: si * Lp + 15 + sl], in0=cva[:, c, so:so + sl],
                        in1=sg[:, c, so:so + sl], op=ALU.mult),
                        reads=[("cva", c, t) for t in range(Tg // 512)] + [("sg", c, t) for t in range(Tg // 512)]
                        + ["upad"], writes=[("upc", c)])
            for c in range(4):
                for k in range(31):
                    ow = PV_OFF["cvw"] + (l * 4 + c) * 31 + k
                    S.op("dve", lambda e, k=k, ow=ow: e.tensor_scalar(
                        dg[:, k, :], ident_b[:], pvt[:, ow:ow + 1], None, ALU.mult),
                        reads=["idb", "pv"], writes=[("dg", k)])
                obb = PV_OFF["cvb"] + l * 4 + c
                for si, (so, sl) in enumerate(seqs):
                    for t0 in range(0, sl, 512):
                        n = min(512, sl - t0)
                        bank = next_bank()
                        for ki, k in enumerate([15] + [kk for kk in range(31) if kk != 15]):
                            S.op("pe", lambda e, c=c, k=k, ki=ki, si=si, t0=t0, n=n, bank=bank: e.matmul(
                                ps[bank][:, 0:n], dg[:, k, :], upad[:, c, si * Lp + t0 + k: si * Lp + t0 + k + n],
                                start=(ki == 0), stop=(ki == 30)),
                                reads=[("dg", k), ("upc", c)], writes=[("ps", bank)])
                        S.op("act", lambda e, c=c, so=so, t0=t0, n=n, bank=bank, obb=obb: e.activation(
                            out=cva[:, c, so + t0: so + t0 + n], in_=ps[bank][:, 0:n], func=AF.Identity,
                            bias=pvt[:, obb:obb + 1]),
                            reads=[("ps", bank), "pv"], writes=[("cc", c, (so + t0) // 512)])
            for tt in range(Tg // 512):
                sl_ = slice(tt * 512, (tt + 1) * 512)
                for c in range(4):
                    tb = (cnt["tb"] // 2 * 2) % 4
                    cnt["tb"] = cnt["tb"] // 2 * 2 + 2
                    S.op("act", lambda e, c=c, tb=tb: e.activation(out=tmpb[tb], in_=cva[:, c, sl_], func=AF.Square),
                         reads=[("cc", c, tt)], writes=[("tmpb", tb)])
                    S.op("dve", lambda e, c=c, tb=tb: e.tensor_copy(out=tmpb[tb + 1], in_=cva[:, c, sl_]),
                         reads=[("cc", c, tt)], writes=[("tmpb", tb + 1)])
                    S.op("pe", lambda e, c=c, tb=tb: e.matmul(ps[4][:], ones_c[:], tmpb[tb + 1],
                                                              start=(c == 0), stop=(c == 3)),
                         reads=[("tmpb", tb + 1), "ones_c"], writes=[("ps", 4)])
                    S.op("pe", lambda e, c=c, tb=tb: e.matmul(ps[5][:], ones_c[:], tmpb[tb],
                                                              start=(c == 0), stop=(c == 3)),
                         reads=[("tmpb", tb), "ones_c"], writes=[("ps", 5)])
                q = tt % 2
                S.op("act", lambda e: e.activation(out=mus, in_=ps[4][:], func=AF.Copy),
                     reads=[("ps", 4)], writes=["mus"])
                S.op("dve", lambda e, q=q: e.tensor_tensor(out=rstd[q], in0=mus, in1=mus, op=ALU.mult),
                     reads=["mus"], writes=[("rstd", q)])
                S.op("dve", lambda e, q=q: e.tensor_tensor(out=rstd[q], in0=ps[5][:], in1=rstd[q],
                                                           op=ALU.subtract),
                     reads=[("ps", 5), ("rstd", q)], writes=[("rstd", q)])
                S.op("act", lambda e: e.activation(out=rstd[q], in_=rstd[q], func=AF.Sqrt, bias=cst[:, 0:1]),
                     reads=[("rstd", q), "cst"], writes=[("rstd", q)])
                S.op("dve", lambda e: e.reciprocal(rstd[q], rstd[q]), reads=[("rstd", q)], writes=[("rstd", q)])
                for c in range(4):
                    tf = cnt["tf"] % 4
                    cnt["tf"] += 1
                    S.op("dve", lambda e, c=c, tf=tf: e.tensor_tensor(out=tmpf[tf], in0=cva[:, c, sl_], in1=mus,
                                                                    op=ALU.subtract),
                         reads=[("cc", c, tt), "mus"], writes=[("tmpf", tf)])
                    S.op("dve", lambda e, q=q, tf=tf: e.tensor_tensor(out=tmpf[tf], in0=tmpf[tf], in1=rstd[q],
                                                                    op=ALU.mult),
                         reads=[("tmpf", tf), ("rstd", q)], writes=[("tmpf", tf)])
                    og_ = PV_OFF["cvg"] + l * 4 + c
                    ob_ = PV_OFF["cvlb"] + l * 4 + c
                    S.op("act", lambda e, c=c, tf=tf, og_=og_, ob_=ob_: e.activation(
                        out=mix[:, 8 + c, sl_], in_=tmpf[tf], func=AF.Silu,
                        scale=pvt[:, og_:og_ + 1], bias=pvt[:, ob_:ob_ + 1]),
                        reads=[("tmpf", tf), "pv"], writes=[("mix", 8 + c, tt)])

        def lru(G, l):
            Tg, seqs = G["T"], G["seqs"]
            ns, Ls = len(seqs), seqs[0][1]
            Lp = Ls + 4
            lxp = bview(SCR, [128, 4, ns * Lp], F32)
            lgf = bview(SCR + 4 * ns * Lp, [128, 4, Tg], F32)
            o = SCR + 4 * ns * Lp + 4 * Tg
            xc = bview(o, [128, Tg], F32)
            xcb = bview(o + Tg, [128, Tg], BF16)
            o2 = o + Tg + Tg // 2
            r_ = bview(o2, [128, Tg], F32)
            i_ = bview(o2 + Tg, [128, Tg], F32)
            hd = [bview(o2 + 2 * Tg, [128, Tg], F32), bview(o2 + 3 * Tg, [128, Tg], F32)]
            S.op("dve", lambda e: e.memset(lxp, 0.0), writes=["lxp"])
            S.op("pool", lambda e: e.dma_start(out=lwt[:], in_=lw_d[l]), writes=["lwt"], dma=True)
            b = load_panel(wview(win_d[l], 4096, 512), 16, 512)

            def ev_x(mc, tt, bank):
                for si, (so, sl) in enumerate(seqs):
                    lo, hi = max(so, tt * 512), min(so + sl, (tt + 1) * 512)
                    if lo >= hi:
                        continue
                    copy_op("dve", lxp[:, mc, si * Lp + 2 + lo - so: si * Lp + 2 + hi - so],
                            ps[bank][:, lo - tt * 512: hi - tt * 512], [("ps", bank), "lxp"], [("lx", mc, tt)])
            mm_fm(b, 4, hb, "h", Tg, ev_x)
            b = load_panel(wview(win_d[l], 4608, 512), 16, 512)

            def ev_g(mc, tt, bank):
                copy_op("act", lgf[:, mc, tt * 512:(tt + 1) * 512], ps[bank][:], [("ps", bank)], [("lg", mc, tt)])
            mm_fm(b, 4, hb, "h", Tg, ev_g)
            ntt = Tg // 512
            for c in range(4):
                for _ in range(3):
                    mod_step(l + 1)
                lxk = [("lx", c, t) for t in range(ntt)]
                for si, (so, sl) in enumerate(seqs):
                    for k in range(4):
                        ow = PV_OFF["lcw"] + (l * 4 + c) * 4 + k
                        src = lxp[:, c, si * Lp + k: si * Lp + k + sl]
                        if k == 0:
                            obb = PV_OFF["lcb"] + l * 4 + c
                            S.op("dve", lambda e, src=src, so=so, sl=sl, ow=ow, obb=obb: e.tensor_scalar(
                                xc[:, so:so + sl], src, pvt[:, ow:ow + 1], pvt[:, obb:obb + 1], ALU.mult, ALU.add),
                                reads=lxk + ["pv"], writes=[("xc", si)])
                        else:
                            S.op("dve", lambda e, src=src, so=so, sl=sl, ow=ow: e.scalar_tensor_tensor(
                                out=xc[:, so:so + sl], in0=src, scalar=pvt[:, ow:ow + 1], in1=xc[:, so:so + sl],
                                op0=ALU.mult, op1=ALU.add),
                                reads=lxk + ["pv", ("xc", si)], writes=[("xc", si)])
                xck = [("xc", si) for si in range(ns)]
                S.op("act", lambda e: e.activation(out=xcb, in_=xc, func=AF.Copy), reads=xck, writes=["xcb"])
                for d in range(2):
                    for gi, (dst, bname) in enumerate(((r_, "lba"), (i_, "lbi"))):
                        obb = PV_OFF[bname] + (l * 2 + d) * 4 + c
                        for tt in range(ntt):
                            bank = next_bank()
                            wi = (d * 2 + gi) * 4 + c
                            S.op("pe", lambda e, wi=wi, tt=tt, bank=bank: e.matmul(
                                ps[bank][:], lwt[:, wi, :], xcb[:, tt * 512:(tt + 1) * 512], start=True, stop=True),
                                reads=["lwt", "xcb"], writes=[("ps", bank)])
                            S.op("act", lambda e, dst=dst, tt=tt, bank=bank, obb=obb: e.activation(
                                out=dst[:, tt * 512:(tt + 1) * 512], in_=ps[bank][:], func=AF.Sigmoid,
                                bias=pvt[:, obb:obb + 1]),
                                reads=[("ps", bank), "pv"], writes=[("gate", gi)])
                    oc_ = (l * 2 + d) * 4 + c
                    S.op("act", lambda e, oc_=oc_: e.activation(out=r_, in_=r_, func=AF.Exp, scale=cneg[:, oc_:oc_ + 1]),
                         reads=[("gate", 0), "cneg"], writes=[("gate", 0)])
                    tfs = misc[:, 0:Tg]
                    S.op("dve", lambda e: e.tensor_tensor(out=tfs, in0=r_, in1=r_, op=ALU.mult),
                         reads=[("gate", 0)], writes=["tfs"])
                    S.op("act", lambda e: e.activation(out=tfs, in_=tfs, func=AF.Sqrt, scale=-1.0, bias=cst[:, 1:2]),
                         reads=["tfs", "cst"], writes=["tfs"])
                    S.op("dve", lambda e: e.tensor_tensor(out=i_, in0=i_, in1=tfs, op=ALU.mult),
                         reads=[("gate", 1), "tfs"], writes=[("gate", 1)])
                    S.op("dve", lambda e: e.tensor_tensor(out=i_, in0=i_, in1=xc, op=ALU.mult),
                         reads=[("gate", 1)] + xck, writes=[("gate", 1)])
                    for si, (so, sl) in enumerate(seqs):
                        if G["h0"]:
                            oh = PV_OFF["h0"] + (l * 2 + d) * 4 + c
                            init = pvt[:, oh:oh + 1]
                        else:
                            init = 0.0
                        if d == 0:
                            oa, a0, a1 = hd[0][:, so:so + sl], r_[:, so:so + sl], i_[:, so:so + sl]
                        else:
                            oa = hd[1][:, so + sl - 1: so - 1 if so > 0 else None: -1]
                            a0 = r_[:, so + sl - 1: so - 1 if so > 0 else None: -1]
                            a1 = i_[:, so + sl - 1: so - 1 if so > 0 else None: -1]
                        S.op("dve", lambda e, oa=oa, a0=a0, a1=a1, init=init: e.tensor_tensor_scan(
                            out=oa, data0=a0, data1=a1, initial=init, op0=ALU.mult, op1=ALU.add),
                            reads=[("gate", 0), ("gate", 1), "pv"], writes=[("hd", d)])
                        if G["nl"]:
                            col = ((G["nl_b"] + si) * L + l) * 2 * 4 + d * 4 + c
                            tcol = so + sl - 1 if d == 0 else so
                            S.op("act", lambda e, d=d, col=col, tcol=tcol: e.activation(
                                out=nl_t[:, col:col + 1], in_=hd[d][:, tcol:tcol + 1], func=AF.Copy),
                                reads=[("hd", d)], writes=["nl_t"])
                lgk = [("lg", c, t) for t in range(ntt)]
                lgc = lgf[:, c, :]
                S.op("dve", lambda e: e.tensor_tensor(out=hd[0], in0=hd[0], in1=hd[1], op=ALU.add),
                     reads=[("hd", 0), ("hd", 1)], writes=[("hd", 0)])
                S.op("dve", lambda e, lgc=lgc: e.tensor_tensor(out=r_, in0=lgc, in1=lgc, op=ALU.mult),
                     reads=lgk + [("gate", 0)], writes=[("gate", 0)])
                S.op("dve", lambda e: e.tensor_scalar(r_, r_, 0.044715, 1.0, ALU.mult, ALU.add),
                     reads=[("gate", 0)], writes=[("gate", 0)])
                S.op("dve", lambda e, lgc=lgc: e.tensor_tensor(out=r_, in0=r_, in1=lgc, op=ALU.mult),
                     reads=lgk + [("gate", 0)], writes=[("gate", 0)])
                S.op("act", lambda e: e.activation(out=r_, in_=r_, func=AF.Sigmoid, scale=1.5957691216057308),
                     reads=[("gate", 0)], writes=[("gate", 0)])
                S.op("dve", lambda e, lgc=lgc: e.tensor_tensor(out=r_, in0=r_, in1=lgc, op=ALU.mult),
                     reads=lgk + [("gate", 0)], writes=[("gate", 0)])
                S.op("dve", lambda e, c=c: e.tensor_tensor(out=mix[:, 12 + c, 0:Tg], in0=r_, in1=hd[0], op=ALU.mult),
                     reads=[("gate", 0), ("hd", 0)], writes=[("mix", 12 + c, t) for t in range(ntt)])

        def attention(G, l):
            Tg, seqs, sample = G["T"], G["seqs"], G["sample"]
            astop = (ATT_STOP if ATT_STOP < 10 else 0) if sample else (ATT_STOP - 10 if ATT_STOP >= 10 else 0)
            q4 = bview(SCR, [128, 4, Tg], BF16)
            k4 = bview(SCR + 2 * Tg, [128, 4, Tg], BF16)
            v4 = bview(SCR + 4 * Tg, [128, Tg // 128, 512], BF16)
            o = SCR + 6 * Tg
            kctx = bview(o, [128, 4, 512], BF16)
            vctx = bview(o + 1024, [128, 4, 512], BF16)
            cin = misc[:, 0:1024].bitcast(BF16).rearrange("p (a b) -> p a b", a=4)
            o += 2048
            sc = [bview(o, [128, 1152], F32), bview(o + 1152, [128, 1152], F32)]
            o += 2304
            pb = [bview(o, [128, 1152], BF16), bview(o + 576, [128, 1152], BF16)]
            o += 1152
            pt = [bview(o, [128, 9, 128], BF16), bview(o + 576, [128, 9, 128], BF16)]
            o += 1152
            btl = [misc[:, 1024:1664], misc[:, 1664:2304]]
            mx = bview(o, [128, 8], F32)
            rb = [bview(o + 8, [128, 128], F32), bview(o + 136, [128, 128], F32)]
            kvo = [misc[:, 2304:2816], misc[:, 2816:3328]]
            assert o + 264 <= 22528, o
            it = 0
            for hg in range(2):
                S.barrier()
                b = load_panel(wview(win_d[l], hg * 512, 512), 16, 512)

                def ev_q(mc, tt, bank):
                    copy_op(evac_eng(), q4[:, mc, tt * 512:(tt + 1) * 512], ps[bank][:], [("ps", bank)], [("q4", mc, tt)])
                mm_fm(b, 4, hb, "h", Tg, ev_q)
                if astop == 5:
                    raise _Stop()
                b = load_panel(wview(win_d[l], 1024 + hg * 512, 512), 16, 512)

                def ev_k(mc, tt, bank):
                    copy_op(evac_eng(), k4[:, mc, tt * 512:(tt + 1) * 512], ps[bank][:], [("ps", bank)], [("k4", mc, tt)])
                mm_fm(b, 4, hb, "h", Tg, ev_k)
                if astop == 6:
                    raise _Stop()
                if not sample:
                    def ev_ktm(ti, bank):
                        kq = ti % 2
                        copy_op(evac_eng(), kvo[kq], ps[bank][:], [("ps", bank)], [("kvo", kq)])
                        bi, t0 = ti // 2, (ti % 2) * 128
                        if SKIP_KV_OUT:
                            return
                        finals.append(S.op("sp", lambda e: e.dma_start(
                            out=nk_d[bi, l, t0:t0 + 128, hg * 512:(hg + 1) * 512], in_=kvo[kq]),
                            reads=[("kvo", kq)], dma=True))
                    mm_tm(b, hb, "h", Tg, ev_ktm)
                    if astop == 7:
                        raise _Stop()
                b = load_panel(wview(win_d[l], 2048 + hg * 512, 512), 16, 512)

                def ev_v(ti, bank):
                    if sample:
                        copy_op("act", v4[:, ti, :], ps[bank][:], [("ps", bank)], [("v4", ti)])
                        return
                    kq = ti % 2
                    copy_op("act", kvo[kq], ps[bank][:], [("ps", bank)], [("kvo", kq)])
                    copy_op("dve", v4[:, ti, :], kvo[kq], [("kvo", kq)], [("v4", ti)])
                    bi, t0 = ti // 2, (ti % 2) * 128
                    if SKIP_KV_OUT:
                        return
                    finals.append(S.op("sp", lambda e: e.dma_start(
                        out=nv_d[bi, l, t0:t0 + 128, hg * 512:(hg + 1) * 512], in_=kvo[kq]),
                        reads=[("kvo", kq)], dma=True))
                mm_tm(b, hb, "h", Tg, ev_v)
                if astop == 1:
                    raise _Stop()
                if sample:
                    ckv = ck_d[l].rearrange("(a p) f -> p a f", p=128)[:, :, hg * 512:(hg + 1) * 512]
                    cvw_ = cv_d[l].rearrange("(a p) f -> p a f", p=128)[:, :, hg * 512:(hg + 1) * 512]
                    S.op("pool", lambda e: e.dma_start(out=cin, in_=ckv), writes=["cin"], dma=True)
                    S.op("pool", lambda e: e.dma_start(out=vctx, in_=cvw_), writes=["vctx"], dma=True)
                    for hh in range(4):
                        for a in range(4):
                            S.op("pe", lambda e, hh=hh, a=a: e.transpose(
                                psb[hh % 2][:, a * 128:(a + 1) * 128], cin[:, a, hh * 128:(hh + 1) * 128], ident_b[:]),
                                reads=["cin", "idb"], writes=[("psb", hh % 2)])
                        copy_op(evac_eng(), kctx[:, hh, :], psb[hh % 2][:, 0:512], [("psb", hh % 2)], [("kctx", hh)])
                if astop == 2:
                    raise _Stop()
                def stage_a(hh, so, q0, pi, u, itn):
                    h = hg * 4 + hh
                    if sample:
                        a_row, nw = _pair_win(pi)
                        k0, nwk = a_row * 64, nw * 64
                        nk_all = nwk + 512
                        S.op("sp", lambda e: e.dma_start(
                            out=btl[u][:, 0:nwk], in_=bt_d[l, h, pi, :, 0:nwk]), writes=[("btl", u)], dma=True)
                    else:
                        k0, nwk = so, 256
                        nk_all = 256
                    bW, bC, bM = (0, 1, 2) if u == 0 else (3, 4, 5)
                    segs = [(k0, min(nwk, 512), bW, 0, 0)]
                    if nwk > 512:
                        segs.append((k0 + 512, nwk - 512, bM, 512, 0))
                    kkeys = [("k4", hh, t) for t in range(Tg // 512)]
                    for (ks, kn, bank, off, pc) in segs:
                        S.op("pe", lambda e: e.matmul(
                            ps[bank][:, pc:pc + kn], q4[:, hh, q0:q0 + 128], k4[:, hh, ks:ks + kn],
                            start=True, stop=True),
                            reads=[("q4", hh, q0 // 512)] + kkeys, writes=[("ps", bank)])
                        if sample:
                            S.op("dve", lambda e: e.scalar_tensor_tensor(
                                out=sc[u][:, off:off + kn], in0=ps[bank][:, pc:pc + kn], scalar=ATT_SCALE,
                                in1=btl[u][:, off:off + kn], op0=ALU.mult, op1=ALU.add),
                                reads=[("ps", bank), ("btl", u)], writes=[("sc", u)])
                        else:
                            S.op("dve", lambda e: e.tensor_scalar(
                                sc[u][:, off:off + kn], ps[bank][:, pc:pc + kn], ATT_SCALE, None, ALU.mult),
                                reads=[("ps", bank)], writes=[("sc", u)])
                    if sample:
                        S.op("pe", lambda e: e.matmul(
                            ps[bC][:], q4[:, hh, q0:q0 + 128], kctx[:, hh, :], start=True, stop=True),
                            reads=[("q4", hh, q0 // 512), ("kctx", hh)], writes=[("ps", bC)])
                        S.op("act", lambda e: e.activation(
                            out=sc[u][:, nwk:nwk + 512], in_=ps[bC][:], func=AF.Copy, scale=ATT_SCALE),
                            reads=[("ps", bC)], writes=[("sc", u)])
                    mcol = itn % 8
                    S.op("dve", lambda e: e.tensor_reduce(
                        out=mx[:, mcol:mcol + 1], in_=sc[u][:, 0:nk_all], axis=mybir.AxisListType.X, op=ALU.max,
                        negate=True),
                        reads=[("sc", u)], writes=[("mx", mcol)])
                    S.op("act", lambda e: e.activation(
                        out=pb[u][:, 0:nk_all], in_=sc[u][:, 0:nk_all], func=AF.Exp, bias=mx[:, mcol:mcol + 1]),
                        reads=[("sc", u), ("mx", mcol)], writes=[("pb", u)])
                    return (hh, h, so, q0, u, k0, nwk, nk_all, bM)

                def stage_b(ctx):
                    hh, h, so, q0, u, k0, nwk, nk_all, bM = ctx
                    nkt = nk_all // 128
                    for g0 in range(0, nkt, 4):
                        gn = min(4, nkt - g0)
                        pbk = (g0 // 4) % 2
                        for j in range(gn):
                            S.op("pe", lambda e: e.transpose(
                                psb[pbk][:, j * 128:(j + 1) * 128], pb[u][:, (g0 + j) * 128:(g0 + j + 1) * 128],
                                ident_b[:]),
                                reads=[("pb", u), "idb"], writes=[("psb", pbk)])
                        copy_op(evac_eng(), pt[u][:, g0:g0 + gn, :],
                                psb[pbk][:, 0:gn * 128].rearrange("p (a b) -> p a b", a=gn),
                                [("psb", pbk)], [("pt", u, g0 // 4)])
                    ptk = [("pt", u, g) for g in range((nkt + 3) // 4)]
                    for j in range(nkt):
                        S.op("pe", lambda e: e.matmul(
                            ps[bM][:, 128:256], ones_1[:], pt[u][:, j, :], start=(j == 0), stop=(j == nkt - 1)),
                            reads=ptk + ["ones_1"], writes=[("ps", bM)])
                    for j in range(nkt):
                        if sample:
                            if j < nwk // 128:
                                vsrc = v4[:, k0 // 128 + j, hh * 128:(hh + 1) * 128]
                                vk = ("v4", k0 // 128 + j)
                            else:
                                vsrc = vctx[:, j - nwk // 128, hh * 128:(hh + 1) * 128]
                                vk = "vctx"
                        else:
                            vsrc = v4[:, so // 128 + j, hh * 128:(hh + 1) * 128]
                            vk = ("v4", so // 128 + j)
                        S.op("pe", lambda e: e.matmul(
                            ps[bM][:, 256:384], vsrc, pt[u][:, j, :], start=(j == 0), stop=(j == nkt - 1)),
                            reads=ptk + [vk], writes=[("ps", bM)])
                    S.op("dve", lambda e: e.reciprocal(rb[u], ps[bM][:, 128:256]),
                         reads=[("ps", bM)], writes=[("rb", u)])
                    S.op("dve", lambda e: e.tensor_tensor(
                        out=mix[:, h, q0:q0 + 128], in0=ps[bM][:, 256:384], in1=rb[u], op=ALU.mult),
                        reads=[("ps", bM), ("rb", u)], writes=[("mixq", h, q0)])

                tl = []
                for hh in range(4):
                    if sample:
                        tl += [(hh, 0, i * 128, i) for i in range(8)]
                    else:
                        tl += [(hh, so, so + t0, None) for (so, sl) in seqs for t0 in range(0, sl, 128)]
                prev = None
                for (hh, so, q0, pi) in tl:
                    ctx = stage_a(hh, so, q0, pi, it % 2, it)
                    it += 1
                    if prev is not None:
                        stage_b(prev)
                    prev = ctx
                stage_b(prev)
            for h in range(8):
                for tt in range(Tg // 512):
                    S.op("dve", lambda e, h=h, tt=tt: e.tensor_copy(out=mx[:, 0:1], in_=mx[:, 0:1]),
                         reads=[("mixq", h, q0) for q0 in range(tt * 512, (tt + 1) * 512, 128)] + [("mx", 0)],
                         writes=[("mix", h, tt), ("mx", 0)])

        def w_out(G, l):
            Tg = G["T"]
            for p in range(4):
                b = load_panel(wview(wout_d[l], p * 512, 512), 16, 512)

                def ev(mc, tt, bank, p=p):
                    resid(G, l, p * 4 + mc, tt, bank, 32)
                mm_fm(b, 4, mix, "mix", Tg, ev)

        def ffn(G, l):
            Tg, seqs = G["T"], G["seqs"]
            ns, Ls = len(seqs), seqs[0][1]
            Lp = Ls + 2
            upd = [sbx_up[0], sbx_up[1]]
            cc = [sbx_cc[0], sbx_cc[1]]
            for uq in range(2):
                S.op("dve", lambda e, uq=uq: e.memset(upd[uq], 0.0), writes=[("upd", uq)])
            jn = 0
            for p in range(22):
                if G["sample"]:
                    mod_step(l + 1)
                b = load_panel(wview(fup_d[l], p * 512, 512), 16, 512)
                for mc in range(4):
                    j = p * 4 + mc
                    ja = j % 44
                    uq = jn % 2
                    jn += 1
                    owc = PV_OFF["fcw"] + (l * 88 + j) * 3
                    obc = PV_OFF["fcb"] + l * 88 + j
                    banks = []
                    for tt in range(Tg // 512):
                        bank = next_bank(0, 5)
                        banks.append(bank)
                        for kc in range(16):
                            S.op("pe", lambda e, kc=kc, mc=mc, tt=tt, bank=bank, b=b: e.matmul(
                                ps[bank][:], wbuf[b][:, kc, mc * 128:(mc + 1) * 128],
                                hb[:, kc, tt * 512:(tt + 1) * 512], start=(kc == 0), stop=(kc == 15)),
                                reads=[("wb", b), ("h", kc, tt)], writes=[("ps", bank)])
                        for si, (so, sl) in enumerate(seqs):
                            lo, hi = max(so, tt * 512), min(so + sl, (tt + 1) * 512)
                            if lo >= hi:
                                continue
                            S.op("act", lambda e, uq=uq, si=si, so=so, lo=lo, hi=hi, tt=tt, bank=bank: e.activation(
                                out=upd[uq][:, si * Lp + 1 + lo - so: si * Lp + 1 + hi - so],
                                in_=ps[bank][:, lo - tt * 512: hi - tt * 512], func=AF.Copy),
                                reads=[("ps", bank), ("upd", uq)], writes=[("updw", uq, tt)])
                        S.op("act", lambda e, uq=uq, tt=tt, bank=bank, owc=owc, obc=obc: e.activation(
                            out=cc[uq][:, tt * 512:(tt + 1) * 512], in_=ps[bank][:], func=AF.Identity,
                            scale=pvt[:, owc + 1:owc + 2], bias=pvt[:, obc:obc + 1]),
                            reads=[("ps", bank), "pv"], writes=[("cc", uq, tt)])
                    ntt = Tg // 512
                    updk = [("updw", uq, t) for t in range(ntt)]
                    cck = [("cc", uq, t) for t in range(ntt)]
                    for si, (so, sl) in enumerate(seqs):
                        for k in (0, 2):
                            S.op("dve", lambda e, uq=uq, si=si, so=so, sl=sl, k=k, owc=owc: e.scalar_tensor_tensor(
                                out=cc[uq][:, so:so + sl], in0=upd[uq][:, si * Lp + k: si * Lp + k + sl],
                                scalar=pvt[:, owc + k:owc + k + 1], in1=cc[uq][:, so:so + sl],
                                op0=ALU.mult, op1=ALU.add),
                                reads=updk + cck + ["pv"], writes=[("ccf", uq)])
                    if j < 44:
                        S.op("act", lambda e, uq=uq, ja=ja: e.activation(out=act[:, ja, 0:Tg], in_=cc[uq][:, 0:Tg],
                                                                        func=AF.Silu),
                             reads=[("ccf", uq)] + cck, writes=[("act", ja)])
                    else:
                        S.op("dve", lambda e, uq=uq, ja=ja: e.tensor_tensor(
                            out=act[:, ja, 0:Tg], in0=act[:, ja, 0:Tg], in1=cc[uq][:, 0:Tg], op=ALU.mult),
                            reads=[("ccf", uq), ("act", ja)] + cck, writes=[("act", ja)])
            S.barrier()
            ntt = Tg // 512
            for mp in range(8):
                base = 0
                for ks, (k0, nk) in enumerate(((0, 16), (16, 16), (32, 12))):
                    if G["sample"] and ks == 0 and mp < 2:
                        mod_step(l + 1)
                    b = load_panel(wview(fdn_d[l], mp * 256, 256, k0, nk), nk, 256)
                    for oc_ in range(2):
                        for tt in range(ntt):
                            bank = base + oc_ * 2 + tt
                            for kc in range(nk):
                                S.op("pe", lambda e, kc=kc, k0=k0, oc_=oc_, tt=tt, bank=bank, b=b, ks=ks, nk=nk: e.matmul(
                                    ps[bank][:], wbuf[b][:, kc, oc_ * 128:(oc_ + 1) * 128],
                                    act[:, k0 + kc, tt * 512:(tt + 1) * 512],
                                    start=(ks == 0 and kc == 0), stop=(ks == 2 and kc == nk - 1)),
                                    reads=[("wb", b), ("act", k0 + kc)], writes=[("ps", bank)])
                for oc_ in range(2):
                    for tt in range(ntt):
                        resid(G, l, mp * 2 + oc_, tt, base + oc_ * 2 + tt, 80)

        def final_norm(G):
            ystage = [wbuf[i][:].rearrange("p a b -> p (a b)").bitcast(F32).rearrange("p (a f) -> p a f", a=4)
                      for i in range(2)]
            Tg, xoff, ydst = G["T"], G["xoff"], G["ydst"]
            for tt in range(Tg // 512):
                q = tt % 2
                gt = (xoff + tt * 512) // 512
                S.op("sp", lambda e: e.dma_start(
                    out=xt16[q], in_=X_d[:, :, xoff + tt * 512: xoff + (tt + 1) * 512]),
                    reads=[("X", gt, c) for c in range(16)], writes=[("xt16", q)], dma=True)
                for kc in range(16):
                    tb = cnt["tb"] % 4
                    cnt["tb"] += 1
                    S.op("act", lambda e: e.activation(out=tmpb[tb], in_=xt16[q][:, kc, :], func=AF.Square),
                         reads=[("xt16", q)], writes=[("tmpb", tb)])
                    S.op("pe", lambda e: e.matmul(ps[4][:], ones_d[:], tmpb[tb], start=(kc == 0), stop=(kc == 15)),
                         reads=[("tmpb", tb), "ones_d"], writes=[("ps", 4)])
                S.op("act", lambda e: e.activation(out=rstd[q], in_=ps[4][:], func=AF.Sqrt, bias=cst[:, 0:1]),
                     reads=[("ps", 4), "cst"], writes=[("rstd", q)])
                S.op("dve", lambda e: e.reciprocal(rstd[q], rstd[q]), reads=[("rstd", q)], writes=[("rstd", q)])
                for half in range(2):
                    for kk in range(8):
                        kc = half * 8 + kk
                        tf = cnt["tf"] % 4
                        cnt["tf"] += 1
                        og = PV_OFF["fing"] + kc
                        S.op("dve", lambda e: e.scalar_tensor_tensor(
                            out=tmpf[tf], in0=xt16[q][:, kc, :], scalar=pvt[:, og:og + 1], in1=rstd[q],
                            op0=ALU.mult, op1=ALU.mult),
                            reads=[("xt16", q), ("rstd", q), "pv"], writes=[("tmpf", tf)])
                        bank = kc % 4
                        for j in range(4):
                            S.op("pe", lambda e: e.transpose(
                                ps[bank][:, j * 128:(j + 1) * 128], tmpf[tf][:, j * 128:(j + 1) * 128], ident_f[:]),
                                reads=[("tmpf", tf), "idf"], writes=[("ps", bank)])
                        copy_op("act" if kc % 2 else "dve", ystage[half][:, :, kk * 128:(kk + 1) * 128],
                                ps[bank][:].rearrange("p (a b) -> p a b", a=4), [("ps", bank)], [("wb", half)])
                    finals.append(S.op("sp", lambda e: e.dma_start(
                        out=ydst[tt * 512:(tt + 1) * 512, half * 1024:(half + 1) * 1024].rearrange(
                            "(a p) f -> p a f", p=128), in_=ystage[half]),
                        reads=[("wb", half)], dma=True))

        GS = dict(T=TS, xoff=0, ci=1, seqs=[(0, TS)], sample=True, h0=True, nl=False, ydst=ys_d)
        GP = dict(T=TP, xoff=TS, ci=0, seqs=[(0, 256), (256, 256)], sample=False, h0=False, nl=True, nl_b=0,
                  ydst=yp_d)
        def chk(n):
            if STAGE == n:
                raise _Stop()

        _chk0 = chk
        try:
            chk(1)
            for l in range(NLAYERS):
                for gi_, G in enumerate((GS, GP)):
                    _chk = chk
                    chk = (lambda n, gi_=gi_, _c=_chk0: _c(n + 10 * gi_))
                    S.barrier()
                    norm(G, l, 0)
                    chk(2)
                    S.barrier()
                    conv_module(G, l)
                    chk(3)
                    S.barrier()
                    lru(G, l)
                    chk(4)
                    attention(G, l)
                    chk(5)
                    S.barrier()
                    w_out(G, l)
                    chk(6)
                    S.barrier()
                    norm(G, l, 1)
                    S.barrier()
                    ffn(G, l)
                    chk(7)
                while mod_state["l"] == l + 1 and mod_state["l"] < NLAYERS:
                    mod_step()
            _chk0(20)
            S.barrier()
            final_norm(GS)
            final_norm(GP)
            _chk0(21)
            S.barrier()
            S.op("pe", lambda e: e.transpose(ps[0][0:64, 0:128], nl_t[:, 0:64], ident_f[:]),
                 reads=["nl_t", "idf"], writes=[("ps", 0)])
            S.op("dve", lambda e: e.tensor_copy(out=tmpf[0][0:64, 0:128], in_=ps[0][0:64, 0:128]),
                 reads=[("ps", 0)], writes=[("tmpf", 0)])
            finals.append(S.op("sp", lambda e: e.dma_start(out=nl_d, in_=tmpf[0][0:64, 0:128]),
                               reads=[("tmpf", 0)], dma=True))
        except _Stop:
            pass
        for _e in S.ENGS:
            if S.ops[_e]:
                S.ops[_e][-1].signal = True
                finals.append(S.ops[_e][-1])
        finals.extend(d for d in S.dlast if d is not None)
        S.emit(final_waits=finals)
    return nc


_NC = None


def kernel(x_prompt, x_sample, cache_k, cache_v, state_lru, c, c_ctx, ln1_g, w_mod, b_mod, w_in, na_bias,
           cv_w, cv_b, cv_ln_g, cv_ln_b, lru_conv_w, lru_conv_b, lru_wa, lru_ba, lru_wi, lru_bi, lru_lam,
           w_out, ln2_g, ffn_up, ffn_conv_w, ffn_conv_b, ffn_down, final_g):
    global _NC
    f = lambda a: np.ascontiguousarray(np.asarray(a, np.float32))
    x_prompt, x_sample, cache_k, cache_v = f(x_prompt), f(x_sample), f(cache_k), f(cache_v)
    if _NC is None:
        _NC = build_nc()
    nc = _NC
    bt = _build_bias(na_bias)
    lwa, lwi = np.asarray(lru_wa, np.float32), np.asarray(lru_wi, np.float32)
    lw = np.zeros((L, 128, 16, 128), np.float32)
    for d in range(2):
        for gi, w in enumerate((lwa, lwi)):
            for cch in range(4):
                for hb_ in range(2):
                    blk = w[:, d, cch * 2 + hb_]
                    lw[:, hb_ * 64:(hb_ + 1) * 64, (d * 2 + gi) * 4 + cch, hb_ * 64:(hb_ + 1) * 64] = blk
    ident = np.eye(128, dtype=np.float32)
    NLc = NLAYERS
    shared = dict(wmod=f(w_mod)[:NLc], win=f(w_in)[:NLc], wout=f(w_out)[:NLc], fup=f(ffn_up)[:NLc],
                  fdn=f(ffn_down)[:NLc], lw=lw[:NLc], bt=bt[:NLc], ident=ident)

    def pv_for(s):
        parts = {
            "ln1g": _fm(ln1_g, 16), "ln2g": _fm(ln2_g, 16), "fing": _fm(final_g, 16), "bmod": _fm(b_mod, 96),
            "cond": np.moveaxis(_fm(np.stack([np.asarray(c_ctx), np.asarray(c)[s]]), 16), 1, 2),
            "cvw": np.moveaxis(_fm(cv_w, 4), 2, 3),
            "cvb": _fm(cv_b, 4), "cvg": _fm(cv_ln_g, 4), "cvlb": _fm(cv_ln_b, 4),
            "lcw": np.moveaxis(_fm(lru_conv_w, 4), 2, 3), "lcb": _fm(lru_conv_b, 4),
            "lba": _fm(lru_ba, 4), "lbi": _fm(lru_bi, 4), "lam": _fm(lru_lam, 4),
            "h0": _fm(np.asarray(state_lru)[s], 4),
            "fcw": np.moveaxis(_fm(ffn_conv_w, 88), 2, 3), "fcb": _fm(ffn_conv_b, 88),
        }
        cols = []
        for n, cnt_ in PV_ITEMS:
            a = np.ascontiguousarray(parts[n], dtype=np.float32).reshape(128, -1)
            assert a.shape[1] == cnt_, (n, a.shape, cnt_)
            cols.append(a)
        return np.ascontiguousarray(np.concatenate(cols, axis=1))

    pvs = [pv_for(0), pv_for(1)]
    in_maps = []
    for core in range(8):
        s = core % 2
        m = dict(shared)
        m["xs"] = x_sample[s]
        m["xp"] = x_prompt[2 * core:2 * core + 2].reshape(TP, D)
        m["ck"] = cache_k[s].reshape(L, 512, 1024)[:NLc]
        m["cvv"] = cache_v[s].reshape(L, 512, 1024)[:NLc]
        m["pv"] = pvs[s]
        in_maps.append(m)
    res = run_bass_kernel_spmd(nc, in_maps[:NCORES], core_ids=list(range(NCORES)))
    R = list(res.results)
    while len(R) < 8:
        R.append(R[len(R) % NCORES])
    y_prompt = np.concatenate([R[i]["yp"].reshape(2, 256, D) for i in range(8)], axis=0)
    y_sample = np.stack([R[0]["ys"], R[1]["ys"]], axis=0)
    new_k = np.concatenate([R[i]["nk"].reshape(2, L, 256, NH, 128) for i in range(8)], axis=0)
    new_v = np.concatenate([R[i]["nv"].reshape(2, L, 256, NH, 128) for i in range(8)], axis=0)
    new_lru = np.concatenate([R[i]["nl"].reshape(2, L, 2, 512) for i in range(8)], axis=0)
    return (y_prompt.astype(np.float32), y_sample.astype(np.float32), new_k.astype(np.float32),
            new_v.astype(np.float32), new_lru.astype(np.float32))
```
